# Optimizing a Trainium2 kernel written in Bass

```python
import math
import jax, jax.numpy as jnp
from jax import lax
import numpy as np

D_MODEL = 1024
BATCH = 16
SEQ = 256
DEPTH = 4
DEC_BATCH = 4
DEC_SEQ = 1024
PAST_LEN = 256

GRID_W = 64
N_BRANCH = 4
BRANCH_W = 256
RET_HEADS = 4
RET_DK = 64
RET_DV = 64
S5_GROUPS = 16
S5_GROUP_CH = 16
S5_STATE = 64
GLA_HEADS = 4
GLA_DK = 32
GLA_DV = 64
GLA_RANK = 16
GLA_TAU = 16.0
NA_HEADS = 4
NA_DH = 64
NA_WIN_H = 8
NA_WIN_W = 16
D_FF = 4 * D_MODEL
CHUNK = 64
QBLOCK = 128
ROPE_BASE = 10000.0
EPS = 1e-6

IN_SIZES = (RET_HEADS * RET_DK, RET_HEADS * RET_DK, RET_HEADS * RET_DV, RET_HEADS * RET_DV,
            S5_GROUPS * S5_GROUP_CH,
            GLA_HEADS * GLA_DK, GLA_HEADS * GLA_DK, GLA_HEADS * GLA_DV, GLA_HEADS * GLA_DV, 2 * GLA_RANK,
            NA_HEADS * NA_DH, NA_HEADS * NA_DH, NA_HEADS * NA_DH)
D_IN = sum(IN_SIZES)

kernel_name = 'hybrid_diffusion_gated_branch_step'


def _in_offsets():
    return [int(o) for o in np.cumsum(IN_SIZES)[:-1]]


def _flip(a):
    return jnp.flip(a, axis=1)


def rmsnorm(x, g):
    xf = x.astype(jnp.float32)
    y = xf * lax.rsqrt(jnp.mean(xf * xf, axis=-1, keepdims=True) + EPS)
    return (y * g.astype(jnp.float32)).astype(x.dtype)


def head_norm(o, g):
    B, L, H, d = o.shape
    of = o.astype(jnp.float32)
    mu = jnp.mean(of, axis=-1, keepdims=True)
    xc = of - mu
    y = xc * lax.rsqrt(jnp.mean(xc * xc, axis=-1, keepdims=True) + EPS)
    return y.reshape(B, L, H * d) * g.astype(jnp.float32)


def axial_rope(x):
    B, L, H, d = x.shape
    half = d // 2
    nf = half // 2
    t = jnp.arange(L)
    row = (t // GRID_W).astype(jnp.float32)
    col = (t % GRID_W).astype(jnp.float32)
    inv = ROPE_BASE ** (-jnp.arange(nf, dtype=jnp.float32) / nf)

    def rot(xa, pos):
        ang = pos[:, None] * inv[None, :]
        cos = jnp.cos(ang)[None, :, None, :]
        sin = jnp.sin(ang)[None, :, None, :]
        x1, x2 = xa[..., :nf], xa[..., nf:]
        return jnp.concatenate([x1 * cos - x2 * sin, x1 * sin + x2 * cos], axis=-1)

    xf = x.astype(jnp.float32)
    return jnp.concatenate([rot(xf[..., :half], row), rot(xf[..., half:], col)], axis=-1).astype(x.dtype)


def _chunks(a, B, n, H):
    return a.astype(jnp.float32).reshape(B, n, CHUNK, H, a.shape[-1]).transpose(1, 0, 3, 2, 4)


def retention_scan(q, k, v, log_gamma, s0):
    B, L, H, _ = q.shape
    dv = v.shape[-1]
    n = L // CHUNK
    lg = log_gamma.astype(jnp.float32)
    pos = jnp.arange(CHUNK, dtype=jnp.float32)
    rel = pos[:, None] - pos[None, :]
    causal = rel >= 0
    decay = jnp.where(causal[None], jnp.exp(lg[:, None, None] * jnp.where(causal, rel, 0.0)[None]), 0.0)
    q_dec = jnp.exp(lg[:, None] * (pos + 1.0)[None])
    k_dec = jnp.exp(lg[:, None] * (CHUNK - 1.0 - pos)[None])
    c_dec = jnp.exp(lg * CHUNK)

    def step(s, inp):
        qc, kc, vc = inp
        scores = jnp.einsum('bhid,bhjd->bhij', qc, kc) * decay[None]
        o = (jnp.einsum('bhij,bhjv->bhiv', scores, vc)
             + jnp.einsum('bhid,bhdv->bhiv', qc * q_dec[None, :, :, None], s))
        s = c_dec[None, :, None, None] * s + jnp.einsum('bhjd,bhjv->bhdv', kc * k_dec[None, :, :, None], vc)
        return s, o

    s_fin, o = lax.scan(step, s0.astype(jnp.float32),
                        (_chunks(q, B, n, H), _chunks(k, B, n, H), _chunks(v, B, n, H)))
    return o.transpose(1, 0, 3, 2, 4).reshape(B, L, H, dv), s_fin


def gla_scan(q, k, v, log_a, s0):
    B, L, H, _ = q.shape
    dv = v.shape[-1]
    n = L // CHUNK
    causal = jnp.tril(jnp.ones((CHUNK, CHUNK), dtype=bool))

    def step(s, inp):
        qc, kc, vc, gc = inp
        b = jnp.cumsum(gc, axis=2)
        diff = b[:, :, :, None, :] - b[:, :, None, :, :]
        w = jnp.exp(jnp.where(causal[None, None, :, :, None], diff, -jnp.inf))
        attn = jnp.einsum('bhid,bhjd,bhijd->bhij', qc, kc, w)
        o = (jnp.einsum('bhij,bhjv->bhiv', attn, vc)
             + jnp.einsum('bhid,bhdv->bhiv', qc * jnp.exp(b), s))
        b_end = b[:, :, -1, :]
        s = (jnp.exp(b_end)[..., None] * s
             + jnp.einsum('bhjd,bhjv->bhdv', kc * jnp.exp(b_end[:, :, None, :] - b), vc))
        return s, o

    s_fin, o = lax.scan(step, s0.astype(jnp.float32),
                        (_chunks(q, B, n, H), _chunks(k, B, n, H), _chunks(v, B, n, H), _chunks(log_a, B, n, H)))
    return o.transpose(1, 0, 3, 2, 4).reshape(B, L, H, dv), s_fin


def _ssm_combine(e1, e2):
    a1, b1 = e1
    a2, b2 = e2
    return a1 * a2, a2 * b1 + b2


def s5_scan(u, lam_re, lam_im, log_dt, b_re, b_im, c_re, c_im, x0):
    L = u.shape[1]
    lam = lax.complex(lam_re.astype(jnp.float32), lam_im.astype(jnp.float32))
    lam_dt = lam * jnp.exp(log_dt.astype(jnp.float32))[:, None]
    lam_bar = jnp.exp(lam_dt)
    b_bar = ((lam_bar - 1.0) / lam)[:, :, None] * lax.complex(b_re.astype(jnp.float32), b_im.astype(jnp.float32))
    c_mat = lax.complex(c_re.astype(jnp.float32), c_im.astype(jnp.float32))
    bu = jnp.einsum('blgh,gph->blgp', u.astype(jnp.complex64), b_bar)
    a = jnp.broadcast_to(lam_bar, bu.shape)
    _, xs = lax.associative_scan(_ssm_combine, (a, bu), axis=1)
    steps = jnp.arange(1, L + 1, dtype=jnp.float32)
    xs = xs + jnp.exp(lam_dt[None] * steps[:, None, None])[None] * x0[:, None]
    y = jnp.einsum('ghp,blgp->blgh', c_mat, xs).real
    return y, xs[:, -1]


def s5_direction(u, lp, d, x0):
    return s5_scan(u, lp['s5_lambda_re'][d], lp['s5_lambda_im'][d], lp['s5_log_dt'][d],
                   lp['s5_b_re'][d], lp['s5_b_im'][d], lp['s5_c_re'][d], lp['s5_c_im'][d], x0)


def context_attention(q, k, v):
    B, L, H, d = q.shape
    nb = L // QBLOCK
    scale = d ** -0.5
    qb = q.reshape(B, nb, QBLOCK, H, d).transpose(1, 0, 2, 3, 4)

    def blk(qi):
        s = jnp.einsum('bqhd,bkhd->bhqk', qi, k).astype(jnp.float32) * scale
        p = jax.nn.softmax(s, axis=-1)
        return jnp.einsum('bhqk,bkhd->bqhd', p, v.astype(jnp.float32))

    o = lax.map(blk, qb)
    return o.transpose(1, 0, 2, 3, 4).reshape(B, L, H, d)


def na_latent(q, k, v, k_ctx, v_ctx, rpb):
    B, L, H, d = q.shape
    rows = L // GRID_W
    kh = min(NA_WIN_H, rows)
    scale = d ** -0.5
    qg = q.reshape(B, rows, GRID_W, H, d)
    kg = k.reshape(B, rows, GRID_W, H, d)
    vg = v.reshape(B, rows, GRID_W, H, d)
    col = jnp.arange(GRID_W)
    col_start = jnp.clip(col - NA_WIN_W // 2, 0, GRID_W - NA_WIN_W)
    col_in = (col[None, :] >= col_start[:, None]) & (col[None, :] < col_start[:, None] + NA_WIN_W)
    col_idx = jnp.clip(col[None, :] - col[:, None] + NA_WIN_W - 1, 0, 2 * NA_WIN_W - 2)
    rpb_cols = rpb[:, :, col_idx]
    n_loc = kh * GRID_W

    def row_fn(r):
        rs = jnp.clip(r - kh // 2, 0, rows - kh)
        q_r = lax.dynamic_index_in_dim(qg, r, axis=1, keepdims=False)
        k_r = lax.dynamic_slice_in_dim(kg, rs, kh, axis=1)
        v_r = lax.dynamic_slice_in_dim(vg, rs, kh, axis=1)
        row_idx = rs + jnp.arange(kh) - r + NA_WIN_H - 1
        bias = jnp.take(rpb_cols, row_idx, axis=1).transpose(0, 2, 1, 3)
        s_loc = jnp.einsum('bqhd,bikhd->bhqik', q_r, k_r).astype(jnp.float32) * scale + bias[None]
        s_loc = jnp.where(col_in[None, None, :, None, :], s_loc, -jnp.inf).reshape(B, H, GRID_W, n_loc)
        s_ctx = jnp.einsum('bqhd,bchd->bhqc', q_r, k_ctx).astype(jnp.float32) * scale
        p = jax.nn.softmax(jnp.concatenate([s_loc, s_ctx], axis=-1), axis=-1)
        return (jnp.einsum('bhqn,bnhd->bqhd', p[..., :n_loc], v_r.reshape(B, n_loc, H, d).astype(jnp.float32))
                + jnp.einsum('bhqc,bchd->bqhd', p[..., n_loc:], v_ctx.astype(jnp.float32)))

    o = lax.map(row_fn, jnp.arange(rows))
    return o.transpose(1, 0, 2, 3, 4).reshape(B, L, H, d)


def token_mixer(h, lp, cache):
    B, L, _ = h.shape
    latent = cache is not None
    proj = h @ lp['w_in']
    (rq, rk, rv, rg, su, gq, gk, gv, gg, glr, nq, nk, nv) = jnp.split(proj, _in_offsets(), axis=-1)
    if latent:
        c_k, c_v, st_ret0, st_s5_0, st_gla0 = cache
        s5_x0 = lax.complex(st_s5_0[..., 0].astype(jnp.float32), st_s5_0[..., 1].astype(jnp.float32))
    else:
        st_ret0 = jnp.zeros((B, 2, RET_HEADS, RET_DK, RET_DV), jnp.float32)
        s5_x0 = jnp.zeros((B, 2, S5_GROUPS, S5_STATE), jnp.complex64)
        st_gla0 = jnp.zeros((B, 2, GLA_HEADS, GLA_DK, GLA_DV), jnp.float32)

    rq = rq.reshape(B, L, RET_HEADS, RET_DK)
    rk = rk.reshape(B, L, RET_HEADS, RET_DK)
    rv = rv.reshape(B, L, RET_HEADS, RET_DV)
    if latent:
        rq = axial_rope(rq)
        rk = axial_rope(rk)
    rk = rk * RET_DK ** -0.5
    o_f, sr_f = retention_scan(rq, rk, rv, lp['ret_log_decay'][0], st_ret0[:, 0])
    o_b, sr_b = retention_scan(_flip(rq), _flip(rk), _flip(rv), lp['ret_log_decay'][1], st_ret0[:, 1])
    ret_out = head_norm(o_f + _flip(o_b), lp['ret_gn']) * jax.nn.silu(rg.astype(jnp.float32))

    u = su.reshape(B, L, S5_GROUPS, S5_GROUP_CH).astype(jnp.float32)
    y_f, xs_f = s5_direction(u, lp, 0, s5_x0[:, 0])
    y_b, xs_b = s5_direction(_flip(u), lp, 1, s5_x0[:, 1])
    y = y_f + _flip(y_b) + lp['s5_d'].astype(jnp.float32).reshape(S5_GROUPS, S5_GROUP_CH) * u
    y = jax.nn.gelu(y.reshape(B, L, BRANCH_W))
    glu_a, glu_g = jnp.split(y @ lp['s5_w_glu'] + lp['s5_b_glu'], 2, axis=-1)
    s5_out = glu_a * jax.nn.sigmoid(glu_g)

    gq = gq.reshape(B, L, GLA_HEADS, GLA_DK) * GLA_DK ** -0.5
    gk = gk.reshape(B, L, GLA_HEADS, GLA_DK)
    gv = gv.reshape(B, L, GLA_HEADS, GLA_DV)
    lr_f, lr_b = jnp.split(glr, 2, axis=-1)
    la_f = (jax.nn.log_sigmoid((lr_f @ lp['gla_w_gate'][0] + lp['gla_b_gate'][0]).astype(jnp.float32))
            / GLA_TAU).reshape(B, L, GLA_HEADS, GLA_DK)
    la_b = (jax.nn.log_sigmoid((lr_b @ lp['gla_w_gate'][1] + lp['gla_b_gate'][1]).astype(jnp.float32))
            / GLA_TAU).reshape(B, L, GLA_HEADS, GLA_DK)
    og_f, sg_f = gla_scan(gq, gk, gv, la_f, st_gla0[:, 0])
    og_b, sg_b = gla_scan(_flip(gq), _flip(gk), _flip(gv), _flip(la_b), st_gla0[:, 1])
    gla_out = head_norm(og_f + _flip(og_b), lp['gla_gn']) * jax.nn.silu(gg.astype(jnp.float32))

    nq = nq.reshape(B, L, NA_HEADS, NA_DH)
    nk = nk.reshape(B, L, NA_HEADS, NA_DH)
    nv = nv.reshape(B, L, NA_HEADS, NA_DH)
    if latent:
        na_o = na_latent(nq, nk, nv, c_k, c_v, lp['na_rpb'])
    else:
        na_o = context_attention(nq, nk, nv)
    na_out = na_o.reshape(B, L, BRANCH_W)

    branches = jnp.stack([ret_out, s5_out, gla_out, na_out], axis=2).astype(h.dtype)
    up = jnp.einsum('blnw,nwd->blnd', branches, lp['w_branch'])
    gates = jax.nn.sigmoid(h @ lp['w_merge'] + lp['b_merge']).reshape(B, L, N_BRANCH, D_MODEL)
    out = jnp.sum(gates * up, axis=2) @ lp['w_out']
    if latent:
        return out, None
    st_s5 = jnp.stack([xs_f, xs_b], axis=1)
    new = (nk, nv, jnp.stack([sr_f, sr_b], axis=1),
           jnp.stack([st_s5.real, st_s5.imag], axis=-1),
           jnp.stack([sg_f, sg_b], axis=1))
    return out, new


def trunk_layer(x, mod, lp, cache):
    sh1, sc1, g1, sh2, sc2, g2 = jnp.split(mod, 6, axis=-1)
    h = rmsnorm(x, lp['g_norm'][0]) * (1.0 + sc1) + sh1
    m, ctx_tensors = token_mixer(h, lp, cache)
    x = x + g1 * rmsnorm(m, lp['g_norm'][1])
    h = rmsnorm(x, lp['g_norm'][2]) * (1.0 + sc2) + sh2
    f = jnp.square(jax.nn.relu(h @ lp['w_mlp1'])) @ lp['w_mlp2']
    x = x + g2 * rmsnorm(f, lp['g_norm'][3])
    return x, ctx_tensors


def setup_inputs(seed: int = 0) -> dict:
    key = jax.random.key(seed)
    ks = jax.random.split(key, 36)

    def nrm(k, shape, s):
        return jax.random.normal(k, shape, jnp.float32) * s

    ret_base = jnp.log(1.0 - 2.0 ** (-5.0 - jnp.arange(RET_HEADS, dtype=jnp.float32)))
    return {
        'x_prompt': nrm(ks[0], (BATCH, SEQ, D_MODEL), 1.0),
        'x_sample': nrm(ks[1], (DEC_BATCH, DEC_SEQ, D_MODEL), 1.0),
        'c': nrm(ks[2], (DEC_BATCH, D_MODEL), 1.0),
        'cache_na_k': nrm(ks[3], (DEC_BATCH, DEPTH, PAST_LEN, NA_HEADS, NA_DH), 1.0),
        'cache_na_v': nrm(ks[4], (DEC_BATCH, DEPTH, PAST_LEN, NA_HEADS, NA_DH), 1.0),
        'state_ret': nrm(ks[5], (DEC_BATCH, DEPTH, 2, RET_HEADS, RET_DK, RET_DV), 1.0),
        'state_s5': nrm(ks[6], (DEC_BATCH, DEPTH, 2, S5_GROUPS, S5_STATE, 2), 0.3),
        'state_gla': nrm(ks[7], (DEC_BATCH, DEPTH, 2, GLA_HEADS, GLA_DK, GLA_DV), 1.0),
        'c_ctx': nrm(ks[8], (D_MODEL,), 1.0),
        'w_ada': nrm(ks[9], (DEPTH, D_MODEL, 6 * D_MODEL), 0.5 * D_MODEL ** -0.5),
        'b_ada': nrm(ks[10], (DEPTH, 6 * D_MODEL), 0.02),
        'g_norm': 1.0 + nrm(ks[11], (DEPTH, 4, D_MODEL), 0.02),
        'w_in': nrm(ks[12], (DEPTH, D_MODEL, D_IN), D_MODEL ** -0.5),
        'ret_log_decay': ret_base[None, None, :] * (1.0 + nrm(ks[13], (DEPTH, 2, RET_HEADS), 0.05)),
        'ret_gn': 1.0 + nrm(ks[14], (DEPTH, BRANCH_W), 0.02),
        's5_lambda_re': -0.5 + nrm(ks[15], (DEPTH, 2, S5_GROUPS, S5_STATE), 0.01),
        's5_lambda_im': math.pi * jnp.arange(S5_STATE, dtype=jnp.float32) + nrm(ks[16], (DEPTH, 2, S5_GROUPS, S5_STATE), 0.01),
        's5_log_dt': jax.random.uniform(ks[17], (DEPTH, 2, S5_GROUPS), jnp.float32, math.log(1e-3), math.log(1e-1)),
        's5_b_re': nrm(ks[18], (DEPTH, 2, S5_GROUPS, S5_STATE, S5_GROUP_CH), (2 * S5_GROUP_CH) ** -0.5),
        's5_b_im': nrm(ks[19], (DEPTH, 2, S5_GROUPS, S5_STATE, S5_GROUP_CH), (2 * S5_GROUP_CH) ** -0.5),
        's5_c_re': nrm(ks[20], (DEPTH, 2, S5_GROUPS, S5_GROUP_CH, S5_STATE), S5_STATE ** -0.5),
        's5_c_im': nrm(ks[21], (DEPTH, 2, S5_GROUPS, S5_GROUP_CH, S5_STATE), S5_STATE ** -0.5),
        's5_d': nrm(ks[22], (DEPTH, BRANCH_W), 1.0),
        's5_w_glu': nrm(ks[23], (DEPTH, BRANCH_W, 2 * BRANCH_W), BRANCH_W ** -0.5),
        's5_b_glu': nrm(ks[24], (DEPTH, 2 * BRANCH_W), 0.02),
        'gla_w_gate': nrm(ks[25], (DEPTH, 2, GLA_RANK, GLA_HEADS * GLA_DK), GLA_RANK ** -0.5),
        'gla_b_gate': nrm(ks[26], (DEPTH, 2, GLA_HEADS * GLA_DK), 0.1),
        'gla_gn': 1.0 + nrm(ks[27], (DEPTH, BRANCH_W), 0.02),
        'na_rpb': nrm(ks[28], (DEPTH, NA_HEADS, 2 * NA_WIN_H - 1, 2 * NA_WIN_W - 1), 0.02),
        'w_branch': nrm(ks[29], (DEPTH, N_BRANCH, BRANCH_W, D_MODEL), BRANCH_W ** -0.5),
        'w_merge': nrm(ks[30], (DEPTH, D_MODEL, N_BRANCH * D_MODEL), D_MODEL ** -0.5),
        'b_merge': nrm(ks[31], (DEPTH, N_BRANCH * D_MODEL), 0.02),
        'w_out': nrm(ks[32], (DEPTH, D_MODEL, D_MODEL), D_MODEL ** -0.5),
        'w_mlp1': nrm(ks[33], (DEPTH, D_MODEL, D_FF), D_MODEL ** -0.5),
        'w_mlp2': nrm(ks[34], (DEPTH, D_FF, D_MODEL), D_FF ** -0.5),
    }


def reference(x_prompt, x_sample, c, cache_na_k, cache_na_v, state_ret, state_s5, state_gla,
              c_ctx, w_ada, b_ada, g_norm, w_in, ret_log_decay, ret_gn,
              s5_lambda_re, s5_lambda_im, s5_log_dt, s5_b_re, s5_b_im, s5_c_re, s5_c_im,
              s5_d, s5_w_glu, s5_b_glu, gla_w_gate, gla_b_gate, gla_gn, na_rpb,
              w_branch, w_merge, b_merge, w_out, w_mlp1, w_mlp2):
    y_prompt = x_prompt
    y_sample = x_sample
    ks_l, vs_l, ret_l, s5_l, gla_l = [], [], [], [], []
    for l in range(DEPTH):
        lp = {'g_norm': g_norm[l], 'w_in': w_in[l], 'ret_log_decay': ret_log_decay[l], 'ret_gn': ret_gn[l],
              's5_lambda_re': s5_lambda_re[l], 's5_lambda_im': s5_lambda_im[l], 's5_log_dt': s5_log_dt[l],
              's5_b_re': s5_b_re[l], 's5_b_im': s5_b_im[l], 's5_c_re': s5_c_re[l], 's5_c_im': s5_c_im[l],
              's5_d': s5_d[l], 's5_w_glu': s5_w_glu[l], 's5_b_glu': s5_b_glu[l],
              'gla_w_gate': gla_w_gate[l], 'gla_b_gate': gla_b_gate[l], 'gla_gn': gla_gn[l],
              'na_rpb': na_rpb[l], 'w_branch': w_branch[l], 'w_merge': w_merge[l], 'b_merge': b_merge[l],
              'w_out': w_out[l], 'w_mlp1': w_mlp1[l], 'w_mlp2': w_mlp2[l]}
        mod_ctx = (jax.nn.silu(c_ctx) @ w_ada[l] + b_ada[l])[None, None, :]
        y_prompt, (k_l, v_l, r_l, s_l, g_l) = trunk_layer(y_prompt, mod_ctx, lp, None)
        ks_l.append(k_l)
        vs_l.append(v_l)
        ret_l.append(r_l)
        s5_l.append(s_l)
        gla_l.append(g_l)
        mod_lat = (jax.nn.silu(c) @ w_ada[l] + b_ada[l])[:, None, :]
        cache_l = (cache_na_k[:, l], cache_na_v[:, l], state_ret[:, l], state_s5[:, l], state_gla[:, l])
        y_sample, _ = trunk_layer(y_sample, mod_lat, lp, cache_l)
    new_na_k = jnp.stack(ks_l, axis=1)
    new_na_v = jnp.stack(vs_l, axis=1)
    new_state_ret = jnp.stack(ret_l, axis=1)
    new_state_s5 = jnp.stack(s5_l, axis=1)
    new_state_gla = jnp.stack(gla_l, axis=1)
    return (y_prompt, y_sample, new_na_k, new_na_v, new_state_ret, new_state_s5, new_state_gla)
```

```python
from concourse.bass_utils import run_bass_kernel_spmd
import numpy as np
import concourse.bass as bass
import concourse.mybir as mybir
from concourse.ap import AP

F32 = mybir.dt.float32
BF16 = mybir.dt.bfloat16
ALU = mybir.AluOpType
AF = mybir.ActivationFunctionType

COMPUTE = ["pe", "act", "dve", "pool", "sp"]
NDMA = 24
SEM_ROT = 4
SAME_GAP = 10 ** 9


class Sched:
    def __init__(self, same_engine_sync=True):
        self.engs = COMPUTE + ["d%d" % i for i in range(NDMA)]
        self.ops = {e: [] for e in COMPUTE}
        self.count = {e: 0 for e in self.engs}
        self.seen = {e: {} for e in self.engs}
        self.snap = {e: [None] for e in self.engs}
        self.last_w = {}
        self.readers = {}
        self.dma_rr = 0
        self.same = same_engine_sync
        self.nwaits = 0

    def _conf_keys(self, res):
        name, sub = res
        if sub is None:
            return [k for k in self._names.get(name, ())]
        return [(name, sub), (name, None)]

    _names = None

    def _deps(self, eng, reads, writes):
        if self._names is None:
            self._names = {}
        deps = {}

        def add(e, i):
            if i > deps.get(e, 0):
                deps[e] = i

        for r in reads:
            for k in self._conf_keys(r):
                lw = self.last_w.get(k)
                if lw:
                    add(*lw)
        for w in writes:
            for k in self._conf_keys(w):
                lw = self.last_w.get(k)
                if lw:
                    add(*lw)
                for e, i in self.readers.get(k, {}).items():
                    add(e, i)
        return deps

    def _register(self, who, reads, writes):
        e, i = who
        for r in reads:
            self._names.setdefault(r[0], set()).add(r)
            self.readers.setdefault(r, {})[e] = i
        for w in writes:
            self._names.setdefault(w[0], set()).add(w)
            self.last_w[w] = (e, i)
            self.readers[w] = {}
            if w[1] is None:
                for k in self._names[w[0]]:
                    if k != w:
                        self.last_w.pop(k, None)
                        self.readers.pop(k, None)
                        self.last_w[k] = (e, i)
                        self.readers[k] = {}

    def _waits(self, eng, deps):
        waits = []
        seen = self.seen[eng]
        for e2, i2 in sorted(deps.items()):
            if e2 == eng and (not self.same or eng in ("sp", "pe")):
                continue
            if e2 == eng and self.count[eng] - i2 >= SAME_GAP:
                continue
            if seen.get(e2, 0) >= i2:
                continue
            waits.append((e2, i2))
        for e2, i2 in waits:
            seen[e2] = max(seen.get(e2, 0), i2)
            sn = self.snap[e2][i2] if i2 < len(self.snap[e2]) else None
            if sn:
                for k, v in sn.items():
                    if k != eng and seen.get(k, 0) < v:
                        seen[k] = v
        self.nwaits += len(waits)
        return waits

    @staticmethod
    def _norm(rs):
        out = []
        for r in rs:
            if isinstance(r, tuple):
                out.append(r)
            else:
                out.append((r, None))
        return out

    def op(self, eng, fn, reads=(), writes=()):
        reads = self._norm(reads)
        writes = self._norm(writes)
        deps = self._deps(eng, reads, writes)
        waits = self._waits(eng, deps)
        self.count[eng] += 1
        idx = self.count[eng]
        self.ops[eng].append(("op", fn, waits, idx))
        self.snap[eng].append(dict(self.seen[eng]))
        self._register((eng, idx), reads, writes)
        return idx

    def dma(self, queue, fn, reads=(), writes=()):
        reads = self._norm(reads)
        writes = self._norm(writes)
        lo, hi = {"pool": (0, 8), "sp": (8, 20), "act": (20, 24)}[queue]
        if not hasattr(self, "_rr"):
            self._rr = {}
        k = self._rr.get(queue, 0)
        self._rr[queue] = k + 1
        d = "d%d" % (lo + k % (hi - lo))
        deps = self._deps(queue, reads, writes)
        if self.count[d] > 0:
            deps[d] = max(deps.get(d, 0), self.count[d])
        waits = self._waits(queue, deps)
        self.count[d] += 1
        idx = self.count[d]
        self.ops[queue].append(("dma", fn, waits, (d, idx)))
        self.snap[d].append(dict(self.seen[queue]))
        self._register((d, idx), reads, writes)
        return (d, idx)

    def barrier(self):
        snapc = {e: c for e, c in self.count.items() if c > 0}
        for eng in COMPUTE:
            deps = {e: c for e, c in snapc.items() if e != eng}
            waits = self._waits(eng, deps)
            if waits:
                self.ops[eng].append(("wait", None, waits, None))

    def final_wait(self, eng="sp"):
        deps = {e: c for e, c in self.count.items() if c > 0 and e != eng}
        waits = self._waits(eng, deps)
        self.ops[eng].append(("wait", None, waits, None))

    def emit(self, nc, stack):
        sems = {}
        for e in COMPUTE:
            sems[e] = [stack.enter_context(nc.semaphore("s_%s%d" % (e, r))) for r in range(SEM_ROT)]
        for i in range(NDMA):
            sems["d%d" % i] = [stack.enter_context(nc.semaphore("s_d%d" % i))]

        def do_wait(engobj, e2, i2):
            if e2[1:].isdigit():
                engobj.wait_ge(sems[e2][0], 16 * i2)
            else:
                r = (i2 - 1) % SEM_ROT
                engobj.wait_ge(sems[e2][r], (i2 - 1) // SEM_ROT + 1)

        def run(engname, engobj):
            for kind, fn, waits, info in self.ops[engname]:
                for e2, i2 in waits:
                    do_wait(engobj, e2, i2)
                if kind == "op":
                    ins = fn(engobj)
                    ins.then_inc(sems[engname][(info - 1) % SEM_ROT], 1)
                elif kind == "dma":
                    ins = fn(engobj)
                    ins.then_inc(sems[info[0]][0], 16)

        block = stack.enter_context(nc.Block())

        @block.tensor
        def _(e):
            run("pe", e)

        @block.scalar
        def _(e):
            run("act", e)

        @block.vector
        def _(e):
            run("dve", e)

        @block.gpsimd
        def _(e):
            run("pool", e)

        @block.sync
        def _(e):
            run("sp", e)

import math
import numpy as np
from contextlib import ExitStack

D = 1024
T = 1536
NTILE = 12
EPS = 1e-6
NEG = -30000.0


def host_consts():
    c = {}
    c["identF"] = np.eye(128, dtype=np.float32)
    jj = np.arange(128)[:, None]
    ii = np.arange(128)[None, :]
    c["maskF"] = (jj <= ii).astype(np.float32)
    c["maskB"] = (jj >= ii).astype(np.float32)
    c["mavg"] = np.kron(np.eye(2, dtype=np.float32), np.full((64, 64), 1.0 / 64, np.float32))
    pos = np.zeros((128, 2, 128), np.float32)
    pos[:, 0, :] = np.arange(1, 129)[None, :]
    pos[:, 1, :] = (128 - np.arange(128))[None, :]
    c["posfb"] = pos
    gb = np.zeros((128, 128), np.float32)
    for idx in range(31):
        gb[idx, idx + 48] = 1.0
    c["gbig"] = gb
    c["dupI"] = np.concatenate([np.eye(64, dtype=np.float32)] * 2, axis=1)
    mn = np.full((128, 64), NEG, np.float32)
    for qc in range(64):
        cs = min(max(qc - 8, 0), 48)
        mn[cs:cs + 16, qc] = 0.0
        mn[64 + cs:64 + cs + 16, qc] = 0.0
    c["mneg"] = mn
    cm = np.zeros((128, 4, 128), np.float32)
    for glv in range(2):
        for q in range(4):
            gloc = 2 * q + glv
            cm[glv * 64:(glv + 1) * 64, q, gloc * 16:(gloc + 1) * 16] = 1.0
    c["cmask"] = cm
    hm = np.zeros((128, 4), np.float32)
    for h in range(4):
        hm[h * 32:(h + 1) * 32, h] = 1.0
    c["hmask"] = hm
    t = np.arange(1024)
    row = (t // 64).astype(np.float32)
    col = (t % 64).astype(np.float32)
    inv = (10000.0 ** (-np.arange(16, dtype=np.float32) / 16)).astype(np.float32)
    C = np.zeros((128, 1024), np.float32)
    Sg = np.zeros((128, 1024), np.float32)
    for p in range(128):
        d = p % 64
        half = d // 32
        j = d % 16
        blk = (d % 32) // 16
        posv = row if half == 0 else col
        ang = (posv * inv[j]).astype(np.float32)
        C[p] = np.cos(ang)
        Sg[p] = np.sin(ang) * (-1.0 if blk == 0 else 1.0)
    c["ropeC"] = C
    c["ropeS"] = Sg
    return c


def build(DEPTH=4, mixers=("ret", "s5", "gla", "na"), dbg=False):
    nc = bass.Bass("TRN2", target_bir_lowering=False)
    din = lambda name, shape: nc.dram_tensor(name, list(shape), F32, kind="ExternalInput").ap()
    dout = lambda name, shape: nc.dram_tensor(name, list(shape), F32, kind="ExternalOutput").ap()
    xp = din("xp", [2, 256, 1024]); xs = din("xs", [1024, 1024]); cv = din("cv", [16, 128])
    ck = din("ck", [4, 256, 256]); cvv = din("cvv", [4, 256, 256])
    sret0 = din("sret0", [4, 2, 4, 64, 64]); ss50 = din("ss50", [4, 2, 16, 64, 2]); sgla0 = din("sgla0", [4, 2, 4, 32, 64])
    w_ada = din("w_ada", [4, 1024, 6144]); b_ada = din("b_ada", [192, 128]); g_norm = din("g_norm", [128, 128])
    w_in = din("w_in", [4, 1024, 2848]); ret_ld = din("ret_ld", [32, 1]); ret_gn = din("ret_gn", [8, 128])
    s5_lre2 = din("s5_lre2", [64, 128]); s5_lim2 = din("s5_lim2", [64, 128]); s5_ldt2 = din("s5_ldt2", [64, 128])
    s5_bre = din("s5_bre", [4, 2, 16, 64, 16]); s5_bim = din("s5_bim", [4, 2, 16, 64, 16])
    s5_cre = din("s5_cre", [4, 2, 16, 16, 64]); s5_cim = din("s5_cim", [4, 2, 16, 16, 64])
    s5_d = din("s5_d", [8, 128]); s5_wglu = din("s5_wglu", [4, 256, 512]); s5_bglu = din("s5_bglu", [16, 128])
    gla_wg = din("gla_wg", [4, 2, 16, 128]); gla_bg = din("gla_bg", [4, 2, 128]); gla_gn = din("gla_gn", [8, 128])
    na_rpb = din("na_rpb", [4, 4, 15, 31])
    w_branch = din("w_branch", [4, 4, 256, 1024]); w_merge = din("w_merge", [4, 1024, 4096]); b_merge = din("b_merge", [128, 128])
    w_out = din("w_out", [4, 1024, 1024]); w_mlp1 = din("w_mlp1", [4, 1024, 4096]); w_mlp2 = din("w_mlp2", [4, 4096, 1024])
    consts = host_consts()
    cd = {k: din("c_" + k, v.shape) for k, v in consts.items()}
    yp = dout("yp", [2, 256, 1024]); ys = dout("ys", [1024, 1024])
    onk = dout("onk", [2, 4, 256, 256]); onv = dout("onv", [2, 4, 256, 256])
    osret = dout("osret", [2, 4, 2, 4, 64, 64]); oss5 = dout("oss5", [2, 4, 2, 16, 64, 2]); osgla = dout("osgla", [2, 4, 2, 4, 32, 64])
    bt_scr = nc.dram_tensor("bt_scr", [4, 15, 64, 64], F32, kind="Internal").ap()

    S = Sched()
    st = ExitStack()
    sbt = lambda n, s, d=F32: st.enter_context(nc.sbuf_tensor(n, list(s), d))
    xT = sbt("xT", [128, 8, T])
    hT = sbt("hT", [128, 8, T], BF16)
    accT = sbt("accT", [128, 8, T])
    brT = sbt("brT", [128, 8, T], BF16)
    wbuf = [sbt("wb%d" % i, [128, 8192], BF16) for i in range(2)]
    rstd = sbt("rstd", [128, T])
    tmpf = [sbt("tmpf%d" % i, [128, 512]) for i in range(2)]
    sqb = [sbt("sqb%d" % i, [128, 512], BF16) for i in range(2)]
    identF = sbt("identF", [128, 128]); identB = sbt("identB", [128, 128], BF16)
    onesB = sbt("onesB", [128, 128], BF16)
    maskF = sbt("maskF", [128, 128], BF16); maskB = sbt("maskB", [128, 128], BF16)
    mavgB = sbt("mavgB", [128, 128], BF16)
    gnT = sbt("gnT", [128, 128]); badaT = sbt("badaT", [128, 192]); bmT = sbt("bmT", [128, 128])
    retgnT = sbt("retgnT", [128, 8]); glagnT = sbt("glagnT", [128, 8]); s5dT = sbt("s5dT", [128, 8]); bgluT = sbt("bgluT", [128, 16])
    cT = sbt("cT", [128, 16]); scT = sbt("scT", [128, 8, 2], BF16)
    modT2 = sbt("modT", [128, 2, 48, 2]); dsc2 = sbt("dsc", [128, 2, 4, 8, 2])
    curl = [0]
    rowst = sbt("rowst", [128, 128])
    cln8 = sbt("cln8", [128, 1])
    dupI = sbt("dupI", [64, 128], BF16); gbig = sbt("gbig", [128, 128]); mneg = sbt("mneg", [128, 64]); Rrp = sbt("Rrp", [32, 60])
    pbank = [st.enter_context(nc.psum_tensor("pb%d" % i, [128, 512], F32)) for i in range(8)]
    bctr = [0]

    def nb():
        i = bctr[0] % 6
        bctr[0] += 1
        return pbank[i], "pb%d" % i

    wslot = [0]
    pre = {}

    def alloc_w(nel):
        if nel <= 4096:
            s = wslot[0] % 4
            wslot[0] += 1
            return wbuf[s // 2], (s % 2) * 4096, ("wb%d" % (s // 2), s % 2)
        if wslot[0] % 2:
            wslot[0] += 1
        s = wslot[0] % 4
        wslot[0] += 2
        return wbuf[s // 2], 0, "wb%d" % (s // 2)

    def load_w(src2d, kc, ncols, key=None, queue="pool"):
        if key is not None and key in pre:
            return pre.pop(key)
        buf, off, name = alloc_w(kc * ncols)
        view = buf[:, off:off + kc * ncols].rearrange("p (k n) -> p k n", k=kc)
        S.dma(queue, lambda e: e.dma_start(out=view, in_=src2d.rearrange("(k p) n -> p k n", p=128)), [], [name])
        return view, name

    def align_w():
        if wslot[0] % 2:
            wslot[0] += 1

    def prefetch(key, src2d, kc, ncols):
        if key not in pre:
            pre[key] = load_w(src2d, kc, ncols)

    def dma(q, out, in_, r, w):
        S.dma(q, lambda e: e.dma_start(out=out, in_=in_), r, w)

    def mm(out, lhsT, rhs, start, stop, r, w):
        S.op("pe", lambda e: e.matmul(out, lhsT, rhs, start=start, stop=stop), r, w)

    def tr(out, in_, ident, r, w):
        S.op("pe", lambda e: e.transpose(out, in_, ident), r, w)

    def act(out, in_, func, r, w, bias=None, scale=None):
        kw = {}
        if bias is not None:
            kw["bias"] = bias
        if scale is not None:
            kw["scale"] = scale
        S.op("act", lambda e: e.activation(out=out, in_=in_, func=func, **kw), r, w)

    def tt(eng, out, in0, in1, op, r, w):
        S.op(eng, lambda e: e.tensor_tensor(out=out, in0=in0, in1=in1, op=op), r, w)

    def ts(eng, out, in0, s1, s2, op0, op1, r, w):
        if s2 is None:
            S.op(eng, lambda e: e.tensor_scalar(out=out, in0=in0, scalar1=s1, scalar2=None, op0=op0), r, w)
        else:
            S.op(eng, lambda e: e.tensor_scalar(out=out, in0=in0, scalar1=s1, scalar2=s2, op0=op0, op1=op1), r, w)

    def stt(eng, out, in0, scalar, in1, op0, op1, r, w):
        S.op(eng, lambda e: e.scalar_tensor_tensor(out=out, in0=in0, scalar=scalar, in1=in1, op0=op0, op1=op1), r, w)

    def bcast_last(ap2, n):
        a = ap2.ap
        return AP(ap2.tensor, ap2.offset, [list(a[0]), list(a[1]), [0, n]])

    def bcast_mid(ap2, n):
        a = ap2.ap
        return AP(ap2.tensor, ap2.offset, [list(a[0]), [0, n], list(a[1])])

    dma("sp", identF[:], cd["identF"][:, :], [], ["identF"])
    dma("pool", identB[:], cd["identF"][:, :], [], ["identB"])
    dma("pool", maskF[:], cd["maskF"][:, :], [], ["maskF"])
    dma("pool", maskB[:], cd["maskB"][:, :], [], ["maskB"])
    dma("pool", mavgB[:], cd["mavg"][:, :], [], ["mavgB"])
    S.op("dve", lambda e: e.memset(onesB[:], 1.0), [], ["onesB"])
    S.op("dve", lambda e: e.memset(cln8[:], math.log(0.125)), [], ["cln8"])
    dma("sp", gbig[:], cd["gbig"][:, :], [], ["gbig"])
    dma("pool", dupI[:], cd["dupI"][:, :], [], ["dupI"])
    dma("sp", mneg[:], cd["mneg"][:, :], [], ["mneg"])

    def rows_to_cols(src, R, dst, dname):
        dma("sp", rowst[0:R, :], src, [], ["rowst"])
        b, bn = nb()
        tr(b[:, 0:R], rowst[0:R, :], identF[0:R, 0:R], ["rowst", "identF"], [bn])
        act(dst, b[:, 0:R], AF.Copy, [bn], [dname])

    rows_to_cols(g_norm[:, :], 128, gnT[:], "gnT")
    rows_to_cols(b_ada[0:128, :], 128, badaT[:, 0:128], "badaT")
    rows_to_cols(b_ada[128:192, :], 64, badaT[:, 128:192], "badaT")
    rows_to_cols(b_merge[:, :], 128, bmT[:], "bmT")
    rows_to_cols(ret_gn[:, :], 8, retgnT[:], "retgnT")
    rows_to_cols(gla_gn[:, :], 8, glagnT[:], "glagnT")
    rows_to_cols(s5_d[:, :], 8, s5dT[:], "s5dT")
    rows_to_cols(s5_bglu[:, :], 16, bgluT[:], "bgluT")
    rows_to_cols(cv[:, :], 16, cT[:], "cT")
    act(scT[:, :, 0], cT[:, 0:8], AF.Silu, ["cT"], ["scT"])
    act(scT[:, :, 1], cT[:, 8:16], AF.Silu, ["cT"], ["scT"])

    accS = accT[:].rearrange("p k t -> p (k t)")

    def stage(j):
        return accS[:, j * 1024:(j + 1) * 1024], ("accT", "st%d" % (j,))

    for ti in range(NTILE):
        sv, sn = stage(ti % 8)
        src = xp[ti // 2, (ti % 2) * 128:(ti % 2 + 1) * 128, :] if ti < 4 else xs[(ti - 4) * 128:(ti - 3) * 128, :]
        dma("sp", sv, src, [], [sn])
        for half in range(2):
            b, bn = nb()
            for kk in range(4):
                k = half * 4 + kk
                tr(b[:, kk * 128:(kk + 1) * 128], sv[:, k * 128:(k + 1) * 128], identF[:], [sn, "identF"], [bn])
            act(xT[:, half * 4:(half + 1) * 4, ti * 128:(ti + 1) * 128], b[:].rearrange("p (a b) -> p a b", a=4), AF.Copy,
                [bn], [("xT", ti)])
    S.barrier()

    def gpath(g):
        return 0 if g == 0 else 1

    def mod_src(l, cb):
        return w_ada[l, :, cb * 1024:(cb + 1) * 1024]

    def mod_mm(l, cb, bank=None):
        wv, wn = load_w(mod_src(l, cb), 8, 1024, key="ada%d_%d" % (l, cb))
        b, bn = bank if bank is not None else nb()
        for qq in range(8):
            for k in range(8):
                mm(b[:, qq * 2:(qq + 1) * 2], wv[:, k, qq * 128:(qq + 1) * 128], scT[:, k, :], k == 0, k == 7, [wn, "scT"], [bn])
        return b, bn

    def mod_evac(l, cb, b, bn):
        modT = modT2[:, l % 2]
        mN = "modT%d" % (l % 2)
        tt("dve", modT[:, cb * 8:(cb + 1) * 8, :], b[:, 0:16].rearrange("p (a b) -> p a b", a=8),
           bcast_last(badaT[:, l * 48 + cb * 8: l * 48 + cb * 8 + 8], 2), ALU.add, [bn, "badaT"], [mN])

    def mod_finish(l):
        modT = modT2[:, l % 2]
        dsc = dsc2[:, l % 2]
        mN = "modT%d" % (l % 2)
        dN = "dsc%d" % (l % 2)
        gn = lambda n: bcast_last(gnT[:, (l * 4 + n) * 8:(l * 4 + n) * 8 + 8], 2)
        stt("dve", dsc[:, 0], modT[:, 8:16, :], 1.0, gn(0), ALU.add, ALU.mult, [mN, "gnT"], [dN])
        tt("dve", dsc[:, 1], modT[:, 16:24, :], gn(1), ALU.mult, [mN, "gnT"], [dN])
        stt("dve", dsc[:, 2], modT[:, 32:40, :], 1.0, gn(2), ALU.add, ALU.mult, [mN, "gnT"], [dN])
        tt("dve", dsc[:, 3], modT[:, 40:48, :], gn(3), ALU.mult, [mN, "gnT"], [dN])

    def emit_mod(l):
        for cb in range(6):
            b, bn = mod_mm(l, cb)
            mod_evac(l, cb, b, bn)
        mod_finish(l)

    def norm_stats(src, sname):
        for g in range(3):
            b, bn = nb()
            for k in range(8):
                q = sqb[k % 2]
                act(q[:], src[:, k, g * 512:(g + 1) * 512], AF.Square, [sname], ["sqb%d" % (k % 2)])
                mm(b[:], onesB[:], q[:], k == 0, k == 7, ["onesB", "sqb%d" % (k % 2)], [bn])
            act(tmpf[0][:], b[:], AF.Ln, [bn], ["tmpf0"], bias=EPS, scale=1.0 / 1024)
            act(rstd[:, g * 512:(g + 1) * 512], tmpf[0][:], AF.Exp, ["tmpf0"], [("rstd", g)], scale=-0.5)

    def norm_to_h(src, sname, ai, bq):
        norm_stats(src, sname)
        modT = modT2[:, curl[0] % 2]
        dsc = dsc2[:, curl[0] % 2]
        mN = "modT%d" % (curl[0] % 2)
        dN = "dsc%d" % (curl[0] % 2)
        for g in range(3):
            p = gpath(g)
            for k in range(8):
                tf = tmpf[k % 2]
                tt("dve", tf[:], src[:, k, g * 512:(g + 1) * 512], rstd[:, g * 512:(g + 1) * 512], ALU.mult,
                   [sname, ("rstd", g)], ["tmpf%d" % (k % 2)])
                act(hT[:, k, g * 512:(g + 1) * 512], tf[:], AF.Identity, ["tmpf%d" % (k % 2), dN, mN], [("hT", g)],
                    bias=modT[:, bq + k, p:p + 1], scale=dsc[:, ai, k, p:p + 1])

    def resid_update(src, sname, ai):
        norm_stats(src, sname)
        modT = modT2[:, curl[0] % 2]
        dsc = dsc2[:, curl[0] % 2]
        mN = "modT%d" % (curl[0] % 2)
        dN = "dsc%d" % (curl[0] % 2)
        for g in range(3):
            p = gpath(g)
            for k in range(8):
                tf = tmpf[k % 2]
                tt("dve", tf[:], src[:, k, g * 512:(g + 1) * 512], rstd[:, g * 512:(g + 1) * 512], ALU.mult,
                   [sname, ("rstd", g)], ["tmpf%d" % (k % 2)])
                stt("dve", xT[:, k, g * 512:(g + 1) * 512], tf[:], dsc[:, ai, k, p:p + 1], xT[:, k, g * 512:(g + 1) * 512],
                    ALU.mult, ALU.add, ["tmpf%d" % (k % 2), dN, "xT"], ["xT"])

    def proj_fm(wv, wn, kc, c0, src, sname, g, evac):
        b, bn = nb()
        for k in range(kc):
            mm(b[:], wv[:, k, c0:c0 + 128], src[:, k, g * 512:(g + 1) * 512], k == 0, k == kc - 1, [wn, sname], [bn])
        evac(b, bn)

    def proj_tm(wv, wn, kc, c0, ncols, src, sname, ti, evac):
        b, bn = nb()
        for k in range(kc):
            mm(b[:, 0:ncols], src[:, k, ti * 128:(ti + 1) * 128], wv[:, k, c0:c0 + ncols], k == 0, k == kc - 1, [wn, sname], [bn])
        evac(b, bn)

    def carve(off, n, dtype=F32):
        a = accS[:, off:off + n]
        if dtype == BF16:
            a = a.bitcast(BF16)
        return a

    rotb = sbt("rotb", [128, 16, 128], BF16)
    rotf = sbt("rotf", [128, 8, 128])
    lgc = sbt("lgc", [128, 4, 2, 2])
    nlgc = sbt("nlgc", [128, 4, 2, 2])
    eend = sbt("eend", [128, 2, 2])
    posfb = sbt("posfb", [128, 2, 128])
    hmask = sbt("hmask", [128, 4])
    dma("sp", posfb[:], cd["posfb"][:, :, :], [], ["posfb"])
    dma("sp", hmask[:], cd["hmask"][:, :], [], ["hmask"])
    for l_ in range(4):
        for d_ in range(2):
            for h_ in range(4):
                e0 = ret_ld[(l_ * 2 + d_) * 4 + h_:(l_ * 2 + d_) * 4 + h_ + 1, 0:1]
                src = AP(e0.tensor, e0.offset, [[0, 64], [1, 1]])
                dma("sp", lgc[(h_ % 2) * 64:(h_ % 2) * 64 + 64, l_, d_, h_ // 2:h_ // 2 + 1], src, [], ["lgc"])
    ts("dve", nlgc[:].rearrange("p a b c -> p (a b c)"), lgc[:].rearrange("p a b c -> p (a b c)"), -1.0, None, ALU.mult, None, ["lgc"], ["nlgc"])

    def head_norm(oG, oGn, j, g, gcol, gate, gaten, outchunk):
        osb = oG[:, j, :]
        b1, b1n = nb()
        mm(b1[:], mavgB[:], osb, True, True, ["mavgB", oGn], [b1n])
        tt("dve", tmpf[0][:], osb, b1[:], ALU.subtract, [oGn, b1n], ["tmpf0"])
        sqv = tmpf[1][:].bitcast(BF16)[:, 0:512]
        act(sqv, tmpf[0][:], AF.Square, ["tmpf0"], ["tmpf1"])
        b2, b2n = nb()
        mm(b2[:], mavgB[:], sqv, True, True, ["mavgB", "tmpf1"], [b2n])
        act(rstd[:, 0:512], b2[:], AF.Ln, [b2n], [("rstd", 0)], bias=EPS, scale=1.0)
        act(rstd[:, 0:512], rstd[:, 0:512], AF.Exp, [("rstd", 0)], [("rstd", 0)], scale=-0.5)
        tt("dve", tmpf[0][:], tmpf[0][:], rstd[:, 0:512], ALU.mult, ["tmpf0", ("rstd", 0)], ["tmpf0"])
        stt("dve", brT[:, outchunk, g * 512:(g + 1) * 512], tmpf[0][:], gcol, gate[:, j, g * 512:(g + 1) * 512], ALU.mult, ALU.mult,
            ["tmpf0", gaten], [("brT", outchunk)])

    SEQS = [(0, 2, 0), (2, 2, 0), (4, 8, 1)]

    def mixer_ret(l):
        rq = carve(0, 1536, BF16).rearrange("p (j t) -> p j t", j=2)
        rk = carve(1536, 1536, BF16).rearrange("p (j t) -> p j t", j=2)
        rg = carve(3072, 1536, BF16).rearrange("p (j t) -> p j t", j=2)
        rvpad = carve(4608, 3072, BF16).rearrange("p (a h c) -> p a h c", a=12, h=4)
        Spad = carve(7680, 2304, BF16).rearrange("p (d c j x) -> p d c j x", d=2, c=9, j=2)
        Srun = carve(9984, 256, F32).rearrange("p (d j x) -> p d j x", d=2, j=2)
        U = carve(10240, 128, F32).rearrange("p (j x) -> p j x", j=2)
        oG = carve(10496, 512, BF16).rearrange("p (j x) -> p j x", j=2)
        Eq = carve(11520, 256, BF16).rearrange("p (d j x) -> p d j x", d=2, j=2)
        Ek = carve(11776, 256, BF16).rearrange("p (d j x) -> p d j x", d=2, j=2)
        rqs = brT[:, 2:4, :]
        rks = brT[:, 4:6, :]
        ropeC = brT[:, 6, 0:1024]
        ropeS = brT[:, 7, 0:1024]
        dma("pool", ropeC, cd["ropeC"][:, :], [], ["ropeC"])
        dma("pool", ropeS, cd["ropeS"][:, :], [], ["ropeS"])
        S.op("pool", lambda e: e.memset(rvpad, 0.0), [], ["rvpad"])
        S.op("pool", lambda e: e.memset(Spad, 0.0), [], ["Spad"])
        for d in range(2):
            for j in range(2):
                act(Eq[:, d, j, :], posfb[:, d, :], AF.Exp, ["posfb", "lgc"], ["Eq"], scale=lgc[:, l, d, j:j + 1])
                act(Ek[:, d, j, :], posfb[:, d, :], AF.Exp, ["posfb", "nlgc"], ["Ek"], scale=nlgc[:, l, d, j:j + 1], bias=cln8[:, 0:1])
        act(eend[:].rearrange("p a b -> p (a b)"), lgc[:, l].rearrange("p a b -> p (a b)"), AF.Exp, ["lgc"], ["eend"], scale=128.0)
        wv, wn = load_w(w_in[l, :, 0:1024], 8, 1024, key="ret%d" % l)
        for g in range(3):
            for j in range(2):
                proj_fm(wv, wn, 8, j * 128, hT, "hT", g, lambda b, bn, j=j, g=g: act(rq[:, j, g * 512:(g + 1) * 512], b[:], AF.Copy, [bn], ["rq"]))
                proj_fm(wv, wn, 8, 256 + j * 128, hT, "hT", g, lambda b, bn, j=j, g=g: act(rk[:, j, g * 512:(g + 1) * 512], b[:], AF.Copy, [bn], ["rk"]))
                proj_fm(wv, wn, 8, 768 + j * 128, hT, "hT", g, lambda b, bn, j=j, g=g: act(rg[:, j, g * 512:(g + 1) * 512], b[:], AF.Silu, [bn], ["rg"]))
        for ti in range(NTILE):
            def ev_v(b, bn, ti=ti):
                for h in range(4):
                    act(rvpad[:, ti, h, (h % 2) * 64:(h % 2) * 64 + 64], b[:, h * 64:(h + 1) * 64], AF.Copy, [bn], ["rvpad"])
            proj_tm(wv, wn, 8, 512, 256, hT, "hT", ti, ev_v)
        prefetch("s5%d" % l, w_in[l, :, 1024:1280], 8, 256)
        sbuf_, soff_, swn = alloc_w(4096)
        swv = sbuf_[:, soff_:soff_ + 4096].rearrange("p (k n) -> p k n", k=8)
        for blk in range(2):
            dstv = swv.rearrange("p k (m b x) -> p k m b x", b=2, x=16)[:, :, :, blk, :]
            srcv = wv[:, :, 0:512].rearrange("p k (m b x) -> p k m b x", b=2, x=16)[:, :, :, 1 - blk, :]
            S.op("act", lambda e, dstv=dstv, srcv=srcv: e.activation(out=dstv, in_=srcv, func=AF.Copy), [wn], [swn])
        for g in (1, 2):
            for j in range(2):
                proj_fm(swv, swn, 8, j * 128, hT, "hT", g, lambda b, bn, j=j, g=g: act(rqs[:, j, g * 512:(g + 1) * 512], b[:], AF.Copy, [bn], ["rqs"]))
                proj_fm(swv, swn, 8, 256 + j * 128, hT, "hT", g, lambda b, bn, j=j, g=g: act(rks[:, j, g * 512:(g + 1) * 512], b[:], AF.Copy, [bn], ["rks"]))
        for (r_, rn, s_, sn_) in ((rq, "rq", rqs, "rqs"), (rk, "rk", rks, "rks")):
            for j in range(2):
                tt("dve", r_[:, j, 512:1536], r_[:, j, 512:1536], ropeC, ALU.mult, [rn, "ropeC"], [rn])
                tt("pool", s_[:, j, 512:1536], s_[:, j, 512:1536], ropeS, ALU.mult, [sn_, "ropeS"], [sn_])
                tt("dve", r_[:, j, 512:1536], r_[:, j, 512:1536], s_[:, j, 512:1536], ALU.add, [rn, sn_], [rn])
        S.barrier()
        if not all(m in mixers for m in ("s5", "gla")):
            for n in range(2, 6):
                S.op("pool", lambda e, n=n: e.memset(brT[:, n, :], 0.0), [], [("brT", n)])
        rc = [0]

        def rb():
            i = rc[0] % 16
            rc[0] += 1
            return rotb[:, i, :], ("rotb", i)

        fc = [0]

        def rf():
            i = fc[0] % 4
            fc[0] += 1
            return rotf[:, i, :], ("rotf", i)

        for si, (t0, n, latent) in enumerate(SEQS):
            for d in range(2):
                for j in range(2):
                    for hl in range(2):
                        h = 2 * j + hl
                        if latent:
                            dma("sp", Srun[hl * 64:(hl + 1) * 64, d, j, :], sret0[l, d, h, :, :], [], ["Srun"])
                        else:
                            S.op("dve", lambda e, hl=hl, d=d, j=j: e.memset(Srun[hl * 64:(hl + 1) * 64, d, j, :], 0.0), [], ["Srun"])
                order = list(range(n)) if d == 0 else list(range(n - 1, -1, -1))
                kts = {}

                def p1_front(c, d=d, kts=kts):
                    ti = t0 + c
                    for j in range(2):
                        kf, kfn = rf()
                        tt("dve", kf, rk[:, j, ti * 128:(ti + 1) * 128], Ek[:, d, j, :], ALU.mult, ["rk", "Ek"], [kfn])
                        b, bn = nb()
                        tr(b[:, 0:128], kf, identF[:], [kfn, "identF"], [bn])
                        kt, ktn = rb()
                        act(kt, b[:, 0:128], AF.Copy, [bn], [ktn])
                        kts[(c, j)] = (kt, ktn)

                def p1_back(c, d=d, kts=kts):
                    ti = t0 + c
                    for j in range(2):
                        for hl in range(2):
                            S.op("dve", lambda e, hl=hl, d=d, j=j, c=c: e.tensor_copy(out=Spad[hl * 64:(hl + 1) * 64, d, c, j, hl * 64:(hl + 1) * 64],
                                                                                   in_=Srun[hl * 64:(hl + 1) * 64, d, j, :]), ["Srun"], ["Spad"])
                    b2, b2n = nb()
                    for j in range(2):
                        kt, ktn = kts.pop((c, j))
                        for hl in range(2):
                            h = 2 * j + hl
                            mm(b2[:, j * 128 + hl * 64:j * 128 + (hl + 1) * 64], kt, rvpad[:, ti, h, hl * 64:(hl + 1) * 64], True, True, [ktn, "rvpad"], [b2n])
                    for hl in range(2):
                        ps = slice(hl * 64, (hl + 1) * 64)
                        uv = b2[ps, hl * 64:hl * 64 + 64]
                        uview = AP(uv.tensor, uv.offset, [list(uv.ap[0]), [128, 2], [1, 64]])
                        tt("dve", Srun[ps, d], Srun[ps, d], uview, ALU.add, ["Srun", b2n], ["Srun"])
                    tt("dve", Srun[:, d], Srun[:, d], bcast_last(eend[:, d, :], 64), ALU.mult, ["Srun", "eend"], ["Srun"])

                p1_front(order[0])
                for oi, c in enumerate(order):
                    if oi + 1 < len(order):
                        p1_front(order[oi + 1])
                    p1_back(c)
                if not latent:
                    for j in range(2):
                        for hl in range(2):
                            dma("sp", osret[si, l, d, 2 * j + hl, :, :], Srun[hl * 64:(hl + 1) * 64, d, j, :], ["Srun"], ["osret_d"])
            p2u = [(c, j, d, hl) for c in range(n) for j in range(2) for d in range(2) for hl in range(2)]
            qk = {}
            ams = {}

            def p2_front(u):
                c, j, d, hl = u
                ti = t0 + c
                if hl == 0:
                    q_, qn_ = rb()
                    tt("dve", q_, rq[:, j, ti * 128:(ti + 1) * 128], Eq[:, d, j, :], ALU.mult, ["rq", "Eq"], [qn_])
                    k_, kn_ = rb()
                    tt("dve", k_, rk[:, j, ti * 128:(ti + 1) * 128], Ek[:, d, j, :], ALU.mult, ["rk", "Ek"], [kn_])
                    qk[(c, j, d)] = (q_, qn_, k_, kn_)
                q_, qn_, k_, kn_ = qk[(c, j, d)]
                b, bn = nb()
                mm(b[:, 0:128], k_[hl * 64:(hl + 1) * 64, :], q_[hl * 64:(hl + 1) * 64, :], True, True, [kn_, qn_], [bn])
                ams[u] = (b, bn)

            def p2_mid(u):
                c, j, d, hl = u
                b, bn = ams[u]
                a_, an_ = rb()
                tt("dve", a_, b[:, 0:128], (maskF if d == 0 else maskB)[:], ALU.mult, [bn, "maskF", "maskB"], [an_])
                ams[u] = (a_, an_)

            def p2_back(u):
                c, j, d, hl = u
                ti = t0 + c
                g = ti // 4
                cg = ti % 4
                ob, obn = (pbank[6], "pb6") if (c * 2 + j) % 2 == 0 else (pbank[7], "pb7")
                a_, an_ = ams.pop(u)
                q_, qn_, k_, kn_ = qk[(c, j, d)]
                h = 2 * j + hl
                idx = 2 * (2 * d + hl)
                mm(ob[:, 0:128], rvpad[:, ti, h, :], a_, idx == 0, False, ["rvpad", an_], [obn])
                mm(ob[:, 0:128], Spad[hl * 64:(hl + 1) * 64, d, c, j, :], q_[hl * 64:(hl + 1) * 64, :], False, idx + 1 == 7, ["Spad", qn_], [obn])
                if d == 1 and hl == 1:
                    act(oG[:, j, cg * 128:(cg + 1) * 128], ob[:, 0:128], AF.Copy, [obn], ["oG"])
                    if cg == 3 and j == 1:
                        for jj in range(2):
                            head_norm(oG, "oG", jj, g, retgnT[:, l * 2 + jj:l * 2 + jj + 1], rg, "rg", jj)

            for ui in range(len(p2u) + 2):
                if ui < len(p2u):
                    p2_front(p2u[ui])
                if 1 <= ui <= len(p2u):
                    p2_mid(p2u[ui - 1])
                if ui >= 2:
                    p2_back(p2u[ui - 2])

    wgf = sbt("wgf", [33, 256]); wgb = sbt("wgb", [33, 256], BF16); gee = sbt("gee", [128, 3, 4])

    def rev(ap2):
        n = ap2.shape[-1]
        a = ap2.ap
        return AP(ap2.tensor, ap2.offset + (n - 1) * a[-1][0], [list(a[0]), [-a[-1][0], n]])

    def mixer_gla(l):
        gq = carve(0, 768, BF16)
        gk = carve(768, 768, BF16)
        gg = carve(1536, 1536, BF16).rearrange("p (j t) -> p j t", j=2)
        gvpad = carve(3072, 3072, BF16).rearrange("p (a h c) -> p a h c", a=12, h=4)
        glrT = carve(6144, 768, BF16)
        Sp = carve(6912, 2304, BF16).rearrange("p (d c j x) -> p d c j x", d=2, c=9, j=2)
        Srun = carve(9216, 128, F32).rearrange("p (d x) -> p d x", d=2)
        U = carve(9344, 64, F32)
        oG = carve(9472, 512, BF16).rearrange("p (j x) -> p j x", j=2)
        S.op("pool", lambda e: e.memset(gvpad, 0.0), [], ["gvpad"])
        S.op("pool", lambda e: e.memset(Sp, 0.0), [], ["Sp"])
        S.op("dve", lambda e: e.memset(wgf[:], 0.0), [], ["wgf"])
        dma("sp", wgf[0:16, 0:128], gla_wg[l, 0, :, :], ["wgf"], ["wgf"])
        dma("sp", wgf[16:32, 128:256], gla_wg[l, 1, :, :], ["wgf"], ["wgf"])
        dma("sp", wgf[32:33, :], gla_bg[l:l + 1].rearrange("a d x -> a (d x)"), ["wgf"], ["wgf"])
        S.op("dve", lambda e: e.tensor_copy(out=wgb[:], in_=wgf[:]), ["wgf"], ["wgb"])
        S.op("dve", lambda e: e.memset(glrT[32:33, :], 1.0), [], ["glrT"])
        wv, wn = load_w(w_in[l, :, 1280:2080], 8, 800, key="gla%d" % l)
        sc_q = 32.0 ** -0.5
        for g in range(3):
            proj_fm(wv, wn, 8, 0, hT, "hT", g, lambda b, bn, g=g: act(gq[:, g * 512:(g + 1) * 512], b[:], AF.Copy, [bn], ["gq"], scale=sc_q))
            proj_fm(wv, wn, 8, 128, hT, "hT", g, lambda b, bn, g=g: act(gk[:, g * 512:(g + 1) * 512], b[:], AF.Copy, [bn], ["gk"]))
            for j in range(2):
                proj_fm(wv, wn, 8, 512 + j * 128, hT, "hT", g, lambda b, bn, j=j, g=g: act(gg[:, j, g * 512:(g + 1) * 512], b[:], AF.Silu, [bn], ["gg"]))
            b, bn = nb()
            for k in range(8):
                mm(b[0:32, :], wv[:, k, 768:800], hT[:, k, g * 512:(g + 1) * 512], k == 0, k == 7, [wn, "hT"], [bn])
            act(glrT[0:32, g * 512:(g + 1) * 512], b[0:32, :], AF.Copy, [bn], ["glrT"])
        for ti in range(NTILE):
            def ev_v(b, bn, ti=ti):
                for h in range(4):
                    act(gvpad[:, ti, h, (h % 2) * 64:(h % 2) * 64 + 64], b[:, h * 64:(h + 1) * 64], AF.Copy, [bn], ["gvpad"])
            proj_tm(wv, wn, 8, 256, 256, hT, "hT", ti, ev_v)
        prefetch("na%d" % l, w_in[l, :, 2080:2848], 8, 768)
        rc = [0]

        def rb():
            i = rc[0] % 16
            rc[0] += 1
            return rotb[:, i, :], ("rotb", i)

        fc = [0]

        def rf():
            i = fc[0] % 8
            fc[0] += 1
            return rotf[:, i, :], ("rotf", i)

        gtb = carve(10496, 1536, BF16).rearrange("p (s c x) -> p s c x", s=3, c=2)
        onesR = carve(12032, 256, BF16)
        S.op("dve", lambda e: e.memset(onesR, 1.0), [], ["onesR"])
        S.op("dve", lambda e: e.memset(onesR.rearrange("p (c x) -> p c x", c=4)[:, :, 0:1], 0.0), ["onesR"], ["onesR"])
        gcache = {}
        gslots = [None, None, None]
        gctr = [0]

        def gtab(g, d):
            if (g, d) in gcache:
                return gcache[(g, d)]
            s = gctr[0] % 3
            gctr[0] += 1
            if gslots[s] is not None:
                del gcache[gslots[s]]
            gslots[s] = (g, d)
            nm = ("gtb", s)
            b, bn = nb()
            mm(b[:], wgb[0:33, d * 128:(d + 1) * 128], glrT[0:33, g * 512:(g + 1) * 512], True, True, ["wgb", "glrT"], [bn])
            sp = tmpf[0][:]
            cs = tmpf[1][:]
            act(sp, b[:], AF.Exp, [bn], ["tmpf0"], scale=-1.0)
            act(sp, sp, AF.Ln, ["tmpf0"], ["tmpf0"], bias=1.0)
            if d == 0:
                S.op("dve", lambda e: e.tensor_tensor_scan(out=cs, data0=onesR, data1=sp, initial=0.0, op0=ALU.mult, op1=ALU.add), ["onesR", "tmpf0"], ["tmpf1"])
                ends = tmpf[1][:].rearrange("p (c x) -> p c x", c=4)[:, :, 127]
            else:
                S.op("dve", lambda e: e.tensor_tensor_scan(out=rev(cs), data0=onesR, data1=rev(sp), initial=0.0, op0=ALU.mult, op1=ALU.add), ["onesR", "tmpf0"], ["tmpf1"])
                ends = tmpf[1][:].rearrange("p (c x) -> p c x", c=4)[:, :, 0]
            act(gtb[:, s, 0, :], cs, AF.Exp, ["tmpf1"], [nm], scale=-1.0 / 16)
            act(gtb[:, s, 1, :], cs, AF.Exp, ["tmpf1"], [nm], scale=1.0 / 16)
            act(gee[:, s, :], ends, AF.Exp, ["tmpf1"], [nm], scale=-1.0 / 16)
            gcache[(g, d)] = (s, nm)
            return gcache[(g, d)]

        def tables(ti, d):
            s, nm = gtab(ti // 4, d)
            cg_ = ti % 4
            return (gtb[:, s, 0, cg_ * 128:(cg_ + 1) * 128], nm, gtb[:, s, 1, cg_ * 128:(cg_ + 1) * 128], nm, gee[:, s, cg_:cg_ + 1])

        for si, (t0, n, latent) in enumerate(SEQS):
            for d in range(2):
                for h in range(4):
                    if latent:
                        dma("sp", Srun[32 * h:32 * h + 32, d, :], sgla0[l, d, h, :, :], [], ["gSrun"])
                if not latent:
                    S.op("dve", lambda e, d=d: e.memset(Srun[:, d, :], 0.0), [], ["gSrun"])
                order = list(range(n)) if d == 0 else list(range(n - 1, -1, -1))
                g1 = {}

                def g1_front(c, d=d, g1=g1):
                    ti = t0 + c
                    Eq_, Eqn, Ek_, Ekn, ee = tables(ti, d)
                    kf, kfn = rf()
                    tt("dve", kf, gk[:, ti * 128:(ti + 1) * 128], Ek_, ALU.mult, ["gk", Ekn], [kfn])
                    b, bn = nb()
                    tr(b[:, 0:128], kf, identF[:], [kfn, "identF"], [bn])
                    kt, ktn = rb()
                    act(kt, b[:, 0:128], AF.Copy, [bn], [ktn])
                    g1[c] = (kt, ktn, ee, Eqn)

                def g1_back(c, d=d, g1=g1):
                    ti = t0 + c
                    kt, ktn, ee, Eqn = g1.pop(c)
                    for hl in range(2):
                        S.op("dve", lambda e, hl=hl, d=d, c=c: e.tensor_copy(out=Sp[:, d, c, hl, hl * 64:(hl + 1) * 64], in_=Srun[:, d, :]), ["gSrun"], ["Sp"])
                    b2, b2n = nb()
                    for h in range(4):
                        mm(b2[:, h * 64:(h + 1) * 64], kt, gvpad[:, ti, h, (h % 2) * 64:(h % 2) * 64 + 64], True, True, [ktn, "gvpad"], [b2n])
                    ts("dve", U, b2[:, 0:64], hmask[:, 0:1], None, ALU.mult, None, [b2n, "hmask"], ["gU"])
                    for h in range(1, 4):
                        stt("dve", U, b2[:, h * 64:(h + 1) * 64], hmask[:, h:h + 1], U, ALU.mult, ALU.add, [b2n, "hmask", "gU"], ["gU"])
                    tt("dve", Srun[:, d, :], Srun[:, d, :], U, ALU.add, ["gSrun", "gU"], ["gSrun"])
                    ts("dve", Srun[:, d, :], Srun[:, d, :], ee, None, ALU.mult, None, ["gSrun", Eqn], ["gSrun"])

                g1_front(order[0])
                for oi, c in enumerate(order):
                    if oi + 1 < len(order):
                        g1_front(order[oi + 1])
                    g1_back(c)
                if not latent:
                    for h in range(4):
                        dma("sp", osgla[si, l, d, h, :, :], Srun[32 * h:32 * h + 32, d, :], ["gSrun"], ["osgla_d"])
            g2u = [(c, d, h) for c in range(n) for d in range(2) for h in range(4)]
            obs = [(pbank[6], "pb6"), (pbank[7], "pb7")]
            tb2 = {}
            g2 = {}

            def g2_front(u):
                c, d, h = u
                ti = t0 + c
                if h == 0:
                    Eq_, Eqn, Ek_, Ekn, ee = tables(ti, d)
                    kfb, kfbn = rb()
                    tt("dve", kfb, gk[:, ti * 128:(ti + 1) * 128], Ek_, ALU.mult, ["gk", Ekn], [kfbn])
                    tb2[(c, d)] = (Eq_, Eqn, kfb, kfbn)
                Eq_, Eqn, kfb, kfbn = tb2[(c, d)]
                qh, qhn = rb()
                stt("dve", qh, gq[:, ti * 128:(ti + 1) * 128], hmask[:, h:h + 1], Eq_, ALU.mult, ALU.mult, ["gq", "hmask", Eqn], [qhn])
                b, bn = nb()
                mm(b[:, 0:128], kfb, qh, True, True, [kfbn, qhn], [bn])
                g2[u] = (qh, qhn, b, bn)

            def g2_mid(u):
                c, d, h = u
                qh, qhn, b, bn = g2[u]
                a_, an_ = rb()
                tt("dve", a_, b[:, 0:128], (maskF if d == 0 else maskB)[:], ALU.mult, [bn, "maskF", "maskB"], [an_])
                g2[u] = (qh, qhn, a_, an_)

            def g2_back(u):
                c, d, h = u
                ti = t0 + c
                g = ti // 4
                cg = ti % 4
                j, hl = h // 2, h % 2
                ob, obn = obs[j]
                qh, qhn, a_, an_ = g2.pop(u)
                first = (d == 0 and hl == 0)
                last = (d == 1 and hl == 1)
                mm(ob[:, 0:128], gvpad[:, ti, h, :], a_, first, False, ["gvpad", an_], [obn])
                mm(ob[:, 0:128], Sp[:, d, c, hl, :], qh, False, last, ["Sp", qhn], [obn])
                if d == 1 and h == 3:
                    for jj in range(2):
                        act(oG[:, jj, cg * 128:(cg + 1) * 128], obs[jj][0][:, 0:128], AF.Copy, [obs[jj][1]], ["goG"])
                    if cg == 3:
                        for jj in range(2):
                            head_norm(oG, "goG", jj, g, glagnT[:, l * 2 + jj:l * 2 + jj + 1], gg, "gg", 4 + jj)

            for ui in range(len(g2u) + 2):
                if ui < len(g2u):
                    g2_front(g2u[ui])
                if 1 <= ui <= len(g2u):
                    g2_mid(g2u[ui - 1])
                if ui >= 2:
                    g2_back(g2u[ui - 2])

    lreT = sbt("lreT", [128, 64]); limT = sbt("limT", [128, 64]); ldtT = sbt("ldtT", [128, 64])
    rows_to_cols(s5_lre2[:, :], 64, lreT[:], "lreT")
    rows_to_cols(s5_lim2[:, :], 64, limT[:], "limT")
    rows_to_cols(s5_ldt2[:, :], 64, ldtT[:], "ldtT")
    cmask = sbt("cmask", [128, 4, 128], BF16)
    dma("pool", cmask[:], cd["cmask"][:, :, :], [], ["cmask"])
    PI = math.pi
    negpi = sbt("negpi", [128, 1]); glm = sbt("glm", [128, 2]); s5fin = sbt("s5fin", [128, 8, 2]); s5x0 = sbt("s5x0", [128, 8, 2])
    S.op("dve", lambda e: e.memset(negpi[:], -PI), [], ["negpi"])
    S.op("dve", lambda e: e.memset(glm[:], 0.0), [], ["glm"])
    S.op("dve", lambda e: e.memset(glm[0:64, 0:1], 1.0), ["glm"], ["glm"])
    S.op("dve", lambda e: e.memset(glm[64:128, 1:2], 1.0), ["glm"], ["glm"])

    def rev3(ap3):
        a = ap3.ap
        n = a[-1][1]
        return AP(ap3.tensor, ap3.offset + (n - 1) * a[-1][0], [list(a[0]), list(a[1]), [-a[-1][0], n]])

    I32 = mybir.dt.int32

    def sin_reduced(out, ang, w, names_in, name_out, out_view=None):
        q = tmpf[1][:, 0:w]
        qi = tmpf[1][:, 256:256 + w].bitcast(I32)
        r = tmpf[1][:, 128:128 + w]
        c = tmpf[1][:, 384:384 + w]
        ts("dve", q, ang, 1.0 / (2 * PI), None, ALU.mult, None, names_in, ["tmpf1"])
        S.op("dve", lambda e: e.tensor_copy(out=qi, in_=q), ["tmpf1"], ["tmpf1"])
        S.op("dve", lambda e: e.tensor_copy(out=q, in_=qi), ["tmpf1"], ["tmpf1"])
        stt("dve", r, q, -2 * PI, ang, ALU.mult, ALU.add, ["tmpf1"] + names_in, ["tmpf1"])
        ts("dve", c, r, PI, -2 * PI, ALU.is_gt, ALU.mult, ["tmpf1"], ["tmpf1"])
        tt("dve", r, r, c, ALU.add, ["tmpf1"], ["tmpf1"])
        ts("dve", c, r, -PI, 2 * PI, ALU.is_lt, ALU.mult, ["tmpf1"], ["tmpf1"])
        tt("dve", r, r, c, ALU.add, ["tmpf1"], ["tmpf1"])
        act(out, r if out_view is None else out_view(r), AF.Sin, ["tmpf1"], name_out)

    def mixer_s5(l):
        su = carve(0, 1536, BF16).rearrange("p (j t) -> p j t", j=2)
        ysum = carve(1536, 1536, BF16).rearrange("p (j t) -> p j t", j=2)
        s5x = carve(3072, 1536, BF16)
        rtab = carve(3072, 1024, F32).rearrange("p (g x) -> p g x", g=8)
        Bb = carve(4608, 2048, BF16).rearrange("p (d c g x) -> p d c g x", d=2, c=2, g=8)
        Cl = carve(6656, 2048, BF16).rearrange("p (d c g x) -> p d c g x", d=2, c=2, g=8)
        csT = carve(8704, 2048, BF16).rearrange("p (d c g x) -> p d c g x", d=2, c=2, g=8)
        sm = carve(10752, 256, F32).rearrange("p (a g) -> p a g", g=8)
        wri = carve(11008, 1024, F32).rearrange("p (c g x) -> p c g x", c=2, g=4)
        zri = rotf[:].rearrange("p (c g) x -> p c g x", c=2)
        xb = rotb[:].rearrange("p (s g) x -> p s g x", s=4)
        DT, A_, W_, R_, T0, T1, NR, NI, DEN, CR, CI = range(11)
        E1R, E1I, E127R, E127I, E128R, E128I = 11, 12, 13, 14, 15, 16
        WIR, WII = 17, 18
        smd = lambda d, i: sm[:, (d * 0 + i), :]
        bbf = wri.rearrange("p c g x -> p (c g x)")
        wv, wn = load_w(w_in[l, :, 1024:1280], 8, 256, key="s5%d" % l)
        for g in range(3):
            for j in range(2):
                proj_fm(wv, wn, 8, j * 128, hT, "hT", g, lambda b, bn, j=j, g=g: act(su[:, j, g * 512:(g + 1) * 512], b[:], AF.Copy, [bn], ["su"]))
        def sin_wide(out, ang, names_in, name_out):
            ys_ = carve(1536, 3072, F32)
            q = ys_[:, 0:1024]
            r = ys_[:, 1024:2048]
            c = ys_[:, 2048:3072]
            qi = rotb[:].rearrange("p a x -> p (a x)").bitcast(I32)
            ts("dve", q, ang, 1.0 / (2 * PI), None, ALU.mult, None, names_in, ["s5q"])
            S.op("dve", lambda e: e.tensor_copy(out=qi, in_=q), ["s5q"], ["s5qi"])
            S.op("dve", lambda e: e.tensor_copy(out=q, in_=qi), ["s5qi"], ["s5q"])
            stt("dve", r, q, -2 * PI, ang, ALU.mult, ALU.add, ["s5q"] + names_in, ["s5r"])
            ts("dve", c, r, PI, -2 * PI, ALU.is_gt, ALU.mult, ["s5r"], ["s5c"])
            tt("dve", r, r, c, ALU.add, ["s5r", "s5c"], ["s5r"])
            ts("dve", c, r, -PI, 2 * PI, ALU.is_lt, ALU.mult, ["s5r"], ["s5c"])
            tt("dve", r, r, c, ALU.add, ["s5r", "s5c"], ["s5r"])
            act(out, r, AF.Sin, ["s5r"], name_out)

        par = {}
        for d in range(2):
            col = (l * 2 + d) * 8
            pr = {}
            t = lambda i, d=d: sm[:, d * 16 + i, :]
            N = "s5sm"
            act(t(0), ldtT[:, col:col + 8], AF.Exp, ["ldtT"], [N])
            tt("dve", t(1), lreT[:, col:col + 8], t(0), ALU.mult, ["lreT", N], [N])
            tt("dve", t(2), limT[:, col:col + 8], t(0), ALU.mult, ["limT", N], [N])
            act(t(3), t(1), AF.Exp, [N], [N])
            ang3 = tmpf[0][:, 0:24]
            ang3v = ang3.rearrange("p (m g) -> p m g", m=3)
            for mi, mult in enumerate((1.0, 127.0, 128.0)):
                ts("dve", ang3v[:, mi, :], t(2), float(mult), None, ALU.mult, None, [N], ["tmpf0"])
            r0 = sm[:, d * 16 + 6, :]
            cos_out = AP(r0.tensor, r0.offset, [list(r0.ap[0]), [16, 3], [1, 8]])
            sin_out = AP(r0.tensor, r0.offset + 8, [list(r0.ap[0]), [16, 3], [1, 8]])
            sin_reduced(sin_out, ang3, 24, ["tmpf0"], [N], out_view=lambda a: a.rearrange("p (m g) -> p m g", m=3))
            ts("dve", ang3, ang3, PI / 2, None, ALU.add, None, ["tmpf0"], ["tmpf0"])
            sin_reduced(cos_out, ang3, 24, ["tmpf0"], [N], out_view=lambda a: a.rearrange("p (m g) -> p m g", m=3))
            tt("dve", t(12), t(3), t(6), ALU.mult, [N], [N])
            ts("dve", t(12), t(12), -1.0, None, ALU.add, None, [N], [N])
            tt("dve", t(13), t(3), t(7), ALU.mult, [N], [N])
            tt("dve", t(4), lreT[:, col:col + 8], lreT[:, col:col + 8], ALU.mult, ["lreT"], [N])
            tt("dve", t(5), limT[:, col:col + 8], limT[:, col:col + 8], ALU.mult, ["limT"], [N])
            tt("dve", t(4), t(4), t(5), ALU.add, [N], [N])
            S.op("dve", lambda e, t=t: e.reciprocal(t(4), t(4)), [N], [N])
            tt("dve", t(14), t(12), lreT[:, col:col + 8], ALU.mult, [N, "lreT"], [N])
            tt("dve", t(5), t(13), limT[:, col:col + 8], ALU.mult, [N, "limT"], [N])
            tt("dve", t(14), t(14), t(5), ALU.add, [N], [N])
            tt("dve", t(14), t(14), t(4), ALU.mult, [N], [N])
            tt("dve", t(15), t(13), lreT[:, col:col + 8], ALU.mult, [N, "lreT"], [N])
            tt("dve", t(5), t(12), limT[:, col:col + 8], ALU.mult, [N, "limT"], [N])
            tt("dve", t(15), t(15), t(5), ALU.subtract, [N], [N])
            tt("dve", t(15), t(15), t(4), ALU.mult, [N], [N])
            for ii in (6, 7, 10, 11):
                tt("dve", t(ii), t(ii), t(3), ALU.mult, [N], [N])
            angw = rotf[:]
            pb_ = posfb[:, 0, :]
            posb = AP(pb_.tensor, pb_.offset, [list(pb_.ap[0]), [0, 8], list(pb_.ap[1])])
            wb_ = bcast_last(t(2), 128)
            tt("dve", angw, posb, wb_, ALU.mult, ["posfb", N], ["rotf"])
            tt("dve", angw, angw, wb_, ALU.subtract, ["rotf", N], ["rotf"])
            sin_wide(csT[:, d, 1].rearrange("p g x -> p (g x)"), rotf[:].rearrange("p g x -> p (g x)"), ["rotf"], ["csT"])
            ts("dve", angw, angw, PI / 2, None, ALU.add, None, ["rotf"], ["rotf"])
            sin_wide(csT[:, d, 0].rearrange("p g x -> p (g x)"), rotf[:].rearrange("p g x -> p (g x)"), ["rotf"], ["csT"])
            bre = bbf[:, 0:128].rearrange("p (g x) -> p g x", g=8)
            bim = bbf[:, 128:256].rearrange("p (g x) -> p g x", g=8)
            bbr = bbf[:, 256:384].rearrange("p (g x) -> p g x", g=8)
            bbi = bbf[:, 384:512].rearrange("p (g x) -> p g x", g=8)
            tmpb = bbf[:, 512:640].rearrange("p (g x) -> p g x", g=8)
            inb = bbf[:, 640:768]
            for (dst_, src_) in ((bre, s5_bre), (bim, s5_bim)):
                e0 = src_[l, d, 0:1, 0:1, 0:1]
                sap = AP(e0.tensor, e0.offset, [[16, 128], [2048, 8], [1, 16]])
                dma("sp", dst_, sap, [], ["s5b"])
            crb = bcast_last(t(14), 16)
            cib = bcast_last(t(15), 16)
            tt("dve", bbr, bre, crb, ALU.mult, ["s5b", N], ["s5b"])
            tt("dve", tmpb, bim, cib, ALU.mult, ["s5b", N], ["s5b"])
            tt("dve", bbr, bbr, tmpb, ALU.subtract, ["s5b"], ["s5b"])
            tt("dve", bbi, bim, crb, ALU.mult, ["s5b", N], ["s5b"])
            tt("dve", tmpb, bre, cib, ALU.mult, ["s5b", N], ["s5b"])
            tt("dve", bbi, bbi, tmpb, ALU.add, ["s5b"], ["s5b"])
            for c_, bb_ in ((0, bbr), (1, bbi)):
                for kc in range(2):
                    inv = inb.rearrange("p (a b x) -> p a b x", a=4, b=2)
                    for glb in range(2):
                        ts("dve", inv[:, :, glb, :], bb_[:, 4 * kc:4 * kc + 4, :], glm[:, glb:glb + 1], None, ALU.mult, None, ["s5b", "glm"], ["s5in"])
                    b, bn = nb()
                    tr(b[:, 0:128], inb, identF[:], ["s5in", "identF"], [bn])
                    for gpl in range(4):
                        ts("dve", Bb[:, d, c_, 4 * kc + gpl, :], b[:, 0:128], hmask[:, gpl:gpl + 1], None, ALU.mult, None, [bn, "hmask"], ["Bb"])
            for c_, src_ in ((0, s5_cre), (1, s5_cim)):
                for mc in range(2):
                    cin = bbf[:, 768:896].rearrange("p (b x) -> p b x", b=2)
                    rows = src_[l, d].rearrange("g h p -> (g h) p")[mc * 128:(mc + 1) * 128, :]
                    for glb in range(2):
                        dma("sp", cin[:, glb, :], rows, [], ["s5cin"])
                    b, bn = nb()
                    tr(b[:, 0:128], bbf[:, 768:896], identF[:], ["s5cin", "identF"], [bn])
                    cT_ = bbf[:, 896:1024]
                    act(cT_, b[:, 0:128], AF.Copy, [bn], ["s5cT"], scale=(1.0 if c_ == 0 else -1.0))
                    for gpl in range(4):
                        tt("dve", Cl[:, d, c_, 4 * mc + gpl, :], cT_, cmask[:, gpl, :], ALU.mult, ["s5cT", "cmask"], ["Cl"])
        S.barrier()
        def cmul(outr, outi, ar, ai, br, bi, n1, n2, wd=8, eng="dve"):
            r4, r5, r6 = (4, 5, 20) if eng == "dve" else (14, 15, 30)
            T4 = sm[:, r4, 0:wd]
            T5 = sm[:, r5, 0:wd]
            T6 = sm[:, r6, 0:wd]
            tn = "s5tmp_" + eng
            tt(eng, T4, ar, br, ALU.mult, n1, [tn])
            tt(eng, T5, ai, bi, ALU.mult, n1, [tn])
            tt(eng, T5, T4, T5, ALU.subtract, [tn], [tn])
            tt(eng, T4, ar, bi, ALU.mult, n1, [tn])
            tt(eng, T6, ai, br, ALU.mult, n1, [tn])
            tt(eng, outi, T6, T4, ALU.add, [tn], n2)
            S.op(eng, lambda e: e.tensor_copy(out=outr, in_=T5), [tn], n2)

        slot = [0]
        ucnt = [0]
        def bslot(ap2):
            return ap2.rearrange("p (g x) -> p g x", g=4)
        tf0 = tmpf[0][:].bitcast(BF16)
        tf1 = tmpf[1][:].bitcast(BF16)
        rfb = rotf[:].rearrange("p a x -> p (a x)").bitcast(BF16)
        wsb = wri.rearrange("p c g x -> p (c g x)").bitcast(BF16)
        s5slots = {("bu", 0, 0): bslot(tf0[:, 0:512]), ("bu", 0, 1): bslot(tf0[:, 512:1024]),
                   ("bu", 1, 0): bslot(tf1[:, 0:512]), ("bu", 1, 1): bslot(tf1[:, 512:1024]),
                   ("t", 0): bslot(rfb[:, 0:512]), ("t", 1): bslot(rfb[:, 512:1024]), ("t", 2): bslot(rfb[:, 1024:1536]), ("t", 3): bslot(rfb[:, 1536:2048]),
                   ("z", 0): bslot(wsb[:, 0:512]), ("z", 1): bslot(wsb[:, 512:1024]),
                   ("w", 0, 0): bslot(wsb[:, 1024:1536]), ("w", 0, 1): bslot(wsb[:, 1536:2048]),
                   ("w", 1, 0): bslot(sqb[0][:]), ("w", 1, 1): bslot(sqb[1][:])}
        S.barrier()
        rt_dir = [None]
        units = []
        for si, (t0, n, latent) in enumerate(SEQS):
            for d in range(2):
                order = list(range(n)) if d == 0 else list(range(n - 1, -1, -1))
                for ci_, c in enumerate(order):
                    for kc in range(2):
                        units.append(dict(si=si, t0=t0, n=n, latent=latent, d=d, ci=ci_, c=c, kc=kc, u=len(units) % 2))

        def stage_a(un):
            d, kc, ti, u_ = un["d"], un["kc"], un["t0"] + un["c"], un["u"]
            br_, brn = nb()
            bi_, bin_ = nb()
            for gpl in range(4):
                gp = 4 * kc + gpl
                mm(br_[:, gpl * 128:(gpl + 1) * 128], Bb[:, d, 0, gp, :], su[:, kc, ti * 128:(ti + 1) * 128], True, True, ["Bb", "su"], [brn])
                mm(bi_[:, gpl * 128:(gpl + 1) * 128], Bb[:, d, 1, gp, :], su[:, kc, ti * 128:(ti + 1) * 128], True, True, ["Bb", "su"], [bin_])
            bur = br_[:].rearrange("p (g x) -> p g x", g=4)
            bui = bi_[:].rearrange("p (g x) -> p g x", g=4)
            if d == 1:
                bur = rev3(bur)
                bui = rev3(bui)
            burb, buib = s5slots[("bu", u_, 0)], s5slots[("bu", u_, 1)]
            bun = ("s5bu", u_)
            act(burb, bur, AF.Copy, [brn], [bun])
            act(buib, bui, AF.Copy, [bin_], [bun])

        def stage_b(un):
            si, t0, n, latent, d, ci_, c, kc, u_ = (un[x] for x in ("si", "t0", "n", "latent", "d", "ci", "c", "kc", "u"))
            ti = t0 + c
            t = lambda i, d=d: sm[:, d * 16 + i, :]
            WR = t(12)
            WI = t(13)
            if ci_ == 0 and kc == 0:
                if latent:
                    e0 = ss50[l, d, 0:1, 0:1, 0:1]
                    for glb in range(2):
                        sap = AP(e0.tensor, e0.offset + glb * 128, [[2, 64], [256, 8], [1, 2]])
                        dma("sp", s5x0[glb * 64:(glb + 1) * 64, :, :], sap, [], ["s5x0"])
                    cmul(WR, WI, s5x0[:, :, 0], s5x0[:, :, 1], t(6), t(7), ["s5x0", "s5sm"], ["s5w0"])
                else:
                    S.op("dve", lambda e, WR=WR: e.memset(WR, 0.0), [], ["s5w0"])
                    S.op("dve", lambda e, WI=WI: e.memset(WI, 0.0), [], ["s5w0"])
            cs_ = csT[:, d, 0, 4 * kc:4 * kc + 4, :]
            sn_ = csT[:, d, 1, 4 * kc:4 * kc + 4, :]
            burb, buib = s5slots[("bu", u_, 0)], s5slots[("bu", u_, 1)]
            bun = ("s5bu", u_)
            t1, t2 = s5slots[("t", 0)], s5slots[("t", 1)]
            t3, t4 = s5slots[("t", 2)], s5slots[("t", 3)]
            zr, zi = s5slots[("z", 0)], s5slots[("z", 1)]
            wr_, wi_ = s5slots[("w", u_, 0)], s5slots[("w", u_, 1)]
            wrn = ("s5w", u_)
            tt("dve", t1, burb, cs_, ALU.mult, [bun, "csT"], [("s5t", 0)])
            tt("dve", t2, buib, sn_, ALU.mult, [bun, "csT"], [("s5t", 1)])
            tt("dve", t3, buib, cs_, ALU.mult, [bun, "csT"], [("s5t", 2)])
            tt("dve", t4, burb, sn_, ALU.mult, [bun, "csT"], [("s5t", 3)])
            tt("dve", zr, t1, t2, ALU.add, [("s5t", 0), ("s5t", 1)], ["s5zr"])
            tt("dve", zi, t3, t4, ALU.subtract, [("s5t", 2), ("s5t", 3)], ["s5zi"])
            gsl0 = slice(4 * kc, 4 * kc + 4)
            if rt_dir[0] != d:
                rt_dir[0] = d
                S.op("dve", lambda e, d=d: e.tensor_copy(out=rtab, in_=bcast_last(sm[:, d * 16 + 3, :], 128)), ["s5sm"], ["rtab"])
                S.op("dve", lambda e: e.memset(rtab[:, :, 0:1], 0.0), ["rtab"], ["rtab"])
            if not (ci_ == 0 and not latent):
                tt("dve", zr[:, :, 0], zr[:, :, 0], WR[:, gsl0], ALU.add, ["s5zr", ("s5w0", kc)], ["s5zr"])
                tt("dve", zi[:, :, 0], zi[:, :, 0], WI[:, gsl0], ALU.add, ["s5zi", ("s5w0", kc)], ["s5zi"])
            fl = lambda a3: a3.rearrange("p g x -> p (g x)")
            rt_ = fl(rtab[:, 4 * kc:4 * kc + 4, :])
            S.op("dve", lambda e, rt_=rt_, zr=zr, wr_=wr_: e.tensor_tensor_scan(out=fl(wr_), data0=rt_, data1=fl(zr), initial=0.0, op0=ALU.mult, op1=ALU.add),
                 ["rtab", "s5zr"], [wrn])
            S.op("dve", lambda e, rt_=rt_, zi=zi, wi_=wi_: e.tensor_tensor_scan(out=fl(wi_), data0=rt_, data1=fl(zi), initial=0.0, op0=ALU.mult, op1=ALU.add),
                 ["rtab", "s5zi"], [wrn])
            s0 = slot[0] % 2
            slot[0] += 1
            p1, p2 = xb[:, 2 * s0], xb[:, 2 * s0 + 1]
            p3, p4 = bslot(s5x[:, 2048:2560]), bslot(s5x[:, 2560:3072])
            pn34 = [("s5p", 3), ("s5p", 4)]
            xn = ("xb", s0)
            act(t1, wi_, AF.Copy, [wrn], [("s5t", 0)], scale=-1.0)
            tt("dve", p1, wr_, cs_, ALU.mult, [wrn, "csT"], [xn])
            tt("dve", p3, wr_, sn_, ALU.mult, [wrn, "csT"], [pn34[0]])
            tt("dve", p4, wi_, cs_, ALU.mult, [wrn, "csT"], [pn34[1]])
            tt("dve", p2, t1, sn_, ALU.mult, [("s5t", 0), "csT"], [xn])
            last = (ci_ == n - 1)
            gsl = slice(4 * kc, 4 * kc + 4)
            wer = wr_[:, :, 127]
            wei = wi_[:, :, 127]
            wn_ = [wrn, "s5sm"]
            if last and not latent:
                cmul(s5fin[:, gsl, 0], s5fin[:, gsl, 1], wer, wei, t(8)[:, gsl], t(9)[:, gsl], wn_, ["s5fin"], 4, "pool")
            if not last:
                cmul(WR[:, gsl], WI[:, gsl], wer, wei, t(10)[:, gsl], t(11)[:, gsl], wn_, [("s5w0", kc)], 4, "pool")
            by, byn = nb()
            k_ = 0
            for gpl in range(4):
                gp = 4 * kc + gpl
                for (pp, pnm, cc) in ((p1, xn, 0), (p2, xn, 0), (p3, pn34[0], 1), (p4, pn34[1], 1)):
                    mm(by[:, 0:128], Cl[:, d, cc, gp, :], pp[:, gpl, :], k_ == 0, k_ == 15, ["Cl", pnm], [byn])
                    k_ += 1
            ysl = ysum[:, kc, ti * 128:(ti + 1) * 128]
            if d == 0:
                act(ysl, by[:, 0:128], AF.Copy, [byn], [("ysum", (kc, ti))])
            else:
                ysr = AP(ysl.tensor, ysl.offset + 127 * ysl.ap[-1][0], [list(ysl.ap[0]), [-ysl.ap[-1][0], 128]])
                tt("dve", ysr, by[:, 0:128], ysr, ALU.add, [byn, ("ysum", (kc, ti))], [("ysum", (kc, ti))])
            if last and kc == 1 and not latent:
                e0 = oss5[si, l, d, 0:1, 0:1, 0:1]
                for glb in range(2):
                    dap = AP(e0.tensor, e0.offset + glb * 128, [[2, 64], [256, 8], [1, 2]])
                    dma("sp", dap, s5fin[glb * 64:(glb + 1) * 64, :, :], ["s5fin"], ["oss5_d"])

        do_mod = (l + 1 < DEPTH)
        modq = []
        if do_mod:
            align_w()
            prefetch("ada%d_0" % (l + 1), mod_src(l + 1, 0), 8, 1024)
        stage_a(units[0])
        for i_, un in enumerate(units):
            if i_ + 1 < len(units):
                stage_a(units[i_ + 1])
            if do_mod and i_ % 8 == 2:
                cb = i_ // 8
                if modq:
                    mod_evac(l + 1, *modq.pop(0))
                if cb + 1 < 6:
                    prefetch("ada%d_%d" % (l + 1, cb + 1), mod_src(l + 1, cb + 1), 8, 1024)
                bk = (pbank[6], "pb6") if cb % 2 == 0 else (pbank[7], "pb7")
                b_, bn_ = mod_mm(l + 1, cb, bank=bk)
                modq.append((cb, b_, bn_))
            stage_b(un)
        while modq:
            mod_evac(l + 1, *modq.pop(0))
        if do_mod:
            mod_finish(l + 1)
        align_w()
        prefetch("glu%d" % l, s5_wglu[l, :, :], 2, 512)
        prefetch("gla%d" % l, w_in[l, :, 1280:2080], 8, 800)
        S.barrier()
        ge = carve(4608, 1536, BF16).rearrange("p (j t) -> p j t", j=2)
        for j in range(2):
            stt("dve", ysum[:, j, :], su[:, j, :], s5dT[:, l * 2 + j:l * 2 + j + 1], ysum[:, j, :], ALU.mult, ALU.add, ["su", "s5dT", "ysum"], ["ysum"])
            for g in range(3):
                ysl = ysum[:, j, g * 512:(g + 1) * 512]
                tt("dve", tmpf[0][:], ysl, ysl, ALU.mult, ["ysum"], ["tmpf0"])
                ts("dve", tmpf[0][:], tmpf[0][:], 0.044715, 1.0, ALU.mult, ALU.add, ["tmpf0"], ["tmpf0"])
                tt("dve", tmpf[0][:], tmpf[0][:], ysl, ALU.mult, ["tmpf0", "ysum"], ["tmpf0"])
                act(tmpf[1][:], tmpf[0][:], AF.Sigmoid, ["tmpf0"], ["tmpf1"], scale=1.5957691216057308)
                tt("dve", ge[:, j, g * 512:(g + 1) * 512], ysl, tmpf[1][:], ALU.mult, ["ysum", "tmpf1"], ["s5ge"])
        wv, wn = load_w(s5_wglu[l, :, :], 2, 512, key="glu%d" % l)
        for j in range(2):
            for g in range(3):
                gt = sqb[g % 2]
                gtn = "sqb%d" % (g % 2)
                proj_fm(wv, wn, 2, (2 + j) * 128, ge, "s5ge", g,
                        lambda b, bn, gt=gt, gtn=gtn, j=j: act(gt[:], b[:], AF.Sigmoid, [bn, "bgluT"], [gtn], bias=bgluT[:, l * 4 + 2 + j:l * 4 + 3 + j]))
                proj_fm(wv, wn, 2, j * 128, ge, "s5ge", g,
                        lambda b, bn, gt=gt, gtn=gtn, j=j, g=g: stt("dve", brT[:, 2 + j, g * 512:(g + 1) * 512], b[:], bgluT[:, l * 4 + j:l * 4 + j + 1], gt[:],
                                                                 ALU.add, ALU.mult, [bn, "bgluT", gtn], [("brT", 2 + j)]))

    def mixer_na(l):
        nq = carve(0, 1536, BF16).rearrange("p (j t) -> p j t", j=2)
        nk = carve(1536, 1536, BF16).rearrange("p (j t) -> p j t", j=2)
        vpad = carve(3072, 3072, BF16).rearrange("p (a h c) -> p a h c", a=12, h=4)
        kcT = carve(6144, 256, BF16).rearrange("p (j t) -> p j t", j=2)
        vcpad = carve(6400, 512, BF16).rearrange("p (a h c) -> p a h c", a=2, h=4)
        opad = carve(6912, 256, BF16).rearrange("p (h c) -> p h c", h=4)
        BT = carve(7168, 1920, BF16).rearrange("p (h d c) -> p h d c", h=4, d=15)
        negt = carve(9088, 64, F32)
        pts = [carve(9152 + i * 256, 256, BF16) for i in range(3)]
        stg = carve(9920, 256, F32)
        stg2 = carve(10176, 256, F32)
        BTall = carve(0, 3840, F32).rearrange("p (a c) -> p a c", a=60)
        den = carve(11392, 512, F32)
        dma("sp", rowst[0:60, 0:31], na_rpb[l].rearrange("h d i -> (h d) i"), [], ["rowst"])
        b, bn = nb()
        tr(b[0:31, 0:60], rowst[0:60, 0:31], identF[0:60, 0:60], ["rowst", "identF"], [bn])
        act(Rrp[0:31, :], b[0:31, 0:60], AF.Copy, [bn], ["Rrp"])
        for q8 in range(8):
            b, bn = nb()
            for qq in range(8):
                qc = q8 * 8 + qq
                g0 = gbig[0:31, 63 - qc:63 - qc + 64]
                mm(b[0:64, qq * 60:(qq + 1) * 60], g0, Rrp[0:31, :], True, True, ["gbig", "Rrp"], [bn])
            bi = b[0:64, 0:1]
            src = AP(bi.tensor, bi.offset, [list(bi.ap[0]), [1, 60], [60, 8]])
            S.op("dve", lambda e, src=src, q8=q8: e.tensor_copy(out=BTall[0:64, :, q8 * 8:(q8 + 1) * 8], in_=src), [bn], ["BTall"])
        m0 = mneg[0:64, :]
        mk = AP(m0.tensor, m0.offset, [list(m0.ap[0]), [0, 60], list(m0.ap[1])])
        BT64 = carve(3840, 1920, BF16)
        tt("dve", BT64[0:64].rearrange("p (a c) -> p a c", a=60), BTall[0:64], mk, ALU.add, ["BTall", "mneg"], ["BT64"])
        BTflat = BT.rearrange("p h d c -> p (h d c)")
        for i8 in range(8):
            b, bn = nb()
            mm(b[:, 0:480], dupI[0:64, :], BT64[0:64, i8 * 480:(i8 + 1) * 480], True, True, ["dupI", "BT64"], [bn])
            act(BTflat[:, i8 * 480:(i8 + 1) * 480], b[:, 0:480], AF.Copy, [bn], ["BT"])
        S.barrier()
        S.op("pool", lambda e: e.memset(vpad, 0.0), [], ["vpad"])
        S.op("pool", lambda e: e.memset(vcpad, 0.0), [], ["vcpad"])
        S.op("pool", lambda e: e.memset(opad, 0.0), [], ["opad"])
        S.op("dve", lambda e: e.memset(negt, NEG), [], ["negt"])
        for h in range(4):
            S.op("pool", lambda e, h=h: e.memset(opad[:, h, (h % 2) * 64:(h % 2) * 64 + 64], 1.0), ["opad"], ["opad"])
        wv, wn = load_w(w_in[l, :, 2080:2848], 8, 768, key="na%d" % l)
        for g in range(3):
            for j in range(2):
                proj_fm(wv, wn, 8, j * 128, hT, "hT", g,
                        lambda b, bn, j=j, g=g: act(nq[:, j, g * 512:(g + 1) * 512], b[:], AF.Copy, [bn], ["nq"], scale=0.125))
                proj_fm(wv, wn, 8, 256 + j * 128, hT, "hT", g,
                        lambda b, bn, j=j, g=g: act(nk[:, j, g * 512:(g + 1) * 512], b[:], AF.Copy, [bn], ["nk"]))
        for ti in range(NTILE):
            def ev_v(b, bn, ti=ti):
                for h in range(4):
                    act(vpad[:, ti, h, (h % 2) * 64:(h % 2) * 64 + 64], b[:, h * 64:(h + 1) * 64], AF.Copy, [bn], ["vpad"])
                if ti < 4:
                    act(stg2, b[:, 0:256], AF.Copy, [bn], ["stg2"])
                    dma("sp", onv[ti // 2, l, (ti % 2) * 128:(ti % 2 + 1) * 128, :], stg2, ["stg2"], ["onv_d"])
            proj_tm(wv, wn, 8, 512, 256, hT, "hT", ti, ev_v)
            if ti < 4:
                def ev_k(b, bn, ti=ti):
                    act(stg, b[:, 0:256], AF.Copy, [bn], ["stg"])
                    dma("sp", onk[ti // 2, l, (ti % 2) * 128:(ti % 2 + 1) * 128, :], stg, ["stg"], ["onk_d"])
                proj_tm(wv, wn, 8, 256, 256, hT, "hT", ti, ev_k)
        align_w()
        prefetch("mg%d_0_0" % l, w_merge[l, :, 0:512], 8, 512)
        prefetch("br%d_0_0" % l, w_branch[l, 0, :, 0:512], 2, 512)
        for a in range(2):
            dma("sp", stg, ck[l, a * 128:(a + 1) * 128, :], [], ["stg"])
            b, bn = nb()
            for j in range(2):
                tr(b[:, j * 128:(j + 1) * 128], stg[:, j * 128:(j + 1) * 128], identF[:], ["stg", "identF"], [bn])
            act(kcT[:, :, a * 128:(a + 1) * 128], b[:, 0:256].rearrange("p (j t) -> p j t", j=2), AF.Copy, [bn], ["kcT"])
            dma("sp", stg2, cvv[l, a * 128:(a + 1) * 128, :], [], ["stg2"])
            for h in range(4):
                S.op("dve", lambda e, a=a, h=h: e.tensor_copy(out=vcpad[:, a, h, (h % 2) * 64:(h % 2) * 64 + 64], in_=stg2[:, h * 64:(h + 1) * 64]),
                     ["stg2"], ["vcpad"])
        pctr = [0]

        def attend(j, qlo, qn, keyspecs, out_cols):
            bo_, bon = pbank[6], "pb6"
            bd_, bdn = pbank[7], "pb7"
            n = len(keyspecs)
            LA = 2
            pend = {}

            def front(idx):
                (h, kT, kname, vp, vname, biasf) = keyspecs[idx]
                hl = h % 2
                bs_, bsn = nb()
                mm(bs_[:, 0:qn], kT, nq[hl * 64:(hl + 1) * 64, j, qlo:qlo + qn], True, True, [kname, "nq"], [bsn])
                pend[idx] = (bs_, bsn)

            def mid(idx):
                (h, kT, kname, vp, vname, biasf) = keyspecs[idx]
                bs_, bsn = pend[idx]
                pt = pts[pctr[0] % 3]
                ptn = "pt%d" % (pctr[0] % 3)
                pctr[0] += 1
                if biasf is None:
                    act(pt[:, 0:qn], bs_[:, 0:qn], AF.Exp, [bsn], [ptn])
                else:
                    tb, tbn = tmpf[idx % 2], "tmpf%d" % (idx % 2)
                    biasf(bs_, bsn, tb, tbn)
                    act(pt[:, 0:qn], tb[:, 0:qn], AF.Exp, [tbn], [ptn])
                pend[idx] = (h, vp, vname, pt, ptn)

            def back(idx):
                (h, vp, vname, pt, ptn) = pend.pop(idx)
                mm(bo_[:, 0:qn], vp, pt[:, 0:qn], idx == 0, idx == n - 1, [vname, ptn], [bon])
                mm(bd_[:, 0:qn], opad[:, h, :], pt[:, 0:qn], idx == 0, idx == n - 1, ["opad", ptn], [bdn])

            for i_ in range(n + 3):
                if i_ < n:
                    front(i_)
                if 1 <= i_ <= n:
                    mid(i_ - 1)
                if i_ >= 3:
                    back(i_ - 3)
            S.op("dve", lambda e: e.reciprocal(den[:, 0:qn], bd_[:, 0:qn]), [bdn], ["den"])
            tt("dve", brT[:, 6 + j, out_cols], bo_[:, 0:qn], den[:, 0:qn], ALU.mult, [bon, "den"], [("brT", 6 + j)])

        for sq in range(2):
            t0 = sq * 256
            for j in range(2):
                specs = []
                for hl in range(2):
                    h = 2 * j + hl
                    for kt in range(2):
                        ti = sq * 2 + kt
                        specs.append((h, nk[hl * 64:(hl + 1) * 64, j, ti * 128:(ti + 1) * 128], "nk", vpad[:, ti, h, :], "vpad", None))
                attend(j, t0, 256, specs, slice(t0, t0 + 256))
        for qi in range(8):
            t0 = 512 + qi * 128
            rows = [2 * qi, 2 * qi + 1]
            rs = [min(max(r - 4, 0), 8) for r in rows]
            jlo = rs[0] // 2
            jhi = (rs[1] + 7) // 2
            for j in range(2):
                specs = []
                for hl in range(2):
                    h = 2 * j + hl
                    for kj in range(jlo, jhi + 1):
                        ti = 4 + kj

                        def biasf(bs_, bsn, tb, tbn, h=h, kj=kj):
                            for krl in range(2):
                                kr = 2 * kj + krl
                                inw = [rs[qrl] <= kr < rs[qrl] + 8 for qrl in range(2)]
                                ps = slice(krl * 64, (krl + 1) * 64)
                                if inw[0] and inw[1]:
                                    b0 = BT[ps, h, kr - rows[0] + 7, :]
                                    bpair = AP(b0.tensor, b0.offset, [list(b0.ap[0]), [-64, 2], [1, 64]])
                                    tt("dve", tb[ps, 0:128].rearrange("p (a b) -> p a b", a=2), bs_[ps, 0:128].rearrange("p (a b) -> p a b", a=2),
                                       bpair, ALU.add, [bsn, "BT"], [tbn])
                                elif not inw[0] and not inw[1]:
                                    n0 = negt[ps, :]
                                    npair = AP(n0.tensor, n0.offset, [list(n0.ap[0]), [0, 2], [1, 64]])
                                    tt("dve", tb[ps, 0:128].rearrange("p (a b) -> p a b", a=2), bs_[ps, 0:128].rearrange("p (a b) -> p a b", a=2),
                                       npair, ALU.add, [bsn, "negt"], [tbn])
                                else:
                                    for qrl in range(2):
                                        qr = rows[qrl]
                                        osl = tb[ps, qrl * 64:(qrl + 1) * 64]
                                        isl = bs_[ps, qrl * 64:(qrl + 1) * 64]
                                        if inw[qrl]:
                                            tt("dve", osl, isl, BT[ps, h, kr - qr + 7, :], ALU.add, [bsn, "BT"], [tbn])
                                        else:
                                            tt("dve", osl, isl, negt[ps, :], ALU.add, [bsn, "negt"], [tbn])
                        specs.append((h, nk[hl * 64:(hl + 1) * 64, j, ti * 128:(ti + 1) * 128], "nk", vpad[:, ti, h, :], "vpad", biasf))
                    for a in range(2):
                        specs.append((h, kcT[hl * 64:(hl + 1) * 64, j, a * 128:(a + 1) * 128], "kcT", vcpad[:, a, h, :], "vcpad", None))
                attend(j, t0, 128, specs, slice(t0, t0 + 128))

    def layer(l):
        curl[0] = l
        if l == 0:
            emit_mod(0)
        norm_to_h(xT, "xT", 0, 0)
        allmix = all(m in mixers for m in ("ret", "s5", "gla", "na"))
        if not allmix:
            S.barrier()
            for n in range(6):
                S.op("pool", lambda e, n=n: e.memset(brT[:, n, :], 0.0), [], [("brT", n)])
        if "ret" in mixers:
            mixer_ret(l)
            S.barrier()
        if "s5" in mixers:
            mixer_s5(l)
            S.barrier()
        if "gla" in mixers:
            mixer_gla(l)
            S.barrier()
        if "na" in mixers:
            mixer_na(l)
        else:
            for n in (6, 7):
                S.op("pool", lambda e, n=n: e.memset(brT[:, n, :], 0.0), [], [("brT", n)])
        S.barrier()
        sT = carve(0, 6144, BF16).rearrange("p (k t) -> p k t", k=8)
        gts = [sqb[0], sqb[1]]
        mg_src = lambda n, half: w_merge[l, :, n * 1024 + half * 512: n * 1024 + half * 512 + 512]
        br_src = lambda n, half: w_branch[l, n, :, half * 512:(half + 1) * 512]
        for n in range(4):
            for half in range(2):
                wm, wmn = load_w(mg_src(n, half), 8, 512, key="mg%d_%d_%d" % (l, n, half))
                wbv, wbn = load_w(br_src(n, half), 2, 512, key="br%d_%d_%d" % (l, n, half))
                nxt = n * 2 + half + 1
                if nxt < 8:
                    prefetch("mg%d_%d_%d" % (l, nxt // 2, nxt % 2), mg_src(nxt // 2, nxt % 2), 8, 512)
                    prefetch("br%d_%d_%d" % (l, nxt // 2, nxt % 2), br_src(nxt // 2, nxt % 2), 2, 512)
                else:
                    prefetch("out%d" % l, w_out[l, :, :], 8, 1024)
                for fc in range(4):
                    fo = half * 4 + fc
                    for g in range(3):
                        gi = (fc + g) % 2
                        gt = gts[gi]
                        proj_fm(wm, wmn, 8, fc * 128, hT, "hT", g,
                                lambda b, bn, gt=gt, gi=gi, fo=fo: act(gt[:], b[:], AF.Sigmoid, [bn, "bmT"], ["sqb%d" % gi],
                                                                     bias=bmT[:, l * 32 + n * 8 + fo: l * 32 + n * 8 + fo + 1]))
                        brv = brT[:, 2 * n:2 * n + 2, :]

                        def ev_up(b, bn, gt=gt, gi=gi, fo=fo, g=g, n=n):
                            sl = sT[:, fo, g * 512:(g + 1) * 512]
                            if n == 0:
                                tt("dve", sl, b[:], gt[:], ALU.mult, [bn, "sqb%d" % gi], [("sT", (fo, g))])
                            else:
                                tt("dve", tmpf[1][:], b[:], gt[:], ALU.mult, [bn, "sqb%d" % gi], ["tmpf1"])
                                tt("pool", sl, sl, tmpf[1][:], ALU.add, ["tmpf1", ("sT", (fo, g))], [("sT", (fo, g))])
                        proj_fm(wbv, wbn, 2, fc * 128, brv, ("brT", 2 * n), g, ev_up)
        mT = carve(6144, 6144, BF16).rearrange("p (k t) -> p k t", k=8)
        wv, wn = load_w(w_out[l, :, :], 8, 1024, key="out%d" % l)
        for fo in range(8):
            for g in range(3):
                proj_fm(wv, wn, 8, fo * 128, sT, "sT", g,
                        lambda b, bn, fo=fo, g=g: act(mT[:, fo, g * 512:(g + 1) * 512], b[:], AF.Copy, [bn], ["mT"]))
        S.barrier()
        if l + 1 < DEPTH and "s5" not in mixers:
            emit_mod(l + 1)
        align_w()
        prefetch("m1_%d_0" % l, w_mlp1[l, :, 0:512], 8, 512)
        prefetch("m2_%d_0" % l, w_mlp2[l, 0:512, :], 4, 1024)
        resid_update(mT, "mT", 1)
        norm_to_h(xT, "xT", 2, 24)
        S.barrier()
        for hg in range(8):
            w1, w1n = load_w(w_mlp1[l, :, hg * 512:(hg + 1) * 512], 8, 512, key="m1_%d_%d" % (l, hg))
            w2, w2n = load_w(w_mlp2[l, hg * 512:(hg + 1) * 512, :], 4, 1024, key="m2_%d_%d" % (l, hg))
            if hg + 1 < 8:
                prefetch("m1_%d_%d" % (l, hg + 1), w_mlp1[l, :, (hg + 1) * 512:(hg + 2) * 512], 8, 512)
                prefetch("m2_%d_%d" % (l, hg + 1), w_mlp2[l, (hg + 1) * 512:(hg + 2) * 512, :], 4, 1024)
            elif l + 1 < DEPTH:
                prefetch("ret%d" % (l + 1), w_in[l + 1, :, 0:1024], 8, 1024)
            fb = brT[:, (hg % 2) * 4:(hg % 2) * 4 + 4, :]
            fbn = "fb%d" % (hg % 2)
            for fc in range(4):
                for g in range(3):
                    def ev_f(b, bn, fc=fc, g=g, fb=fb, fbn=fbn):
                        act(tmpf[0][:], b[:], AF.Relu, [bn], ["tmpf0"])
                        tt("dve", fb[:, fc, g * 512:(g + 1) * 512], tmpf[0][:], tmpf[0][:], ALU.mult, ["tmpf0"], [(fbn, (fc, g))])
                    proj_fm(w1, w1n, 8, fc * 128, hT, "hT", g, ev_f)
            for fo in range(8):
                for g in range(3):
                    def ev_o(b, bn, fo=fo, g=g, hg=hg):
                        sl = accT[:, fo, g * 512:(g + 1) * 512]
                        if hg == 0:
                            act(sl, b[:], AF.Copy, [bn], [("accT", (fo, g))])
                        else:
                            tt("dve", sl, b[:], sl, ALU.add, [bn, ("accT", (fo, g))], [("accT", (fo, g))])
                    proj_fm(w2, w2n, 4, fo * 128, fb, fbn, g, ev_o)
        resid_update(accT, "accT", 3)
        S.barrier()

    for l in range(DEPTH):
        layer(l)

    for ti in range(NTILE):
        sv, sn = stage(ti % 8)
        for half in range(2):
            b, bn = nb()
            for kk in range(4):
                k = half * 4 + kk
                tr(b[:, kk * 128:(kk + 1) * 128], xT[:, k, ti * 128:(ti + 1) * 128], identF[:], ["xT", "identF"], [bn])
            act(sv[:, half * 512:(half + 1) * 512], b[:], AF.Copy, [bn], [sn])
        dst = yp[ti // 2, (ti % 2) * 128:(ti % 2 + 1) * 128, :] if ti < 4 else ys[(ti - 4) * 128:(ti - 3) * 128, :]
        dma("sp", dst, sv, [sn], ["y_d"])
    S.final_wait("sp")
    S.emit(nc, st)
    st.close()
    return nc, consts


def core_inputs(inp, c, consts):
    b = c % 4
    f = lambda a: np.ascontiguousarray(a, dtype=np.float32)
    m = {
        "xp": f(inp["x_prompt"][2 * c:2 * c + 2]), "xs": f(inp["x_sample"][b]),
        "cv": f(np.concatenate([inp["c_ctx"].reshape(8, 128), inp["c"][b].reshape(8, 128)], 0)),
        "ck": f(inp["cache_na_k"][b].reshape(4, 256, 256)), "cvv": f(inp["cache_na_v"][b].reshape(4, 256, 256)),
        "sret0": f(inp["state_ret"][b]), "ss50": f(inp["state_s5"][b]), "sgla0": f(inp["state_gla"][b]),
        "w_ada": f(inp["w_ada"]), "b_ada": f(inp["b_ada"].reshape(192, 128)), "g_norm": f(inp["g_norm"].reshape(128, 128)),
        "w_in": f(inp["w_in"]), "ret_ld": f(inp["ret_log_decay"].reshape(32, 1)), "ret_gn": f(inp["ret_gn"].reshape(8, 128)),
        "s5_lre2": f(inp["s5_lambda_re"].reshape(64, 128)), "s5_lim2": f(inp["s5_lambda_im"].reshape(64, 128)),
        "s5_ldt2": f(np.broadcast_to(inp["s5_log_dt"][..., None], (4, 2, 16, 64)).reshape(64, 128)),
        "s5_bre": f(inp["s5_b_re"]), "s5_bim": f(inp["s5_b_im"]), "s5_cre": f(inp["s5_c_re"]), "s5_cim": f(inp["s5_c_im"]),
        "s5_d": f(inp["s5_d"].reshape(8, 128)), "s5_wglu": f(inp["s5_w_glu"]), "s5_bglu": f(inp["s5_b_glu"].reshape(16, 128)),
        "gla_wg": f(inp["gla_w_gate"]), "gla_bg": f(inp["gla_b_gate"]), "gla_gn": f(inp["gla_gn"].reshape(8, 128)),
        "na_rpb": f(inp["na_rpb"]), "w_branch": f(inp["w_branch"]), "w_merge": f(inp["w_merge"]),
        "b_merge": f(inp["b_merge"].reshape(128, 128)), "w_out": f(inp["w_out"]), "w_mlp1": f(inp["w_mlp1"]), "w_mlp2": f(inp["w_mlp2"]),
    }
    for k, v in consts.items():
        m["c_" + k] = v
    return m


_CACHE = {}


def kernel(**inp):
    inp = {k: np.asarray(v) for k, v in inp.items()}
    if "nc" not in _CACHE:
        _CACHE["nc"] = build(DEPTH=4)
    nc, consts = _CACHE["nc"]
    in_maps = [core_inputs(inp, c, consts) for c in range(8)]
    res = run_bass_kernel_spmd(nc, in_maps, core_ids=list(range(8))).results
    yp = np.concatenate([r["yp"] for r in res], 0).astype(np.float32)
    ys = np.stack([res[b]["ys"] for b in range(4)], 0).astype(np.float32)
    nk = np.concatenate([r["onk"] for r in res], 0).reshape(16, 4, 256, 4, 64).astype(np.float32)
    nv = np.concatenate([r["onv"] for r in res], 0).reshape(16, 4, 256, 4, 64).astype(np.float32)
    sret = np.concatenate([r["osret"] for r in res], 0).astype(np.float32)
    ss5 = np.concatenate([r["oss5"] for r in res], 0).astype(np.float32)
    sgla = np.concatenate([r["osgla"] for r in res], 0).astype(np.float32)
    return (yp, ys, nk, nv, sret, ss5, sgla)
```

```python
from concourse.bass_utils import run_bass_kernel_spmd
import numpy as np
import concourse.bass as bass
import concourse.mybir as mybir
from concourse.ap import AP

F32 = mybir.dt.float32
BF16 = mybir.dt.bfloat16
ALU = mybir.AluOpType
AF = mybir.ActivationFunctionType

COMPUTE = ["pe", "act", "dve", "pool", "sp"]
NDMA = 24
SEM_ROT = 4
SAME_GAP = 10 ** 9


class Sched:
    def __init__(self, same_engine_sync=True):
        self.engs = COMPUTE + ["d%d" % i for i in range(NDMA)]
        self.ops = {e: [] for e in COMPUTE}
        self.count = {e: 0 for e in self.engs}
        self.seen = {e: {} for e in self.engs}
        self.snap = {e: [None] for e in self.engs}
        self.last_w = {}
        self.readers = {}
        self.dma_rr = 0
        self.same = same_engine_sync
        self.nwaits = 0

    def _conf_keys(self, res):
        name, sub = res
        if sub is None:
            return [k for k in self._names.get(name, ())]
        return [(name, sub), (name, None)]

    _names = None

    def _deps(self, eng, reads, writes):
        if self._names is None:
            self._names = {}
        deps = {}

        def add(e, i):
            if i > deps.get(e, 0):
                deps[e] = i

        for r in reads:
            for k in self._conf_keys(r):
                lw = self.last_w.get(k)
                if lw:
                    add(*lw)
        for w in writes:
            for k in self._conf_keys(w):
                lw = self.last_w.get(k)
                if lw:
                    add(*lw)
                for e, i in self.readers.get(k, {}).items():
                    add(e, i)
        return deps

    def _register(self, who, reads, writes):
        e, i = who
        for r in reads:
            self._names.setdefault(r[0], set()).add(r)
            self.readers.setdefault(r, {})[e] = i
        for w in writes:
            self._names.setdefault(w[0], set()).add(w)
            self.last_w[w] = (e, i)
            self.readers[w] = {}
            if w[1] is None:
                for k in self._names[w[0]]:
                    if k != w:
                        self.last_w.pop(k, None)
                        self.readers.pop(k, None)
                        self.last_w[k] = (e, i)
                        self.readers[k] = {}

    def _waits(self, eng, deps):
        waits = []
        seen = self.seen[eng]
        for e2, i2 in sorted(deps.items()):
            if e2 == eng and (not self.same or eng in ("sp", "pe")):
                continue
            if e2 == eng and self.count[eng] - i2 >= SAME_GAP:
                continue
            if seen.get(e2, 0) >= i2:
                continue
            waits.append((e2, i2))
        for e2, i2 in waits:
            seen[e2] = max(seen.get(e2, 0), i2)
            sn = self.snap[e2][i2] if i2 < len(self.snap[e2]) else None
            if sn:
                for k, v in sn.items():
                    if k != eng and seen.get(k, 0) < v:
                        seen[k] = v
        self.nwaits += len(waits)
        return waits

    @staticmethod
    def _norm(rs):
        out = []
        for r in rs:
            if isinstance(r, tuple):
                out.append(r)
            else:
                out.append((r, None))
        return out

    def op(self, eng, fn, reads=(), writes=()):
        reads = self._norm(reads)
        writes = self._norm(writes)
        deps = self._deps(eng, reads, writes)
        waits = self._waits(eng, deps)
        self.count[eng] += 1
        idx = self.count[eng]
        self.ops[eng].append(("op", fn, waits, idx))
        self.snap[eng].append(dict(self.seen[eng]))
        self._register((eng, idx), reads, writes)
        return idx

    def dma(self, queue, fn, reads=(), writes=()):
        reads = self._norm(reads)
        writes = self._norm(writes)
        lo, hi = {"pool": (0, 8), "sp": (8, 20), "act": (20, 24)}[queue]
        if not hasattr(self, "_rr"):
            self._rr = {}
        k = self._rr.get(queue, 0)
        self._rr[queue] = k + 1
        d = "d%d" % (lo + k % (hi - lo))
        deps = self._deps(queue, reads, writes)
        if self.count[d] > 0:
            deps[d] = max(deps.get(d, 0), self.count[d])
        waits = self._waits(queue, deps)
        self.count[d] += 1
        idx = self.count[d]
        self.ops[queue].append(("dma", fn, waits, (d, idx)))
        self.snap[d].append(dict(self.seen[queue]))
        self._register((d, idx), reads, writes)
        return (d, idx)

    def barrier(self):
        snapc = {e: c for e, c in self.count.items() if c > 0}
        for eng in COMPUTE:
            deps = {e: c for e, c in snapc.items() if e != eng}
            waits = self._waits(eng, deps)
            if waits:
                self.ops[eng].append(("wait", None, waits, None))

    def final_wait(self, eng="sp"):
        deps = {e: c for e, c in self.count.items() if c > 0 and e != eng}
        waits = self._waits(eng, deps)
        self.ops[eng].append(("wait", None, waits, None))

    def emit(self, nc, stack):
        sems = {}
        for e in COMPUTE:
            sems[e] = [stack.enter_context(nc.semaphore("s_%s%d" % (e, r))) for r in range(SEM_ROT)]
        for i in range(NDMA):
            sems["d%d" % i] = [stack.enter_context(nc.semaphore("s_d%d" % i))]

        def do_wait(engobj, e2, i2):
            if e2[1:].isdigit():
                engobj.wait_ge(sems[e2][0], 16 * i2)
            else:
                r = (i2 - 1) % SEM_ROT
                engobj.wait_ge(sems[e2][r], (i2 - 1) // SEM_ROT + 1)

        def run(engname, engobj):
            for kind, fn, waits, info in self.ops[engname]:
                for e2, i2 in waits:
                    do_wait(engobj, e2, i2)
                if kind == "op":
                    ins = fn(engobj)
                    ins.then_inc(sems[engname][(info - 1) % SEM_ROT], 1)
                elif kind == "dma":
                    ins = fn(engobj)
                    ins.then_inc(sems[info[0]][0], 16)

        block = stack.enter_context(nc.Block())

        @block.tensor
        def _(e):
            run("pe", e)

        @block.scalar
        def _(e):
            run("act", e)

        @block.vector
        def _(e):
            run("dve", e)

        @block.gpsimd
        def _(e):
            run("pool", e)

        @block.sync
        def _(e):
            run("sp", e)

import math
import numpy as np
from contextlib import ExitStack

D = 1024
T = 1536
NTILE = 12
EPS = 1e-6
NEG = -30000.0


def host_consts():
    c = {}
    c["identF"] = np.eye(128, dtype=np.float32)
    jj = np.arange(128)[:, None]
    ii = np.arange(128)[None, :]
    c["maskF"] = (jj <= ii).astype(np.float32)
    c["maskB"] = (jj >= ii).astype(np.float32)
    c["mavg"] = np.kron(np.eye(2, dtype=np.float32), np.full((64, 64), 1.0 / 64, np.float32))
    pos = np.zeros((128, 2, 128), np.float32)
    pos[:, 0, :] = np.arange(1, 129)[None, :]
    pos[:, 1, :] = (128 - np.arange(128))[None, :]
    c["posfb"] = pos
    gb = np.zeros((128, 128), np.float32)
    for idx in range(31):
        gb[idx, idx + 48] = 1.0
    c["gbig"] = gb
    c["dupI"] = np.concatenate([np.eye(64, dtype=np.float32)] * 2, axis=1)
    mn = np.full((128, 64), NEG, np.float32)
    for qc in range(64):
        cs = min(max(qc - 8, 0), 48)
        mn[cs:cs + 16, qc] = 0.0
        mn[64 + cs:64 + cs + 16, qc] = 0.0
    c["mneg"] = mn
    cm = np.zeros((128, 4, 128), np.float32)
    for glv in range(2):
        for q in range(4):
            gloc = 2 * q + glv
            cm[glv * 64:(glv + 1) * 64, q, gloc * 16:(gloc + 1) * 16] = 1.0
    c["cmask"] = cm
    hm = np.zeros((128, 4), np.float32)
    for h in range(4):
        hm[h * 32:(h + 1) * 32, h] = 1.0
    c["hmask"] = hm
    t = np.arange(1024)
    row = (t // 64).astype(np.float32)
    col = (t % 64).astype(np.float32)
    inv = (10000.0 ** (-np.arange(16, dtype=np.float32) / 16)).astype(np.float32)
    C = np.zeros((128, 1024), np.float32)
    Sg = np.zeros((128, 1024), np.float32)
    for p in range(128):
        d = p % 64
        half = d // 32
        j = d % 16
        blk = (d % 32) // 16
        posv = row if half == 0 else col
        ang = (posv * inv[j]).astype(np.float32)
        C[p] = np.cos(ang)
        Sg[p] = np.sin(ang) * (-1.0 if blk == 0 else 1.0)
    c["ropeC"] = C
    c["ropeS"] = Sg
    return c


def build(DEPTH=4, mixers=("ret", "s5", "gla", "na"), dbg=False):
    nc = bass.Bass("TRN2", target_bir_lowering=False)
    din = lambda name, shape: nc.dram_tensor(name, list(shape), F32, kind="ExternalInput").ap()
    dout = lambda name, shape: nc.dram_tensor(name, list(shape), F32, kind="ExternalOutput").ap()
    xp = din("xp", [2, 256, 1024]); xs = din("xs", [1024, 1024]); cv = din("cv", [16, 128])
    ck = din("ck", [4, 256, 256]); cvv = din("cvv", [4, 256, 256])
    sret0 = din("sret0", [4, 2, 4, 64, 64]); ss50 = din("ss50", [4, 2, 16, 64, 2]); sgla0 = din("sgla0", [4, 2, 4, 32, 64])
    w_ada = din("w_ada", [4, 1024, 6144]); b_ada = din("b_ada", [192, 128]); g_norm = din("g_norm", [128, 128])
    w_in = din("w_in", [4, 1024, 2848]); ret_ld = din("ret_ld", [32, 1]); ret_gn = din("ret_gn", [8, 128])
    s5_lre2 = din("s5_lre2", [64, 128]); s5_lim2 = din("s5_lim2", [64, 128]); s5_ldt2 = din("s5_ldt2", [64, 128])
    s5_bre = din("s5_bre", [4, 2, 16, 64, 16]); s5_bim = din("s5_bim", [4, 2, 16, 64, 16])
    s5_cre = din("s5_cre", [4, 2, 16, 16, 64]); s5_cim = din("s5_cim", [4, 2, 16, 16, 64])
    s5_d = din("s5_d", [8, 128]); s5_wglu = din("s5_wglu", [4, 256, 512]); s5_bglu = din("s5_bglu", [16, 128])
    gla_wg = din("gla_wg", [4, 2, 16, 128]); gla_bg = din("gla_bg", [4, 2, 128]); gla_gn = din("gla_gn", [8, 128])
    na_rpb = din("na_rpb", [4, 4, 15, 31])
    w_branch = din("w_branch", [4, 4, 256, 1024]); w_merge = din("w_merge", [4, 1024, 4096]); b_merge = din("b_merge", [128, 128])
    w_out = din("w_out", [4, 1024, 1024]); w_mlp1 = din("w_mlp1", [4, 1024, 4096]); w_mlp2 = din("w_mlp2", [4, 4096, 1024])
    consts = host_consts()
    cd = {k: din("c_" + k, v.shape) for k, v in consts.items()}
    yp = dout("yp", [2, 256, 1024]); ys = dout("ys", [1024, 1024])
    onk = dout("onk", [2, 4, 256, 256]); onv = dout("onv", [2, 4, 256, 256])
    osret = dout("osret", [2, 4, 2, 4, 64, 64]); oss5 = dout("oss5", [2, 4, 2, 16, 64, 2]); osgla = dout("osgla", [2, 4, 2, 4, 32, 64])
    bt_scr = nc.dram_tensor("bt_scr", [4, 15, 64, 64], F32, kind="Internal").ap()

    S = Sched()
    st = ExitStack()
    sbt = lambda n, s, d=F32: st.enter_context(nc.sbuf_tensor(n, list(s), d))
    xT = sbt("xT", [128, 8, T])
    hT = sbt("hT", [128, 8, T], BF16)
    accT = sbt("accT", [128, 8, T])
    brT = sbt("brT", [128, 8, T], BF16)
    wbuf = [sbt("wb%d" % i, [128, 8192], BF16) for i in range(2)]
    rstd = sbt("rstd", [128, T])
    tmpf = [sbt("tmpf%d" % i, [128, 512]) for i in range(2)]
    sqb = [sbt("sqb%d" % i, [128, 512], BF16) for i in range(2)]
    identF = sbt("identF", [128, 128]); identB = sbt("identB", [128, 128], BF16)
    onesB = sbt("onesB", [128, 128], BF16)
    maskF = sbt("maskF", [128, 128], BF16); maskB = sbt("maskB", [128, 128], BF16)
    mavgB = sbt("mavgB", [128, 128], BF16)
    gnT = sbt("gnT", [128, 128]); badaT = sbt("badaT", [128, 192]); bmT = sbt("bmT", [128, 128])
    retgnT = sbt("retgnT", [128, 8]); glagnT = sbt("glagnT", [128, 8]); s5dT = sbt("s5dT", [128, 8]); bgluT = sbt("bgluT", [128, 16])
    cT = sbt("cT", [128, 16]); scT = sbt("scT", [128, 8, 2], BF16)
    modT2 = sbt("modT", [128, 2, 48, 2]); dsc2 = sbt("dsc", [128, 2, 4, 8, 2])
    curl = [0]
    rowst = sbt("rowst", [128, 128])
    cln8 = sbt("cln8", [128, 1])
    dupI = sbt("dupI", [64, 128], BF16); gbig = sbt("gbig", [128, 128]); mneg = sbt("mneg", [128, 64]); Rrp = sbt("Rrp", [32, 60])
    pbank = [st.enter_context(nc.psum_tensor("pb%d" % i, [128, 512], F32)) for i in range(8)]
    bctr = [0]

    def nb():
        i = bctr[0] % 6
        bctr[0] += 1
        return pbank[i], "pb%d" % i

    wslot = [0]
    pre = {}

    def alloc_w(nel):
        if nel <= 4096:
            s = wslot[0] % 4
            wslot[0] += 1
            return wbuf[s // 2], (s % 2) * 4096, ("wb%d" % (s // 2), s % 2)
        if wslot[0] % 2:
            wslot[0] += 1
        s = wslot[0] % 4
        wslot[0] += 2
        return wbuf[s // 2], 0, "wb%d" % (s // 2)

    def load_w(src2d, kc, ncols, key=None, queue="pool"):
        if key is not None and key in pre:
            return pre.pop(key)
        buf, off, name = alloc_w(kc * ncols)
        view = buf[:, off:off + kc * ncols].rearrange("p (k n) -> p k n", k=kc)
        S.dma(queue, lambda e: e.dma_start(out=view, in_=src2d.rearrange("(k p) n -> p k n", p=128)), [], [name])
        return view, name

    def align_w():
        if wslot[0] % 2:
            wslot[0] += 1

    def prefetch(key, src2d, kc, ncols):
        if key not in pre:
            pre[key] = load_w(src2d, kc, ncols)

    def dma(q, out, in_, r, w):
        S.dma(q, lambda e: e.dma_start(out=out, in_=in_), r, w)

    def mm(out, lhsT, rhs, start, stop, r, w):
        S.op("pe", lambda e: e.matmul(out, lhsT, rhs, start=start, stop=stop), r, w)

    def tr(out, in_, ident, r, w):
        S.op("pe", lambda e: e.transpose(out, in_, ident), r, w)

    def act(out, in_, func, r, w, bias=None, scale=None):
        kw = {}
        if bias is not None:
            kw["bias"] = bias
        if scale is not None:
            kw["scale"] = scale
        S.op("act", lambda e: e.activation(out=out, in_=in_, func=func, **kw), r, w)

    def tt(eng, out, in0, in1, op, r, w):
        S.op(eng, lambda e: e.tensor_tensor(out=out, in0=in0, in1=in1, op=op), r, w)

    def ts(eng, out, in0, s1, s2, op0, op1, r, w):
        if s2 is None:
            S.op(eng, lambda e: e.tensor_scalar(out=out, in0=in0, scalar1=s1, scalar2=None, op0=op0), r, w)
        else:
            S.op(eng, lambda e: e.tensor_scalar(out=out, in0=in0, scalar1=s1, scalar2=s2, op0=op0, op1=op1), r, w)

    def stt(eng, out, in0, scalar, in1, op0, op1, r, w):
        S.op(eng, lambda e: e.scalar_tensor_tensor(out=out, in0=in0, scalar=scalar, in1=in1, op0=op0, op1=op1), r, w)

    def bcast_last(ap2, n):
        a = ap2.ap
        return AP(ap2.tensor, ap2.offset, [list(a[0]), list(a[1]), [0, n]])

    def bcast_mid(ap2, n):
        a = ap2.ap
        return AP(ap2.tensor, ap2.offset, [list(a[0]), [0, n], list(a[1])])

    dma("sp", identF[:], cd["identF"][:, :], [], ["identF"])
    dma("pool", identB[:], cd["identF"][:, :], [], ["identB"])
    dma("pool", maskF[:], cd["maskF"][:, :], [], ["maskF"])
    dma("pool", maskB[:], cd["maskB"][:, :], [], ["maskB"])
    dma("pool", mavgB[:], cd["mavg"][:, :], [], ["mavgB"])
    S.op("dve", lambda e: e.memset(onesB[:], 1.0), [], ["onesB"])
    S.op("dve", lambda e: e.memset(cln8[:], math.log(0.125)), [], ["cln8"])
    dma("sp", gbig[:], cd["gbig"][:, :], [], ["gbig"])
    dma("pool", dupI[:], cd["dupI"][:, :], [], ["dupI"])
    dma("sp", mneg[:], cd["mneg"][:, :], [], ["mneg"])

    def rows_to_cols(src, R, dst, dname):
        dma("sp", rowst[0:R, :], src, [], ["rowst"])
        b, bn = nb()
        tr(b[:, 0:R], rowst[0:R, :], identF[0:R, 0:R], ["rowst", "identF"], [bn])
        act(dst, b[:, 0:R], AF.Copy, [bn], [dname])

    rows_to_cols(g_norm[:, :], 128, gnT[:], "gnT")
    rows_to_cols(b_ada[0:128, :], 128, badaT[:, 0:128], "badaT")
    rows_to_cols(b_ada[128:192, :], 64, badaT[:, 128:192], "badaT")
    rows_to_cols(b_merge[:, :], 128, bmT[:], "bmT")
    rows_to_cols(ret_gn[:, :], 8, retgnT[:], "retgnT")
    rows_to_cols(gla_gn[:, :], 8, glagnT[:], "glagnT")
    rows_to_cols(s5_d[:, :], 8, s5dT[:], "s5dT")
    rows_to_cols(s5_bglu[:, :], 16, bgluT[:], "bgluT")
    rows_to_cols(cv[:, :], 16, cT[:], "cT")
    act(scT[:, :, 0], cT[:, 0:8], AF.Silu, ["cT"], ["scT"])
    act(scT[:, :, 1], cT[:, 8:16], AF.Silu, ["cT"], ["scT"])

    accS = accT[:].rearrange("p k t -> p (k t)")

    def stage(j):
        return accS[:, j * 1024:(j + 1) * 1024], ("accT", "st%d" % (j,))

    for ti in range(NTILE):
        sv, sn = stage(ti % 8)
        src = xp[ti // 2, (ti % 2) * 128:(ti % 2 + 1) * 128, :] if ti < 4 else xs[(ti - 4) * 128:(ti - 3) * 128, :]
        dma("sp", sv, src, [], [sn])
        for half in range(2):
            b, bn = nb()
            for kk in range(4):
                k = half * 4 + kk
                tr(b[:, kk * 128:(kk + 1) * 128], sv[:, k * 128:(k + 1) * 128], identF[:], [sn, "identF"], [bn])
            act(xT[:, half * 4:(half + 1) * 4, ti * 128:(ti + 1) * 128], b[:].rearrange("p (a b) -> p a b", a=4), AF.Copy,
                [bn], [("xT", ti)])
    S.barrier()

    def gpath(g):
        return 0 if g == 0 else 1

    def mod_src(l, cb):
        return w_ada[l, :, cb * 1024:(cb + 1) * 1024]

    def mod_mm(l, cb, bank=None):
        wv, wn = load_w(mod_src(l, cb), 8, 1024, key="ada%d_%d" % (l, cb))
        b, bn = bank if bank is not None else nb()
        for qq in range(8):
            for k in range(8):
                mm(b[:, qq * 2:(qq + 1) * 2], wv[:, k, qq * 128:(qq + 1) * 128], scT[:, k, :], k == 0, k == 7, [wn, "scT"], [bn])
        return b, bn

    def mod_evac(l, cb, b, bn):
        modT = modT2[:, l % 2]
        mN = "modT%d" % (l % 2)
        tt("dve", modT[:, cb * 8:(cb + 1) * 8, :], b[:, 0:16].rearrange("p (a b) -> p a b", a=8),
           bcast_last(badaT[:, l * 48 + cb * 8: l * 48 + cb * 8 + 8], 2), ALU.add, [bn, "badaT"], [mN])

    def mod_finish(l):
        modT = modT2[:, l % 2]
        dsc = dsc2[:, l % 2]
        mN = "modT%d" % (l % 2)
        dN = "dsc%d" % (l % 2)
        gn = lambda n: bcast_last(gnT[:, (l * 4 + n) * 8:(l * 4 + n) * 8 + 8], 2)
        stt("dve", dsc[:, 0], modT[:, 8:16, :], 1.0, gn(0), ALU.add, ALU.mult, [mN, "gnT"], [dN])
        tt("dve", dsc[:, 1], modT[:, 16:24, :], gn(1), ALU.mult, [mN, "gnT"], [dN])
        stt("dve", dsc[:, 2], modT[:, 32:40, :], 1.0, gn(2), ALU.add, ALU.mult, [mN, "gnT"], [dN])
        tt("dve", dsc[:, 3], modT[:, 40:48, :], gn(3), ALU.mult, [mN, "gnT"], [dN])

    def emit_mod(l):
        for cb in range(6):
            b, bn = mod_mm(l, cb)
            mod_evac(l, cb, b, bn)
        mod_finish(l)

    def norm_stats(src, sname):
        for g in range(3):
            b, bn = nb()
            for k in range(8):
                q = sqb[k % 2]
                act(q[:], src[:, k, g * 512:(g + 1) * 512], AF.Square, [sname], ["sqb%d" % (k % 2)])
                mm(b[:], onesB[:], q[:], k == 0, k == 7, ["onesB", "sqb%d" % (k % 2)], [bn])
            act(tmpf[0][:], b[:], AF.Ln, [bn], ["tmpf0"], bias=EPS, scale=1.0 / 1024)
            act(rstd[:, g * 512:(g + 1) * 512], tmpf[0][:], AF.Exp, ["tmpf0"], [("rstd", g)], scale=-0.5)

    def norm_to_h(src, sname, ai, bq):
        norm_stats(src, sname)
        modT = modT2[:, curl[0] % 2]
        dsc = dsc2[:, curl[0] % 2]
        mN = "modT%d" % (curl[0] % 2)
        dN = "dsc%d" % (curl[0] % 2)
        for g in range(3):
            p = gpath(g)
            for k in range(8):
                tf = tmpf[k % 2]
                tt("dve", tf[:], src[:, k, g * 512:(g + 1) * 512], rstd[:, g * 512:(g + 1) * 512], ALU.mult,
                   [sname, ("rstd", g)], ["tmpf%d" % (k % 2)])
                act(hT[:, k, g * 512:(g + 1) * 512], tf[:], AF.Identity, ["tmpf%d" % (k % 2), dN, mN], [("hT", g)],
                    bias=modT[:, bq + k, p:p + 1], scale=dsc[:, ai, k, p:p + 1])

    def resid_update(src, sname, ai, have_stats=False):
        if not have_stats:
            norm_stats(src, sname)
        modT = modT2[:, curl[0] % 2]
        dsc = dsc2[:, curl[0] % 2]
        mN = "modT%d" % (curl[0] % 2)
        dN = "dsc%d" % (curl[0] % 2)
        for g in range(3):
            p = gpath(g)
            for k in range(8):
                tf = tmpf[k % 2]
                tt("dve", tf[:], src[:, k, g * 512:(g + 1) * 512], rstd[:, g * 512:(g + 1) * 512], ALU.mult,
                   [sname, ("rstd", g)], ["tmpf%d" % (k % 2)])
                stt("dve", xT[:, k, g * 512:(g + 1) * 512], tf[:], dsc[:, ai, k, p:p + 1], xT[:, k, g * 512:(g + 1) * 512],
                    ALU.mult, ALU.add, ["tmpf%d" % (k % 2), dN, "xT"], ["xT"])

    def proj_fm(wv, wn, kc, c0, src, sname, g, evac):
        b, bn = nb()
        for k in range(kc):
            mm(b[:], wv[:, k, c0:c0 + 128], src[:, k, g * 512:(g + 1) * 512], k == 0, k == kc - 1, [wn, sname], [bn])
        evac(b, bn)

    def proj_tm(wv, wn, kc, c0, ncols, src, sname, ti, evac):
        b, bn = nb()
        for k in range(kc):
            mm(b[:, 0:ncols], src[:, k, ti * 128:(ti + 1) * 128], wv[:, k, c0:c0 + ncols], k == 0, k == kc - 1, [wn, sname], [bn])
        evac(b, bn)

    def carve(off, n, dtype=F32):
        a = accS[:, off:off + n]
        if dtype == BF16:
            a = a.bitcast(BF16)
        return a

    rotb = sbt("rotb", [128, 16, 128], BF16)
    rotf = sbt("rotf", [128, 8, 128])
    lgc = sbt("lgc", [128, 4, 2, 2])
    nlgc = sbt("nlgc", [128, 4, 2, 2])
    eend = sbt("eend", [128, 2, 2])
    posfb = sbt("posfb", [128, 2, 128])
    hmask = sbt("hmask", [128, 4])
    dma("sp", posfb[:], cd["posfb"][:, :, :], [], ["posfb"])
    dma("sp", hmask[:], cd["hmask"][:, :], [], ["hmask"])
    for l_ in range(4):
        for d_ in range(2):
            for h_ in range(4):
                e0 = ret_ld[(l_ * 2 + d_) * 4 + h_:(l_ * 2 + d_) * 4 + h_ + 1, 0:1]
                src = AP(e0.tensor, e0.offset, [[0, 64], [1, 1]])
                dma("sp", lgc[(h_ % 2) * 64:(h_ % 2) * 64 + 64, l_, d_, h_ // 2:h_ // 2 + 1], src, [], ["lgc"])
    ts("dve", nlgc[:].rearrange("p a b c -> p (a b c)"), lgc[:].rearrange("p a b c -> p (a b c)"), -1.0, None, ALU.mult, None, ["lgc"], ["nlgc"])

    def head_norm(oG, oGn, j, g, gcol, gate, gaten, outchunk):
        osb = oG[:, j, :]
        b1, b1n = nb()
        mm(b1[:], mavgB[:], osb, True, True, ["mavgB", oGn], [b1n])
        tt("dve", tmpf[0][:], osb, b1[:], ALU.subtract, [oGn, b1n], ["tmpf0"])
        sqv = tmpf[1][:].bitcast(BF16)[:, 0:512]
        act(sqv, tmpf[0][:], AF.Square, ["tmpf0"], ["tmpf1"])
        b2, b2n = nb()
        mm(b2[:], mavgB[:], sqv, True, True, ["mavgB", "tmpf1"], [b2n])
        act(rstd[:, 0:512], b2[:], AF.Ln, [b2n], [("rstd", 0)], bias=EPS, scale=1.0)
        act(rstd[:, 0:512], rstd[:, 0:512], AF.Exp, [("rstd", 0)], [("rstd", 0)], scale=-0.5)
        tt("dve", tmpf[0][:], tmpf[0][:], rstd[:, 0:512], ALU.mult, ["tmpf0", ("rstd", 0)], ["tmpf0"])
        stt("dve", brT[:, outchunk, g * 512:(g + 1) * 512], tmpf[0][:], gcol, gate[:, j, g * 512:(g + 1) * 512], ALU.mult, ALU.mult,
            ["tmpf0", gaten], [("brT", outchunk)])

    SEQS = [(0, 2, 0), (2, 2, 0), (4, 8, 1)]

    def mixer_ret(l):
        rq = carve(0, 1536, BF16).rearrange("p (j t) -> p j t", j=2)
        rk = carve(1536, 1536, BF16).rearrange("p (j t) -> p j t", j=2)
        rg = carve(3072, 1536, BF16).rearrange("p (j t) -> p j t", j=2)
        rvpad = carve(4608, 3072, BF16).rearrange("p (a h c) -> p a h c", a=12, h=4)
        Spad = carve(7680, 2304, BF16).rearrange("p (d c j x) -> p d c j x", d=2, c=9, j=2)
        Srun = carve(9984, 256, F32).rearrange("p (d j x) -> p d j x", d=2, j=2)
        U = carve(10240, 128, F32).rearrange("p (j x) -> p j x", j=2)
        oG = carve(10496, 512, BF16).rearrange("p (j x) -> p j x", j=2)
        Eq = carve(11520, 256, BF16).rearrange("p (d j x) -> p d j x", d=2, j=2)
        Ek = carve(11776, 256, BF16).rearrange("p (d j x) -> p d j x", d=2, j=2)
        rqs = brT[:, 2:4, :]
        rks = brT[:, 4:6, :]
        ropeC = brT[:, 6, 0:1024]
        ropeS = brT[:, 7, 0:1024]
        dma("pool", ropeC, cd["ropeC"][:, :], [], ["ropeC"])
        dma("pool", ropeS, cd["ropeS"][:, :], [], ["ropeS"])
        S.op("pool", lambda e: e.memset(rvpad, 0.0), [], ["rvpad"])
        S.op("pool", lambda e: e.memset(Spad, 0.0), [], ["Spad"])
        for d in range(2):
            for j in range(2):
                act(Eq[:, d, j, :], posfb[:, d, :], AF.Exp, ["posfb", "lgc"], ["Eq"], scale=lgc[:, l, d, j:j + 1])
                act(Ek[:, d, j, :], posfb[:, d, :], AF.Exp, ["posfb", "nlgc"], ["Ek"], scale=nlgc[:, l, d, j:j + 1], bias=cln8[:, 0:1])
        act(eend[:].rearrange("p a b -> p (a b)"), lgc[:, l].rearrange("p a b -> p (a b)"), AF.Exp, ["lgc"], ["eend"], scale=128.0)
        wv, wn = load_w(w_in[l, :, 0:1024], 8, 1024, key="ret%d" % l)
        for g in range(3):
            for j in range(2):
                proj_fm(wv, wn, 8, j * 128, hT, "hT", g, lambda b, bn, j=j, g=g: act(rq[:, j, g * 512:(g + 1) * 512], b[:], AF.Copy, [bn], ["rq"]))
                proj_fm(wv, wn, 8, 256 + j * 128, hT, "hT", g, lambda b, bn, j=j, g=g: act(rk[:, j, g * 512:(g + 1) * 512], b[:], AF.Copy, [bn], ["rk"]))
                proj_fm(wv, wn, 8, 768 + j * 128, hT, "hT", g, lambda b, bn, j=j, g=g: act(rg[:, j, g * 512:(g + 1) * 512], b[:], AF.Silu, [bn], ["rg"]))
        for ti in range(NTILE):
            def ev_v(b, bn, ti=ti):
                for h in range(4):
                    act(rvpad[:, ti, h, (h % 2) * 64:(h % 2) * 64 + 64], b[:, h * 64:(h + 1) * 64], AF.Copy, [bn], ["rvpad"])
            proj_tm(wv, wn, 8, 512, 256, hT, "hT", ti, ev_v)
        prefetch("s5%d" % l, w_in[l, :, 1024:1280], 8, 256)
        sbuf_, soff_, swn = alloc_w(4096)
        swv = sbuf_[:, soff_:soff_ + 4096].rearrange("p (k n) -> p k n", k=8)
        for blk in range(2):
            dstv = swv.rearrange("p k (m b x) -> p k m b x", b=2, x=16)[:, :, :, blk, :]
            srcv = wv[:, :, 0:512].rearrange("p k (m b x) -> p k m b x", b=2, x=16)[:, :, :, 1 - blk, :]
            S.op("act", lambda e, dstv=dstv, srcv=srcv: e.activation(out=dstv, in_=srcv, func=AF.Copy), [wn], [swn])
        for g in (1, 2):
            for j in range(2):
                proj_fm(swv, swn, 8, j * 128, hT, "hT", g, lambda b, bn, j=j, g=g: act(rqs[:, j, g * 512:(g + 1) * 512], b[:], AF.Copy, [bn], ["rqs"]))
                proj_fm(swv, swn, 8, 256 + j * 128, hT, "hT", g, lambda b, bn, j=j, g=g: act(rks[:, j, g * 512:(g + 1) * 512], b[:], AF.Copy, [bn], ["rks"]))
        for (r_, rn, s_, sn_) in ((rq, "rq", rqs, "rqs"), (rk, "rk", rks, "rks")):
            for j in range(2):
                tt("dve", r_[:, j, 512:1536], r_[:, j, 512:1536], ropeC, ALU.mult, [rn, "ropeC"], [rn])
                tt("pool", s_[:, j, 512:1536], s_[:, j, 512:1536], ropeS, ALU.mult, [sn_, "ropeS"], [sn_])
                tt("dve", r_[:, j, 512:1536], r_[:, j, 512:1536], s_[:, j, 512:1536], ALU.add, [rn, sn_], [rn])
        S.barrier()
        if not all(m in mixers for m in ("s5", "gla")):
            for n in range(2, 6):
                S.op("pool", lambda e, n=n: e.memset(brT[:, n, :], 0.0), [], [("brT", n)])
        rc = [0]

        def rb():
            i = rc[0] % 16
            rc[0] += 1
            return rotb[:, i, :], ("rotb", i)

        fc = [0]

        def rf():
            i = fc[0] % 4
            fc[0] += 1
            return rotf[:, i, :], ("rotf", i)

        for si, (t0, n, latent) in enumerate(SEQS):
            for d in range(2):
                for j in range(2):
                    for hl in range(2):
                        h = 2 * j + hl
                        if latent:
                            dma("sp", Srun[hl * 64:(hl + 1) * 64, d, j, :], sret0[l, d, h, :, :], [], ["Srun"])
                        else:
                            S.op("dve", lambda e, hl=hl, d=d, j=j: e.memset(Srun[hl * 64:(hl + 1) * 64, d, j, :], 0.0), [], ["Srun"])
                order = list(range(n)) if d == 0 else list(range(n - 1, -1, -1))
                kts = {}

                def p1_front(c, d=d, kts=kts):
                    ti = t0 + c
                    for j in range(2):
                        kf, kfn = rf()
                        tt("dve", kf, rk[:, j, ti * 128:(ti + 1) * 128], Ek[:, d, j, :], ALU.mult, ["rk", "Ek"], [kfn])
                        b, bn = nb()
                        tr(b[:, 0:128], kf, identF[:], [kfn, "identF"], [bn])
                        kt, ktn = rb()
                        act(kt, b[:, 0:128], AF.Copy, [bn], [ktn])
                        kts[(c, j)] = (kt, ktn)

                def p1_back(c, d=d, kts=kts):
                    ti = t0 + c
                    for j in range(2):
                        for hl in range(2):
                            S.op("dve", lambda e, hl=hl, d=d, j=j, c=c: e.tensor_copy(out=Spad[hl * 64:(hl + 1) * 64, d, c, j, hl * 64:(hl + 1) * 64],
                                                                                   in_=Srun[hl * 64:(hl + 1) * 64, d, j, :]), ["Srun"], ["Spad"])
                    b2, b2n = nb()
                    for j in range(2):
                        kt, ktn = kts.pop((c, j))
                        for hl in range(2):
                            h = 2 * j + hl
                            mm(b2[:, j * 128 + hl * 64:j * 128 + (hl + 1) * 64], kt, rvpad[:, ti, h, hl * 64:(hl + 1) * 64], True, True, [ktn, "rvpad"], [b2n])
                    for hl in range(2):
                        ps = slice(hl * 64, (hl + 1) * 64)
                        uv = b2[ps, hl * 64:hl * 64 + 64]
                        uview = AP(uv.tensor, uv.offset, [list(uv.ap[0]), [128, 2], [1, 64]])
                        tt("dve", Srun[ps, d], Srun[ps, d], uview, ALU.add, ["Srun", b2n], ["Srun"])
                    tt("dve", Srun[:, d], Srun[:, d], bcast_last(eend[:, d, :], 64), ALU.mult, ["Srun", "eend"], ["Srun"])

                p1_front(order[0])
                for oi, c in enumerate(order):
                    if oi + 1 < len(order):
                        p1_front(order[oi + 1])
                    p1_back(c)
                if not latent:
                    for j in range(2):
                        for hl in range(2):
                            dma("sp", osret[si, l, d, 2 * j + hl, :, :], Srun[hl * 64:(hl + 1) * 64, d, j, :], ["Srun"], ["osret_d"])
            p2u = [(c, j, d, hl) for c in range(n) for j in range(2) for d in range(2) for hl in range(2)]
            qk = {}
            ams = {}

            def p2_front(u):
                c, j, d, hl = u
                ti = t0 + c
                if hl == 0:
                    q_, qn_ = rb()
                    tt("dve", q_, rq[:, j, ti * 128:(ti + 1) * 128], Eq[:, d, j, :], ALU.mult, ["rq", "Eq"], [qn_])
                    k_, kn_ = rb()
                    tt("dve", k_, rk[:, j, ti * 128:(ti + 1) * 128], Ek[:, d, j, :], ALU.mult, ["rk", "Ek"], [kn_])
                    qk[(c, j, d)] = (q_, qn_, k_, kn_)
                q_, qn_, k_, kn_ = qk[(c, j, d)]
                b, bn = nb()
                mm(b[:, 0:128], k_[hl * 64:(hl + 1) * 64, :], q_[hl * 64:(hl + 1) * 64, :], True, True, [kn_, qn_], [bn])
                ams[u] = (b, bn)

            def p2_mid(u):
                c, j, d, hl = u
                b, bn = ams[u]
                a_, an_ = rb()
                tt("dve", a_, b[:, 0:128], (maskF if d == 0 else maskB)[:], ALU.mult, [bn, "maskF", "maskB"], [an_])
                ams[u] = (a_, an_)

            def p2_back(u):
                c, j, d, hl = u
                ti = t0 + c
                g = ti // 4
                cg = ti % 4
                ob, obn = (pbank[6], "pb6") if (c * 2 + j) % 2 == 0 else (pbank[7], "pb7")
                a_, an_ = ams.pop(u)
                q_, qn_, k_, kn_ = qk[(c, j, d)]
                h = 2 * j + hl
                idx = 2 * (2 * d + hl)
                mm(ob[:, 0:128], rvpad[:, ti, h, :], a_, idx == 0, False, ["rvpad", an_], [obn])
                mm(ob[:, 0:128], Spad[hl * 64:(hl + 1) * 64, d, c, j, :], q_[hl * 64:(hl + 1) * 64, :], False, idx + 1 == 7, ["Spad", qn_], [obn])
                if d == 1 and hl == 1:
                    act(oG[:, j, cg * 128:(cg + 1) * 128], ob[:, 0:128], AF.Copy, [obn], ["oG"])
                    if cg == 3 and j == 1:
                        for jj in range(2):
                            head_norm(oG, "oG", jj, g, retgnT[:, l * 2 + jj:l * 2 + jj + 1], rg, "rg", jj)

            for ui in range(len(p2u) + 2):
                if ui < len(p2u):
                    p2_front(p2u[ui])
                if 1 <= ui <= len(p2u):
                    p2_mid(p2u[ui - 1])
                if ui >= 2:
                    p2_back(p2u[ui - 2])

    wgf = sbt("wgf", [33, 256]); wgb = sbt("wgb", [33, 256], BF16); gee = sbt("gee", [128, 3, 4])

    def rev(ap2):
        n = ap2.shape[-1]
        a = ap2.ap
        return AP(ap2.tensor, ap2.offset + (n - 1) * a[-1][0], [list(a[0]), [-a[-1][0], n]])

    def mixer_gla(l):
        gq = carve(0, 768, BF16)
        gk = carve(768, 768, BF16)
        gg = carve(1536, 1536, BF16).rearrange("p (j t) -> p j t", j=2)
        gvpad = carve(3072, 3072, BF16).rearrange("p (a h c) -> p a h c", a=12, h=4)
        glrT = carve(6144, 768, BF16)
        Sp = carve(6912, 2304, BF16).rearrange("p (d c j x) -> p d c j x", d=2, c=9, j=2)
        Srun = carve(9216, 128, F32).rearrange("p (d x) -> p d x", d=2)
        U = carve(9344, 64, F32)
        oG = carve(9472, 512, BF16).rearrange("p (j x) -> p j x", j=2)
        S.op("pool", lambda e: e.memset(gvpad, 0.0), [], ["gvpad"])
        S.op("pool", lambda e: e.memset(Sp, 0.0), [], ["Sp"])
        S.op("dve", lambda e: e.memset(wgf[:], 0.0), [], ["wgf"])
        dma("sp", wgf[0:16, 0:128], gla_wg[l, 0, :, :], ["wgf"], ["wgf"])
        dma("sp", wgf[16:32, 128:256], gla_wg[l, 1, :, :], ["wgf"], ["wgf"])
        dma("sp", wgf[32:33, :], gla_bg[l:l + 1].rearrange("a d x -> a (d x)"), ["wgf"], ["wgf"])
        S.op("dve", lambda e: e.tensor_copy(out=wgb[:], in_=wgf[:]), ["wgf"], ["wgb"])
        S.op("dve", lambda e: e.memset(glrT[32:33, :], 1.0), [], ["glrT"])
        wv, wn = load_w(w_in[l, :, 1280:2080], 8, 800, key="gla%d" % l)
        sc_q = 32.0 ** -0.5
        for g in range(3):
            proj_fm(wv, wn, 8, 0, hT, "hT", g, lambda b, bn, g=g: act(gq[:, g * 512:(g + 1) * 512], b[:], AF.Copy, [bn], ["gq"], scale=sc_q))
            proj_fm(wv, wn, 8, 128, hT, "hT", g, lambda b, bn, g=g: act(gk[:, g * 512:(g + 1) * 512], b[:], AF.Copy, [bn], ["gk"]))
            for j in range(2):
                proj_fm(wv, wn, 8, 512 + j * 128, hT, "hT", g, lambda b, bn, j=j, g=g: act(gg[:, j, g * 512:(g + 1) * 512], b[:], AF.Silu, [bn], ["gg"]))
            b, bn = nb()
            for k in range(8):
                mm(b[0:32, :], wv[:, k, 768:800], hT[:, k, g * 512:(g + 1) * 512], k == 0, k == 7, [wn, "hT"], [bn])
            act(glrT[0:32, g * 512:(g + 1) * 512], b[0:32, :], AF.Copy, [bn], ["glrT"])
        for ti in range(NTILE):
            def ev_v(b, bn, ti=ti):
                for h in range(4):
                    act(gvpad[:, ti, h, (h % 2) * 64:(h % 2) * 64 + 64], b[:, h * 64:(h + 1) * 64], AF.Copy, [bn], ["gvpad"])
            proj_tm(wv, wn, 8, 256, 256, hT, "hT", ti, ev_v)
        prefetch("na%d" % l, w_in[l, :, 2080:2848], 8, 768)
        rc = [0]

        def rb():
            i = rc[0] % 16
            rc[0] += 1
            return rotb[:, i, :], ("rotb", i)

        fc = [0]

        def rf():
            i = fc[0] % 8
            fc[0] += 1
            return rotf[:, i, :], ("rotf", i)

        gtb = carve(10496, 1536, BF16).rearrange("p (s c x) -> p s c x", s=3, c=2)
        onesR = carve(12032, 256, BF16)
        S.op("dve", lambda e: e.memset(onesR, 1.0), [], ["onesR"])
        S.op("dve", lambda e: e.memset(onesR.rearrange("p (c x) -> p c x", c=4)[:, :, 0:1], 0.0), ["onesR"], ["onesR"])
        gcache = {}
        gslots = [None, None, None]
        gctr = [0]

        def gtab(g, d):
            if (g, d) in gcache:
                return gcache[(g, d)]
            s = gctr[0] % 3
            gctr[0] += 1
            if gslots[s] is not None:
                del gcache[gslots[s]]
            gslots[s] = (g, d)
            nm = ("gtb", s)
            b, bn = nb()
            mm(b[:], wgb[0:33, d * 128:(d + 1) * 128], glrT[0:33, g * 512:(g + 1) * 512], True, True, ["wgb", "glrT"], [bn])
            sp = tmpf[0][:]
            cs = tmpf[1][:]
            act(sp, b[:], AF.Exp, [bn], ["tmpf0"], scale=-1.0)
            act(sp, sp, AF.Ln, ["tmpf0"], ["tmpf0"], bias=1.0)
            if d == 0:
                S.op("dve", lambda e: e.tensor_tensor_scan(out=cs, data0=onesR, data1=sp, initial=0.0, op0=ALU.mult, op1=ALU.add), ["onesR", "tmpf0"], ["tmpf1"])
                ends = tmpf[1][:].rearrange("p (c x) -> p c x", c=4)[:, :, 127]
            else:
                S.op("dve", lambda e: e.tensor_tensor_scan(out=rev(cs), data0=onesR, data1=rev(sp), initial=0.0, op0=ALU.mult, op1=ALU.add), ["onesR", "tmpf0"], ["tmpf1"])
                ends = tmpf[1][:].rearrange("p (c x) -> p c x", c=4)[:, :, 0]
            act(gtb[:, s, 0, :], cs, AF.Exp, ["tmpf1"], [nm], scale=-1.0 / 16)
            act(gtb[:, s, 1, :], cs, AF.Exp, ["tmpf1"], [nm], scale=1.0 / 16)
            act(gee[:, s, :], ends, AF.Exp, ["tmpf1"], [nm], scale=-1.0 / 16)
            gcache[(g, d)] = (s, nm)
            return gcache[(g, d)]

        def tables(ti, d):
            s, nm = gtab(ti // 4, d)
            cg_ = ti % 4
            return (gtb[:, s, 0, cg_ * 128:(cg_ + 1) * 128], nm, gtb[:, s, 1, cg_ * 128:(cg_ + 1) * 128], nm, gee[:, s, cg_:cg_ + 1])

        for si, (t0, n, latent) in enumerate(SEQS):
            for d in range(2):
                for h in range(4):
                    if latent:
                        dma("sp", Srun[32 * h:32 * h + 32, d, :], sgla0[l, d, h, :, :], [], ["gSrun"])
                if not latent:
                    S.op("dve", lambda e, d=d: e.memset(Srun[:, d, :], 0.0), [], ["gSrun"])
                order = list(range(n)) if d == 0 else list(range(n - 1, -1, -1))
                g1 = {}

                def g1_front(c, d=d, g1=g1):
                    ti = t0 + c
                    Eq_, Eqn, Ek_, Ekn, ee = tables(ti, d)
                    kf, kfn = rf()
                    tt("dve", kf, gk[:, ti * 128:(ti + 1) * 128], Ek_, ALU.mult, ["gk", Ekn], [kfn])
                    b, bn = nb()
                    tr(b[:, 0:128], kf, identF[:], [kfn, "identF"], [bn])
                    kt, ktn = rb()
                    act(kt, b[:, 0:128], AF.Copy, [bn], [ktn])
                    g1[c] = (kt, ktn, ee, Eqn)

                def g1_back(c, d=d, g1=g1):
                    ti = t0 + c
                    kt, ktn, ee, Eqn = g1.pop(c)
                    for hl in range(2):
                        S.op("dve", lambda e, hl=hl, d=d, c=c: e.tensor_copy(out=Sp[:, d, c, hl, hl * 64:(hl + 1) * 64], in_=Srun[:, d, :]), ["gSrun"], ["Sp"])
                    b2, b2n = nb()
                    for h in range(4):
                        mm(b2[:, h * 64:(h + 1) * 64], kt, gvpad[:, ti, h, (h % 2) * 64:(h % 2) * 64 + 64], True, True, [ktn, "gvpad"], [b2n])
                    ts("dve", U, b2[:, 0:64], hmask[:, 0:1], None, ALU.mult, None, [b2n, "hmask"], ["gU"])
                    for h in range(1, 4):
                        stt("dve", U, b2[:, h * 64:(h + 1) * 64], hmask[:, h:h + 1], U, ALU.mult, ALU.add, [b2n, "hmask", "gU"], ["gU"])
                    tt("dve", Srun[:, d, :], Srun[:, d, :], U, ALU.add, ["gSrun", "gU"], ["gSrun"])
                    ts("dve", Srun[:, d, :], Srun[:, d, :], ee, None, ALU.mult, None, ["gSrun", Eqn], ["gSrun"])

                g1_front(order[0])
                for oi, c in enumerate(order):
                    if oi + 1 < len(order):
                        g1_front(order[oi + 1])
                    g1_back(c)
                if not latent:
                    for h in range(4):
                        dma("sp", osgla[si, l, d, h, :, :], Srun[32 * h:32 * h + 32, d, :], ["gSrun"], ["osgla_d"])
            g2u = [(c, d, h) for c in range(n) for d in range(2) for h in range(4)]
            obs = [(pbank[6], "pb6"), (pbank[7], "pb7")]
            tb2 = {}
            g2 = {}

            def g2_front(u):
                c, d, h = u
                ti = t0 + c
                if h == 0:
                    Eq_, Eqn, Ek_, Ekn, ee = tables(ti, d)
                    kfb, kfbn = rb()
                    tt("dve", kfb, gk[:, ti * 128:(ti + 1) * 128], Ek_, ALU.mult, ["gk", Ekn], [kfbn])
                    tb2[(c, d)] = (Eq_, Eqn, kfb, kfbn)
                Eq_, Eqn, kfb, kfbn = tb2[(c, d)]
                qh, qhn = rb()
                stt("dve", qh, gq[:, ti * 128:(ti + 1) * 128], hmask[:, h:h + 1], Eq_, ALU.mult, ALU.mult, ["gq", "hmask", Eqn], [qhn])
                b, bn = nb()
                mm(b[:, 0:128], kfb, qh, True, True, [kfbn, qhn], [bn])
                g2[u] = (qh, qhn, b, bn)

            def g2_mid(u):
                c, d, h = u
                qh, qhn, b, bn = g2[u]
                a_, an_ = rb()
                tt("dve", a_, b[:, 0:128], (maskF if d == 0 else maskB)[:], ALU.mult, [bn, "maskF", "maskB"], [an_])
                g2[u] = (qh, qhn, a_, an_)

            def g2_back(u):
                c, d, h = u
                ti = t0 + c
                g = ti // 4
                cg = ti % 4
                j, hl = h // 2, h % 2
                ob, obn = obs[j]
                qh, qhn, a_, an_ = g2.pop(u)
                first = (d == 0 and hl == 0)
                last = (d == 1 and hl == 1)
                mm(ob[:, 0:128], gvpad[:, ti, h, :], a_, first, False, ["gvpad", an_], [obn])
                mm(ob[:, 0:128], Sp[:, d, c, hl, :], qh, False, last, ["Sp", qhn], [obn])
                if d == 1 and h == 3:
                    for jj in range(2):
                        act(oG[:, jj, cg * 128:(cg + 1) * 128], obs[jj][0][:, 0:128], AF.Copy, [obs[jj][1]], ["goG"])
                    if cg == 3:
                        for jj in range(2):
                            head_norm(oG, "goG", jj, g, glagnT[:, l * 2 + jj:l * 2 + jj + 1], gg, "gg", 4 + jj)

            for ui in range(len(g2u) + 2):
                if ui < len(g2u):
                    g2_front(g2u[ui])
                if 1 <= ui <= len(g2u):
                    g2_mid(g2u[ui - 1])
                if ui >= 2:
                    g2_back(g2u[ui - 2])

    lreT = sbt("lreT", [128, 64]); limT = sbt("limT", [128, 64]); ldtT = sbt("ldtT", [128, 64])
    rows_to_cols(s5_lre2[:, :], 64, lreT[:], "lreT")
    rows_to_cols(s5_lim2[:, :], 64, limT[:], "limT")
    rows_to_cols(s5_ldt2[:, :], 64, ldtT[:], "ldtT")
    cmask = sbt("cmask", [128, 4, 128], BF16)
    dma("pool", cmask[:], cd["cmask"][:, :, :], [], ["cmask"])
    PI = math.pi
    negpi = sbt("negpi", [128, 1]); glm = sbt("glm", [128, 2]); s5fin = sbt("s5fin", [128, 8, 2]); s5x0 = sbt("s5x0", [128, 8, 2])
    S.op("dve", lambda e: e.memset(negpi[:], -PI), [], ["negpi"])
    S.op("dve", lambda e: e.memset(glm[:], 0.0), [], ["glm"])
    S.op("dve", lambda e: e.memset(glm[0:64, 0:1], 1.0), ["glm"], ["glm"])
    S.op("dve", lambda e: e.memset(glm[64:128, 1:2], 1.0), ["glm"], ["glm"])

    def rev3(ap3):
        a = ap3.ap
        n = a[-1][1]
        return AP(ap3.tensor, ap3.offset + (n - 1) * a[-1][0], [list(a[0]), list(a[1]), [-a[-1][0], n]])

    I32 = mybir.dt.int32

    def sin_reduced(out, ang, w, names_in, name_out, out_view=None):
        q = tmpf[1][:, 0:w]
        qi = tmpf[1][:, 256:256 + w].bitcast(I32)
        r = tmpf[1][:, 128:128 + w]
        c = tmpf[1][:, 384:384 + w]
        ts("dve", q, ang, 1.0 / (2 * PI), None, ALU.mult, None, names_in, ["tmpf1"])
        S.op("dve", lambda e: e.tensor_copy(out=qi, in_=q), ["tmpf1"], ["tmpf1"])
        S.op("dve", lambda e: e.tensor_copy(out=q, in_=qi), ["tmpf1"], ["tmpf1"])
        stt("dve", r, q, -2 * PI, ang, ALU.mult, ALU.add, ["tmpf1"] + names_in, ["tmpf1"])
        ts("dve", c, r, PI, -2 * PI, ALU.is_gt, ALU.mult, ["tmpf1"], ["tmpf1"])
        tt("dve", r, r, c, ALU.add, ["tmpf1"], ["tmpf1"])
        ts("dve", c, r, -PI, 2 * PI, ALU.is_lt, ALU.mult, ["tmpf1"], ["tmpf1"])
        tt("dve", r, r, c, ALU.add, ["tmpf1"], ["tmpf1"])
        act(out, r if out_view is None else out_view(r), AF.Sin, ["tmpf1"], name_out)

    def mixer_s5(l):
        su = carve(0, 1536, BF16).rearrange("p (j t) -> p j t", j=2)
        ysum = carve(1536, 1536, BF16).rearrange("p (j t) -> p j t", j=2)
        s5x = carve(3072, 1536, BF16)
        rtab = carve(3072, 1024, F32).rearrange("p (g x) -> p g x", g=8)
        Bb = carve(4608, 2048, BF16).rearrange("p (d c g x) -> p d c g x", d=2, c=2, g=8)
        Cl = carve(6656, 2048, BF16).rearrange("p (d c g x) -> p d c g x", d=2, c=2, g=8)
        csT = carve(8704, 2048, BF16).rearrange("p (d c g x) -> p d c g x", d=2, c=2, g=8)
        sm = carve(10752, 256, F32).rearrange("p (a g) -> p a g", g=8)
        wri = carve(11008, 1024, F32).rearrange("p (c g x) -> p c g x", c=2, g=4)
        zri = rotf[:].rearrange("p (c g) x -> p c g x", c=2)
        xb = rotb[:].rearrange("p (s g) x -> p s g x", s=4)
        DT, A_, W_, R_, T0, T1, NR, NI, DEN, CR, CI = range(11)
        E1R, E1I, E127R, E127I, E128R, E128I = 11, 12, 13, 14, 15, 16
        WIR, WII = 17, 18
        smd = lambda d, i: sm[:, (d * 0 + i), :]
        bbf = wri.rearrange("p c g x -> p (c g x)")
        wv, wn = load_w(w_in[l, :, 1024:1280], 8, 256, key="s5%d" % l)
        for g in range(3):
            for j in range(2):
                proj_fm(wv, wn, 8, j * 128, hT, "hT", g, lambda b, bn, j=j, g=g: act(su[:, j, g * 512:(g + 1) * 512], b[:], AF.Copy, [bn], ["su"]))
        def sin_wide(out, ang, names_in, name_out):
            ys_ = carve(1536, 3072, F32)
            q = ys_[:, 0:1024]
            r = ys_[:, 1024:2048]
            c = ys_[:, 2048:3072]
            qi = rotb[:].rearrange("p a x -> p (a x)").bitcast(I32)
            ts("dve", q, ang, 1.0 / (2 * PI), None, ALU.mult, None, names_in, ["s5q"])
            S.op("dve", lambda e: e.tensor_copy(out=qi, in_=q), ["s5q"], ["s5qi"])
            S.op("dve", lambda e: e.tensor_copy(out=q, in_=qi), ["s5qi"], ["s5q"])
            stt("dve", r, q, -2 * PI, ang, ALU.mult, ALU.add, ["s5q"] + names_in, ["s5r"])
            ts("dve", c, r, PI, -2 * PI, ALU.is_gt, ALU.mult, ["s5r"], ["s5c"])
            tt("dve", r, r, c, ALU.add, ["s5r", "s5c"], ["s5r"])
            ts("dve", c, r, -PI, 2 * PI, ALU.is_lt, ALU.mult, ["s5r"], ["s5c"])
            tt("dve", r, r, c, ALU.add, ["s5r", "s5c"], ["s5r"])
            act(out, r, AF.Sin, ["s5r"], name_out)

        par = {}
        for d in range(2):
            col = (l * 2 + d) * 8
            pr = {}
            t = lambda i, d=d: sm[:, d * 16 + i, :]
            N = "s5sm"
            act(t(0), ldtT[:, col:col + 8], AF.Exp, ["ldtT"], [N])
            tt("dve", t(1), lreT[:, col:col + 8], t(0), ALU.mult, ["lreT", N], [N])
            tt("dve", t(2), limT[:, col:col + 8], t(0), ALU.mult, ["limT", N], [N])
            act(t(3), t(1), AF.Exp, [N], [N])
            ang3 = tmpf[0][:, 0:24]
            ang3v = ang3.rearrange("p (m g) -> p m g", m=3)
            for mi, mult in enumerate((1.0, 127.0, 128.0)):
                ts("dve", ang3v[:, mi, :], t(2), float(mult), None, ALU.mult, None, [N], ["tmpf0"])
            r0 = sm[:, d * 16 + 6, :]
            cos_out = AP(r0.tensor, r0.offset, [list(r0.ap[0]), [16, 3], [1, 8]])
            sin_out = AP(r0.tensor, r0.offset + 8, [list(r0.ap[0]), [16, 3], [1, 8]])
            sin_reduced(sin_out, ang3, 24, ["tmpf0"], [N], out_view=lambda a: a.rearrange("p (m g) -> p m g", m=3))
            ts("dve", ang3, ang3, PI / 2, None, ALU.add, None, ["tmpf0"], ["tmpf0"])
            sin_reduced(cos_out, ang3, 24, ["tmpf0"], [N], out_view=lambda a: a.rearrange("p (m g) -> p m g", m=3))
            tt("dve", t(12), t(3), t(6), ALU.mult, [N], [N])
            ts("dve", t(12), t(12), -1.0, None, ALU.add, None, [N], [N])
            tt("dve", t(13), t(3), t(7), ALU.mult, [N], [N])
            tt("dve", t(4), lreT[:, col:col + 8], lreT[:, col:col + 8], ALU.mult, ["lreT"], [N])
            tt("dve", t(5), limT[:, col:col + 8], limT[:, col:col + 8], ALU.mult, ["limT"], [N])
            tt("dve", t(4), t(4), t(5), ALU.add, [N], [N])
            S.op("dve", lambda e, t=t: e.reciprocal(t(4), t(4)), [N], [N])
            tt("dve", t(14), t(12), lreT[:, col:col + 8], ALU.mult, [N, "lreT"], [N])
            tt("dve", t(5), t(13), limT[:, col:col + 8], ALU.mult, [N, "limT"], [N])
            tt("dve", t(14), t(14), t(5), ALU.add, [N], [N])
            tt("dve", t(14), t(14), t(4), ALU.mult, [N], [N])
            tt("dve", t(15), t(13), lreT[:, col:col + 8], ALU.mult, [N, "lreT"], [N])
            tt("dve", t(5), t(12), limT[:, col:col + 8], ALU.mult, [N, "limT"], [N])
            tt("dve", t(15), t(15), t(5), ALU.subtract, [N], [N])
            tt("dve", t(15), t(15), t(4), ALU.mult, [N], [N])
            for ii in (6, 7, 10, 11):
                tt("dve", t(ii), t(ii), t(3), ALU.mult, [N], [N])
            angw = rotf[:]
            pb_ = posfb[:, 0, :]
            posb = AP(pb_.tensor, pb_.offset, [list(pb_.ap[0]), [0, 8], list(pb_.ap[1])])
            wb_ = bcast_last(t(2), 128)
            tt("dve", angw, posb, wb_, ALU.mult, ["posfb", N], ["rotf"])
            tt("dve", angw, angw, wb_, ALU.subtract, ["rotf", N], ["rotf"])
            sin_wide(csT[:, d, 1].rearrange("p g x -> p (g x)"), rotf[:].rearrange("p g x -> p (g x)"), ["rotf"], ["csT"])
            ts("dve", angw, angw, PI / 2, None, ALU.add, None, ["rotf"], ["rotf"])
            sin_wide(csT[:, d, 0].rearrange("p g x -> p (g x)"), rotf[:].rearrange("p g x -> p (g x)"), ["rotf"], ["csT"])
            bre = bbf[:, 0:128].rearrange("p (g x) -> p g x", g=8)
            bim = bbf[:, 128:256].rearrange("p (g x) -> p g x", g=8)
            bbr = bbf[:, 256:384].rearrange("p (g x) -> p g x", g=8)
            bbi = bbf[:, 384:512].rearrange("p (g x) -> p g x", g=8)
            tmpb = bbf[:, 512:640].rearrange("p (g x) -> p g x", g=8)
            inb = bbf[:, 640:768]
            for (dst_, src_) in ((bre, s5_bre), (bim, s5_bim)):
                e0 = src_[l, d, 0:1, 0:1, 0:1]
                sap = AP(e0.tensor, e0.offset, [[16, 128], [2048, 8], [1, 16]])
                dma("sp", dst_, sap, [], ["s5b"])
            crb = bcast_last(t(14), 16)
            cib = bcast_last(t(15), 16)
            tt("dve", bbr, bre, crb, ALU.mult, ["s5b", N], ["s5b"])
            tt("dve", tmpb, bim, cib, ALU.mult, ["s5b", N], ["s5b"])
            tt("dve", bbr, bbr, tmpb, ALU.subtract, ["s5b"], ["s5b"])
            tt("dve", bbi, bim, crb, ALU.mult, ["s5b", N], ["s5b"])
            tt("dve", tmpb, bre, cib, ALU.mult, ["s5b", N], ["s5b"])
            tt("dve", bbi, bbi, tmpb, ALU.add, ["s5b"], ["s5b"])
            for c_, bb_ in ((0, bbr), (1, bbi)):
                for kc in range(2):
                    inv = inb.rearrange("p (a b x) -> p a b x", a=4, b=2)
                    for glb in range(2):
                        ts("dve", inv[:, :, glb, :], bb_[:, 4 * kc:4 * kc + 4, :], glm[:, glb:glb + 1], None, ALU.mult, None, ["s5b", "glm"], ["s5in"])
                    b, bn = nb()
                    tr(b[:, 0:128], inb, identF[:], ["s5in", "identF"], [bn])
                    for gpl in range(4):
                        ts("dve", Bb[:, d, c_, 4 * kc + gpl, :], b[:, 0:128], hmask[:, gpl:gpl + 1], None, ALU.mult, None, [bn, "hmask"], ["Bb"])
            for c_, src_ in ((0, s5_cre), (1, s5_cim)):
                for mc in range(2):
                    cin = bbf[:, 768:896].rearrange("p (b x) -> p b x", b=2)
                    rows = src_[l, d].rearrange("g h p -> (g h) p")[mc * 128:(mc + 1) * 128, :]
                    for glb in range(2):
                        dma("sp", cin[:, glb, :], rows, [], ["s5cin"])
                    b, bn = nb()
                    tr(b[:, 0:128], bbf[:, 768:896], identF[:], ["s5cin", "identF"], [bn])
                    cT_ = bbf[:, 896:1024]
                    act(cT_, b[:, 0:128], AF.Copy, [bn], ["s5cT"], scale=(1.0 if c_ == 0 else -1.0))
                    for gpl in range(4):
                        tt("dve", Cl[:, d, c_, 4 * mc + gpl, :], cT_, cmask[:, gpl, :], ALU.mult, ["s5cT", "cmask"], ["Cl"])
        S.barrier()
        def cmul(outr, outi, ar, ai, br, bi, n1, n2, wd=8, eng="dve"):
            r4, r5, r6 = (4, 5, 20) if eng == "dve" else (14, 15, 30)
            T4 = sm[:, r4, 0:wd]
            T5 = sm[:, r5, 0:wd]
            T6 = sm[:, r6, 0:wd]
            tn = "s5tmp_" + eng
            tt(eng, T4, ar, br, ALU.mult, n1, [tn])
            tt(eng, T5, ai, bi, ALU.mult, n1, [tn])
            tt(eng, T5, T4, T5, ALU.subtract, [tn], [tn])
            tt(eng, T4, ar, bi, ALU.mult, n1, [tn])
            tt(eng, T6, ai, br, ALU.mult, n1, [tn])
            tt(eng, outi, T6, T4, ALU.add, [tn], n2)
            S.op(eng, lambda e: e.tensor_copy(out=outr, in_=T5), [tn], n2)

        slot = [0]
        ucnt = [0]
        def bslot(ap2):
            return ap2.rearrange("p (g x) -> p g x", g=4)
        tf0 = tmpf[0][:].bitcast(BF16)
        tf1 = tmpf[1][:].bitcast(BF16)
        rfb = rotf[:].rearrange("p a x -> p (a x)").bitcast(BF16)
        wsb = wri.rearrange("p c g x -> p (c g x)").bitcast(BF16)
        s5slots = {("bu", 0, 0): bslot(tf0[:, 0:512]), ("bu", 0, 1): bslot(tf0[:, 512:1024]),
                   ("bu", 1, 0): bslot(tf1[:, 0:512]), ("bu", 1, 1): bslot(tf1[:, 512:1024]),
                   ("t", 0): bslot(rfb[:, 0:512]), ("t", 1): bslot(rfb[:, 512:1024]), ("t", 2): bslot(rfb[:, 1024:1536]), ("t", 3): bslot(rfb[:, 1536:2048]),
                   ("z", 0): bslot(wsb[:, 0:512]), ("z", 1): bslot(wsb[:, 512:1024]),
                   ("w", 0, 0): bslot(wsb[:, 1024:1536]), ("w", 0, 1): bslot(wsb[:, 1536:2048]),
                   ("w", 1, 0): bslot(sqb[0][:]), ("w", 1, 1): bslot(sqb[1][:])}
        S.barrier()
        rt_dir = [None]
        units = []
        for si, (t0, n, latent) in enumerate(SEQS):
            for d in range(2):
                order = list(range(n)) if d == 0 else list(range(n - 1, -1, -1))
                for ci_, c in enumerate(order):
                    for kc in range(2):
                        units.append(dict(si=si, t0=t0, n=n, latent=latent, d=d, ci=ci_, c=c, kc=kc, u=len(units) % 2))

        def stage_a(un):
            d, kc, ti, u_ = un["d"], un["kc"], un["t0"] + un["c"], un["u"]
            br_, brn = nb()
            bi_, bin_ = nb()
            for gpl in range(4):
                gp = 4 * kc + gpl
                mm(br_[:, gpl * 128:(gpl + 1) * 128], Bb[:, d, 0, gp, :], su[:, kc, ti * 128:(ti + 1) * 128], True, True, ["Bb", "su"], [brn])
                mm(bi_[:, gpl * 128:(gpl + 1) * 128], Bb[:, d, 1, gp, :], su[:, kc, ti * 128:(ti + 1) * 128], True, True, ["Bb", "su"], [bin_])
            bur = br_[:].rearrange("p (g x) -> p g x", g=4)
            bui = bi_[:].rearrange("p (g x) -> p g x", g=4)
            if d == 1:
                bur = rev3(bur)
                bui = rev3(bui)
            burb, buib = s5slots[("bu", u_, 0)], s5slots[("bu", u_, 1)]
            bun = ("s5bu", u_)
            act(burb, bur, AF.Copy, [brn], [bun])
            act(buib, bui, AF.Copy, [bin_], [bun])

        def stage_b(un):
            si, t0, n, latent, d, ci_, c, kc, u_ = (un[x] for x in ("si", "t0", "n", "latent", "d", "ci", "c", "kc", "u"))
            ti = t0 + c
            t = lambda i, d=d: sm[:, d * 16 + i, :]
            WR = t(12)
            WI = t(13)
            if ci_ == 0 and kc == 0:
                if latent:
                    e0 = ss50[l, d, 0:1, 0:1, 0:1]
                    for glb in range(2):
                        sap = AP(e0.tensor, e0.offset + glb * 128, [[2, 64], [256, 8], [1, 2]])
                        dma("sp", s5x0[glb * 64:(glb + 1) * 64, :, :], sap, [], ["s5x0"])
                    cmul(WR, WI, s5x0[:, :, 0], s5x0[:, :, 1], t(6), t(7), ["s5x0", "s5sm"], ["s5w0"])
                else:
                    S.op("dve", lambda e, WR=WR: e.memset(WR, 0.0), [], ["s5w0"])
                    S.op("dve", lambda e, WI=WI: e.memset(WI, 0.0), [], ["s5w0"])
            cs_ = csT[:, d, 0, 4 * kc:4 * kc + 4, :]
            sn_ = csT[:, d, 1, 4 * kc:4 * kc + 4, :]
            burb, buib = s5slots[("bu", u_, 0)], s5slots[("bu", u_, 1)]
            bun = ("s5bu", u_)
            t1, t2 = s5slots[("t", 0)], s5slots[("t", 1)]
            t3, t4 = s5slots[("t", 2)], s5slots[("t", 3)]
            zr, zi = s5slots[("z", 0)], s5slots[("z", 1)]
            wr_, wi_ = s5slots[("w", u_, 0)], s5slots[("w", u_, 1)]
            wrn = ("s5w", u_)
            tt("dve", t1, burb, cs_, ALU.mult, [bun, "csT"], [("s5t", 0)])
            tt("dve", t2, buib, sn_, ALU.mult, [bun, "csT"], [("s5t", 1)])
            tt("dve", t3, buib, cs_, ALU.mult, [bun, "csT"], [("s5t", 2)])
            tt("dve", t4, burb, sn_, ALU.mult, [bun, "csT"], [("s5t", 3)])
            tt("dve", zr, t1, t2, ALU.add, [("s5t", 0), ("s5t", 1)], ["s5zr"])
            tt("dve", zi, t3, t4, ALU.subtract, [("s5t", 2), ("s5t", 3)], ["s5zi"])
            gsl0 = slice(4 * kc, 4 * kc + 4)
            if rt_dir[0] != d:
                rt_dir[0] = d
                S.op("dve", lambda e, d=d: e.tensor_copy(out=rtab, in_=bcast_last(sm[:, d * 16 + 3, :], 128)), ["s5sm"], ["rtab"])
                S.op("dve", lambda e: e.memset(rtab[:, :, 0:1], 0.0), ["rtab"], ["rtab"])
            if not (ci_ == 0 and not latent):
                tt("dve", zr[:, :, 0], zr[:, :, 0], WR[:, gsl0], ALU.add, ["s5zr", ("s5w0", kc)], ["s5zr"])
                tt("dve", zi[:, :, 0], zi[:, :, 0], WI[:, gsl0], ALU.add, ["s5zi", ("s5w0", kc)], ["s5zi"])
            fl = lambda a3: a3.rearrange("p g x -> p (g x)")
            rt_ = fl(rtab[:, 4 * kc:4 * kc + 4, :])
            S.op("dve", lambda e, rt_=rt_, zr=zr, wr_=wr_: e.tensor_tensor_scan(out=fl(wr_), data0=rt_, data1=fl(zr), initial=0.0, op0=ALU.mult, op1=ALU.add),
                 ["rtab", "s5zr"], [wrn])
            S.op("dve", lambda e, rt_=rt_, zi=zi, wi_=wi_: e.tensor_tensor_scan(out=fl(wi_), data0=rt_, data1=fl(zi), initial=0.0, op0=ALU.mult, op1=ALU.add),
                 ["rtab", "s5zi"], [wrn])
            s0 = slot[0] % 2
            slot[0] += 1
            p1, p2 = xb[:, 2 * s0], xb[:, 2 * s0 + 1]
            p3, p4 = bslot(s5x[:, 2048:2560]), bslot(s5x[:, 2560:3072])
            pn34 = [("s5p", 3), ("s5p", 4)]
            xn = ("xb", s0)
            tt("dve", p1, wr_, cs_, ALU.mult, [wrn, "csT"], [xn])
            stt("dve", p2, wi_, -1.0, sn_, ALU.mult, ALU.mult, [wrn, "csT"], [xn])
            tt("dve", p3, wr_, sn_, ALU.mult, [wrn, "csT"], [pn34[0]])
            tt("dve", p4, wi_, cs_, ALU.mult, [wrn, "csT"], [pn34[1]])
            last = (ci_ == n - 1)
            gsl = slice(4 * kc, 4 * kc + 4)
            wer = wr_[:, :, 127]
            wei = wi_[:, :, 127]
            wn_ = [wrn, "s5sm"]
            if last and not latent:
                cmul(s5fin[:, gsl, 0], s5fin[:, gsl, 1], wer, wei, t(8)[:, gsl], t(9)[:, gsl], wn_, ["s5fin"], 4, "pool")
            if not last:
                cmul(WR[:, gsl], WI[:, gsl], wer, wei, t(10)[:, gsl], t(11)[:, gsl], wn_, [("s5w0", kc)], 4, "pool")
            by, byn = nb()
            k_ = 0
            for gpl in range(4):
                gp = 4 * kc + gpl
                for (pp, pnm, cc) in ((p1, xn, 0), (p2, xn, 0), (p3, pn34[0], 1), (p4, pn34[1], 1)):
                    mm(by[:, 0:128], Cl[:, d, cc, gp, :], pp[:, gpl, :], k_ == 0, k_ == 15, ["Cl", pnm], [byn])
                    k_ += 1
            ysl = ysum[:, kc, ti * 128:(ti + 1) * 128]
            if d == 0:
                act(ysl, by[:, 0:128], AF.Copy, [byn], [("ysum", (kc, ti))])
            else:
                ysr = AP(ysl.tensor, ysl.offset + 127 * ysl.ap[-1][0], [list(ysl.ap[0]), [-ysl.ap[-1][0], 128]])
                tt("dve", ysr, by[:, 0:128], ysr, ALU.add, [byn, ("ysum", (kc, ti))], [("ysum", (kc, ti))])
            if last and kc == 1 and not latent:
                e0 = oss5[si, l, d, 0:1, 0:1, 0:1]
                for glb in range(2):
                    dap = AP(e0.tensor, e0.offset + glb * 128, [[2, 64], [256, 8], [1, 2]])
                    dma("sp", dap, s5fin[glb * 64:(glb + 1) * 64, :, :], ["s5fin"], ["oss5_d"])

        do_mod = (l + 1 < DEPTH)
        modq = []
        if do_mod:
            align_w()
            prefetch("ada%d_0" % (l + 1), mod_src(l + 1, 0), 8, 1024)
        stage_a(units[0])
        for i_, un in enumerate(units):
            if i_ + 1 < len(units):
                stage_a(units[i_ + 1])
            if do_mod and i_ % 8 == 2:
                cb = i_ // 8
                if modq:
                    mod_evac(l + 1, *modq.pop(0))
                if cb + 1 < 6:
                    prefetch("ada%d_%d" % (l + 1, cb + 1), mod_src(l + 1, cb + 1), 8, 1024)
                bk = (pbank[6], "pb6") if cb % 2 == 0 else (pbank[7], "pb7")
                b_, bn_ = mod_mm(l + 1, cb, bank=bk)
                modq.append((cb, b_, bn_))
            stage_b(un)
        while modq:
            mod_evac(l + 1, *modq.pop(0))
        if do_mod:
            mod_finish(l + 1)
        align_w()
        prefetch("glu%d" % l, s5_wglu[l, :, :], 2, 512)
        prefetch("gla%d" % l, w_in[l, :, 1280:2080], 8, 800)
        S.barrier()
        ge = carve(4608, 1536, BF16).rearrange("p (j t) -> p j t", j=2)
        for j in range(2):
            stt("dve", ysum[:, j, :], su[:, j, :], s5dT[:, l * 2 + j:l * 2 + j + 1], ysum[:, j, :], ALU.mult, ALU.add, ["su", "s5dT", "ysum"], ["ysum"])
            for g in range(3):
                ysl = ysum[:, j, g * 512:(g + 1) * 512]
                tt("dve", tmpf[0][:], ysl, ysl, ALU.mult, ["ysum"], ["tmpf0"])
                ts("dve", tmpf[0][:], tmpf[0][:], 0.044715, 1.0, ALU.mult, ALU.add, ["tmpf0"], ["tmpf0"])
                tt("dve", tmpf[0][:], tmpf[0][:], ysl, ALU.mult, ["tmpf0", "ysum"], ["tmpf0"])
                act(tmpf[1][:], tmpf[0][:], AF.Sigmoid, ["tmpf0"], ["tmpf1"], scale=1.5957691216057308)
                tt("dve", ge[:, j, g * 512:(g + 1) * 512], ysl, tmpf[1][:], ALU.mult, ["ysum", "tmpf1"], ["s5ge"])
        wv, wn = load_w(s5_wglu[l, :, :], 2, 512, key="glu%d" % l)
        for j in range(2):
            for g in range(3):
                gt = sqb[g % 2]
                gtn = "sqb%d" % (g % 2)
                proj_fm(wv, wn, 2, (2 + j) * 128, ge, "s5ge", g,
                        lambda b, bn, gt=gt, gtn=gtn, j=j: act(gt[:], b[:], AF.Sigmoid, [bn, "bgluT"], [gtn], bias=bgluT[:, l * 4 + 2 + j:l * 4 + 3 + j]))
                proj_fm(wv, wn, 2, j * 128, ge, "s5ge", g,
                        lambda b, bn, gt=gt, gtn=gtn, j=j, g=g: stt("dve", brT[:, 2 + j, g * 512:(g + 1) * 512], b[:], bgluT[:, l * 4 + j:l * 4 + j + 1], gt[:],
                                                                 ALU.add, ALU.mult, [bn, "bgluT", gtn], [("brT", 2 + j)]))

    def mixer_na(l):
        nq = carve(0, 1536, BF16).rearrange("p (j t) -> p j t", j=2)
        nk = carve(1536, 1536, BF16).rearrange("p (j t) -> p j t", j=2)
        vpad = carve(3072, 3072, BF16).rearrange("p (a h c) -> p a h c", a=12, h=4)
        kcT = carve(6144, 256, BF16).rearrange("p (j t) -> p j t", j=2)
        vcpad = carve(6400, 512, BF16).rearrange("p (a h c) -> p a h c", a=2, h=4)
        opad = carve(6912, 256, BF16).rearrange("p (h c) -> p h c", h=4)
        BT = carve(7168, 1920, BF16).rearrange("p (h d c) -> p h d c", h=4, d=15)
        negt = carve(9088, 64, F32)
        pts = [carve(9152 + i * 256, 256, BF16) for i in range(3)]
        stg = carve(9920, 256, F32)
        stg2 = carve(10176, 256, F32)
        BTall = carve(0, 3840, F32).rearrange("p (a c) -> p a c", a=60)
        den = carve(11392, 512, F32)
        dma("sp", rowst[0:60, 0:31], na_rpb[l].rearrange("h d i -> (h d) i"), [], ["rowst"])
        b, bn = nb()
        tr(b[0:31, 0:60], rowst[0:60, 0:31], identF[0:60, 0:60], ["rowst", "identF"], [bn])
        act(Rrp[0:31, :], b[0:31, 0:60], AF.Copy, [bn], ["Rrp"])
        for q8 in range(8):
            b, bn = nb()
            for qq in range(8):
                qc = q8 * 8 + qq
                g0 = gbig[0:31, 63 - qc:63 - qc + 64]
                mm(b[0:64, qq * 60:(qq + 1) * 60], g0, Rrp[0:31, :], True, True, ["gbig", "Rrp"], [bn])
            bi = b[0:64, 0:1]
            src = AP(bi.tensor, bi.offset, [list(bi.ap[0]), [1, 60], [60, 8]])
            S.op("dve", lambda e, src=src, q8=q8: e.tensor_copy(out=BTall[0:64, :, q8 * 8:(q8 + 1) * 8], in_=src), [bn], ["BTall"])
        m0 = mneg[0:64, :]
        mk = AP(m0.tensor, m0.offset, [list(m0.ap[0]), [0, 60], list(m0.ap[1])])
        BT64 = carve(3840, 1920, BF16)
        tt("dve", BT64[0:64].rearrange("p (a c) -> p a c", a=60), BTall[0:64], mk, ALU.add, ["BTall", "mneg"], ["BT64"])
        BTflat = BT.rearrange("p h d c -> p (h d c)")
        for i8 in range(8):
            b, bn = nb()
            mm(b[:, 0:480], dupI[0:64, :], BT64[0:64, i8 * 480:(i8 + 1) * 480], True, True, ["dupI", "BT64"], [bn])
            act(BTflat[:, i8 * 480:(i8 + 1) * 480], b[:, 0:480], AF.Copy, [bn], ["BT"])
        S.barrier()
        S.op("pool", lambda e: e.memset(vpad, 0.0), [], ["vpad"])
        S.op("pool", lambda e: e.memset(vcpad, 0.0), [], ["vcpad"])
        S.op("pool", lambda e: e.memset(opad, 0.0), [], ["opad"])
        S.op("dve", lambda e: e.memset(negt, NEG), [], ["negt"])
        for h in range(4):
            S.op("pool", lambda e, h=h: e.memset(opad[:, h, (h % 2) * 64:(h % 2) * 64 + 64], 1.0), ["opad"], ["opad"])
        wv, wn = load_w(w_in[l, :, 2080:2848], 8, 768, key="na%d" % l)
        for g in range(3):
            for j in range(2):
                proj_fm(wv, wn, 8, j * 128, hT, "hT", g,
                        lambda b, bn, j=j, g=g: act(nq[:, j, g * 512:(g + 1) * 512], b[:], AF.Copy, [bn], ["nq"], scale=0.125))
                proj_fm(wv, wn, 8, 256 + j * 128, hT, "hT", g,
                        lambda b, bn, j=j, g=g: act(nk[:, j, g * 512:(g + 1) * 512], b[:], AF.Copy, [bn], ["nk"]))
        for ti in range(NTILE):
            def ev_v(b, bn, ti=ti):
                for h in range(4):
                    act(vpad[:, ti, h, (h % 2) * 64:(h % 2) * 64 + 64], b[:, h * 64:(h + 1) * 64], AF.Copy, [bn], ["vpad"])
                if ti < 4:
                    act(stg2, b[:, 0:256], AF.Copy, [bn], ["stg2"])
                    dma("sp", onv[ti // 2, l, (ti % 2) * 128:(ti % 2 + 1) * 128, :], stg2, ["stg2"], ["onv_d"])
            proj_tm(wv, wn, 8, 512, 256, hT, "hT", ti, ev_v)
            if ti < 4:
                def ev_k(b, bn, ti=ti):
                    act(stg, b[:, 0:256], AF.Copy, [bn], ["stg"])
                    dma("sp", onk[ti // 2, l, (ti % 2) * 128:(ti % 2 + 1) * 128, :], stg, ["stg"], ["onk_d"])
                proj_tm(wv, wn, 8, 256, 256, hT, "hT", ti, ev_k)
        align_w()
        prefetch("mg%d_0_0" % l, w_merge[l, :, 0:512], 8, 512)
        prefetch("br%d_0_0" % l, w_branch[l, 0, :, 0:512], 2, 512)
        for a in range(2):
            dma("sp", stg, ck[l, a * 128:(a + 1) * 128, :], [], ["stg"])
            b, bn = nb()
            for j in range(2):
                tr(b[:, j * 128:(j + 1) * 128], stg[:, j * 128:(j + 1) * 128], identF[:], ["stg", "identF"], [bn])
            act(kcT[:, :, a * 128:(a + 1) * 128], b[:, 0:256].rearrange("p (j t) -> p j t", j=2), AF.Copy, [bn], ["kcT"])
            dma("sp", stg2, cvv[l, a * 128:(a + 1) * 128, :], [], ["stg2"])
            for h in range(4):
                S.op("dve", lambda e, a=a, h=h: e.tensor_copy(out=vcpad[:, a, h, (h % 2) * 64:(h % 2) * 64 + 64], in_=stg2[:, h * 64:(h + 1) * 64]),
                     ["stg2"], ["vcpad"])
        pctr = [0]

        def attend(j, qlo, qn, keyspecs, out_cols):
            bo_, bon = pbank[6], "pb6"
            bd_, bdn = pbank[7], "pb7"
            n = len(keyspecs)
            LA = 2
            pend = {}

            def front(idx):
                (h, kT, kname, vp, vname, biasf) = keyspecs[idx]
                hl = h % 2
                bs_, bsn = nb()
                mm(bs_[:, 0:qn], kT, nq[hl * 64:(hl + 1) * 64, j, qlo:qlo + qn], True, True, [kname, "nq"], [bsn])
                pend[idx] = (bs_, bsn)

            def mid(idx):
                (h, kT, kname, vp, vname, biasf) = keyspecs[idx]
                bs_, bsn = pend[idx]
                pt = pts[pctr[0] % 3]
                ptn = "pt%d" % (pctr[0] % 3)
                pctr[0] += 1
                if biasf is None:
                    act(pt[:, 0:qn], bs_[:, 0:qn], AF.Exp, [bsn], [ptn])
                else:
                    tb, tbn = tmpf[idx % 2], "tmpf%d" % (idx % 2)
                    biasf(bs_, bsn, tb, tbn)
                    act(pt[:, 0:qn], tb[:, 0:qn], AF.Exp, [tbn], [ptn])
                pend[idx] = (h, vp, vname, pt, ptn)

            def back(idx):
                (h, vp, vname, pt, ptn) = pend.pop(idx)
                mm(bo_[:, 0:qn], vp, pt[:, 0:qn], idx == 0, idx == n - 1, [vname, ptn], [bon])
                mm(bd_[:, 0:qn], opad[:, h, :], pt[:, 0:qn], idx == 0, idx == n - 1, ["opad", ptn], [bdn])

            for i_ in range(n + 3):
                if i_ < n:
                    front(i_)
                if 1 <= i_ <= n:
                    mid(i_ - 1)
                if i_ >= 3:
                    back(i_ - 3)
            S.op("dve", lambda e: e.reciprocal(den[:, 0:qn], bd_[:, 0:qn]), [bdn], ["den"])
            tt("dve", brT[:, 6 + j, out_cols], bo_[:, 0:qn], den[:, 0:qn], ALU.mult, [bon, "den"], [("brT", 6 + j)])

        for sq in range(2):
            t0 = sq * 256
            for j in range(2):
                specs = []
                for hl in range(2):
                    h = 2 * j + hl
                    for kt in range(2):
                        ti = sq * 2 + kt
                        specs.append((h, nk[hl * 64:(hl + 1) * 64, j, ti * 128:(ti + 1) * 128], "nk", vpad[:, ti, h, :], "vpad", None))
                attend(j, t0, 256, specs, slice(t0, t0 + 256))
        for qi in range(8):
            t0 = 512 + qi * 128
            rows = [2 * qi, 2 * qi + 1]
            rs = [min(max(r - 4, 0), 8) for r in rows]
            jlo = rs[0] // 2
            jhi = (rs[1] + 7) // 2
            for j in range(2):
                specs = []
                for hl in range(2):
                    h = 2 * j + hl
                    for kj in range(jlo, jhi + 1):
                        ti = 4 + kj

                        def biasf(bs_, bsn, tb, tbn, h=h, kj=kj):
                            for krl in range(2):
                                kr = 2 * kj + krl
                                inw = [rs[qrl] <= kr < rs[qrl] + 8 for qrl in range(2)]
                                ps = slice(krl * 64, (krl + 1) * 64)
                                if inw[0] and inw[1]:
                                    b0 = BT[ps, h, kr - rows[0] + 7, :]
                                    bpair = AP(b0.tensor, b0.offset, [list(b0.ap[0]), [-64, 2], [1, 64]])
                                    tt("dve", tb[ps, 0:128].rearrange("p (a b) -> p a b", a=2), bs_[ps, 0:128].rearrange("p (a b) -> p a b", a=2),
                                       bpair, ALU.add, [bsn, "BT"], [tbn])
                                elif not inw[0] and not inw[1]:
                                    n0 = negt[ps, :]
                                    npair = AP(n0.tensor, n0.offset, [list(n0.ap[0]), [0, 2], [1, 64]])
                                    tt("dve", tb[ps, 0:128].rearrange("p (a b) -> p a b", a=2), bs_[ps, 0:128].rearrange("p (a b) -> p a b", a=2),
                                       npair, ALU.add, [bsn, "negt"], [tbn])
                                else:
                                    for qrl in range(2):
                                        qr = rows[qrl]
                                        osl = tb[ps, qrl * 64:(qrl + 1) * 64]
                                        isl = bs_[ps, qrl * 64:(qrl + 1) * 64]
                                        if inw[qrl]:
                                            tt("dve", osl, isl, BT[ps, h, kr - qr + 7, :], ALU.add, [bsn, "BT"], [tbn])
                                        else:
                                            tt("dve", osl, isl, negt[ps, :], ALU.add, [bsn, "negt"], [tbn])
                        specs.append((h, nk[hl * 64:(hl + 1) * 64, j, ti * 128:(ti + 1) * 128], "nk", vpad[:, ti, h, :], "vpad", biasf))
                    for a in range(2):
                        specs.append((h, kcT[hl * 64:(hl + 1) * 64, j, a * 128:(a + 1) * 128], "kcT", vcpad[:, a, h, :], "vcpad", None))
                attend(j, t0, 128, specs, slice(t0, t0 + 128))

    def layer(l):
        curl[0] = l
        if l == 0:
            emit_mod(0)
        norm_to_h(xT, "xT", 0, 0)
        allmix = all(m in mixers for m in ("ret", "s5", "gla", "na"))
        if not allmix:
            S.barrier()
            for n in range(6):
                S.op("pool", lambda e, n=n: e.memset(brT[:, n, :], 0.0), [], [("brT", n)])
        if "ret" in mixers:
            mixer_ret(l)
            S.barrier()
        if "s5" in mixers:
            mixer_s5(l)
            S.barrier()
        if "gla" in mixers:
            mixer_gla(l)
            S.barrier()
        if "na" in mixers:
            mixer_na(l)
        else:
            for n in (6, 7):
                S.op("pool", lambda e, n=n: e.memset(brT[:, n, :], 0.0), [], [("brT", n)])
        S.barrier()
        sT = carve(0, 6144, BF16).rearrange("p (k t) -> p k t", k=8)
        gts = [sqb[0], sqb[1]]
        mg_src = lambda n, half: w_merge[l, :, n * 1024 + half * 512: n * 1024 + half * 512 + 512]
        br_src = lambda n, half: w_branch[l, n, :, half * 512:(half + 1) * 512]
        for n in range(4):
            for half in range(2):
                wm, wmn = load_w(mg_src(n, half), 8, 512, key="mg%d_%d_%d" % (l, n, half))
                wbv, wbn = load_w(br_src(n, half), 2, 512, key="br%d_%d_%d" % (l, n, half))
                nxt = n * 2 + half + 1
                if nxt < 8:
                    prefetch("mg%d_%d_%d" % (l, nxt // 2, nxt % 2), mg_src(nxt // 2, nxt % 2), 8, 512)
                    prefetch("br%d_%d_%d" % (l, nxt // 2, nxt % 2), br_src(nxt // 2, nxt % 2), 2, 512)
                else:
                    prefetch("out%d" % l, w_out[l, :, :], 8, 1024)
                for fc in range(4):
                    fo = half * 4 + fc
                    for g in range(3):
                        gi = (fc + g) % 2
                        gt = gts[gi]
                        proj_fm(wm, wmn, 8, fc * 128, hT, "hT", g,
                                lambda b, bn, gt=gt, gi=gi, fo=fo: act(gt[:], b[:], AF.Sigmoid, [bn, "bmT"], ["sqb%d" % gi],
                                                                     bias=bmT[:, l * 32 + n * 8 + fo: l * 32 + n * 8 + fo + 1]))
                        brv = brT[:, 2 * n:2 * n + 2, :]

                        def ev_up(b, bn, gt=gt, gi=gi, fo=fo, g=g, n=n):
                            sl = sT[:, fo, g * 512:(g + 1) * 512]
                            if n == 0:
                                tt("dve", sl, b[:], gt[:], ALU.mult, [bn, "sqb%d" % gi], [("sT", (fo, g))])
                            else:
                                tt("dve", tmpf[1][:], b[:], gt[:], ALU.mult, [bn, "sqb%d" % gi], ["tmpf1"])
                                tt("pool", sl, sl, tmpf[1][:], ALU.add, ["tmpf1", ("sT", (fo, g))], [("sT", (fo, g))])
                        proj_fm(wbv, wbn, 2, fc * 128, brv, ("brT", 2 * n), g, ev_up)
        mT = carve(6144, 6144, BF16).rearrange("p (k t) -> p k t", k=8)
        wv, wn = load_w(w_out[l, :, :], 8, 1024, key="out%d" % l)
        for g in range(3):
            sb_, sbn_ = pbank[6], "pb6"
            for fo in range(8):
                def ev_m(b, bn, fo=fo, g=g, sb_=sb_, sbn_=sbn_):
                    act(mT[:, fo, g * 512:(g + 1) * 512], b[:], AF.Copy, [bn], ["mT"])
                    q = sqb[fo % 2]
                    act(q[:], b[:], AF.Square, [bn], ["sqb%d" % (fo % 2)])
                    mm(sb_[:], onesB[:], q[:], fo == 0, fo == 7, ["onesB", "sqb%d" % (fo % 2)], [sbn_])
                proj_fm(wv, wn, 8, fo * 128, sT, "sT", g, ev_m)
            act(tmpf[0][:], sb_[:], AF.Ln, [sbn_], ["tmpf0"], bias=EPS, scale=1.0 / 1024)
            act(rstd[:, g * 512:(g + 1) * 512], tmpf[0][:], AF.Exp, ["tmpf0"], [("rstd", g)], scale=-0.5)
        S.barrier()
        if l + 1 < DEPTH and "s5" not in mixers:
            emit_mod(l + 1)
        align_w()
        prefetch("m1_%d_0" % l, w_mlp1[l, :, 0:512], 8, 512)
        prefetch("m2_%d_0" % l, w_mlp2[l, 0:512, :], 4, 1024)
        resid_update(mT, "mT", 1, have_stats=True)
        norm_to_h(xT, "xT", 2, 24)
        S.barrier()
        for hg in range(8):
            w1, w1n = load_w(w_mlp1[l, :, hg * 512:(hg + 1) * 512], 8, 512, key="m1_%d_%d" % (l, hg))
            w2, w2n = load_w(w_mlp2[l, hg * 512:(hg + 1) * 512, :], 4, 1024, key="m2_%d_%d" % (l, hg))
            if hg + 1 < 8:
                prefetch("m1_%d_%d" % (l, hg + 1), w_mlp1[l, :, (hg + 1) * 512:(hg + 2) * 512], 8, 512)
                prefetch("m2_%d_%d" % (l, hg + 1), w_mlp2[l, (hg + 1) * 512:(hg + 2) * 512, :], 4, 1024)
            elif l + 1 < DEPTH:
                prefetch("ret%d" % (l + 1), w_in[l + 1, :, 0:1024], 8, 1024)
            fb = brT[:, (hg % 2) * 4:(hg % 2) * 4 + 4, :]
            fbn = "fb%d" % (hg % 2)
            for fc in range(4):
                for g in range(3):
                    def ev_f(b, bn, fc=fc, g=g, fb=fb, fbn=fbn):
                        act(tmpf[0][:], b[:], AF.Relu, [bn], ["tmpf0"])
                        tt("dve", fb[:, fc, g * 512:(g + 1) * 512], tmpf[0][:], tmpf[0][:], ALU.mult, ["tmpf0"], [(fbn, (fc, g))])
                    proj_fm(w1, w1n, 8, fc * 128, hT, "hT", g, ev_f)
            for fo in range(8):
                for g in range(3):
                    def ev_o(b, bn, fo=fo, g=g, hg=hg):
                        sl = accT[:, fo, g * 512:(g + 1) * 512]
                        if hg == 0:
                            act(sl, b[:], AF.Copy, [bn], [("accT", (fo, g))])
                        else:
                            tt("dve", sl, b[:], sl, ALU.add, [bn, ("accT", (fo, g))], [("accT", (fo, g))])
                    proj_fm(w2, w2n, 4, fo * 128, fb, fbn, g, ev_o)
        resid_update(accT, "accT", 3)
        S.barrier()

    for l in range(DEPTH):
        layer(l)

    for ti in range(NTILE):
        sv, sn = stage(ti % 8)
        for half in range(2):
            b, bn = nb()
            for kk in range(4):
                k = half * 4 + kk
                tr(b[:, kk * 128:(kk + 1) * 128], xT[:, k, ti * 128:(ti + 1) * 128], identF[:], ["xT", "identF"], [bn])
            act(sv[:, half * 512:(half + 1) * 512], b[:], AF.Copy, [bn], [sn])
        dst = yp[ti // 2, (ti % 2) * 128:(ti % 2 + 1) * 128, :] if ti < 4 else ys[(ti - 4) * 128:(ti - 3) * 128, :]
        dma("sp", dst, sv, [sn], ["y_d"])
    S.final_wait("sp")
    S.emit(nc, st)
    st.close()
    return nc, consts


def core_inputs(inp, c, consts):
    b = c % 4
    f = lambda a: np.ascontiguousarray(a, dtype=np.float32)
    m = {
        "xp": f(inp["x_prompt"][2 * c:2 * c + 2]), "xs": f(inp["x_sample"][b]),
        "cv": f(np.concatenate([inp["c_ctx"].reshape(8, 128), inp["c"][b].reshape(8, 128)], 0)),
        "ck": f(inp["cache_na_k"][b].reshape(4, 256, 256)), "cvv": f(inp["cache_na_v"][b].reshape(4, 256, 256)),
        "sret0": f(inp["state_ret"][b]), "ss50": f(inp["state_s5"][b]), "sgla0": f(inp["state_gla"][b]),
        "w_ada": f(inp["w_ada"]), "b_ada": f(inp["b_ada"].reshape(192, 128)), "g_norm": f(inp["g_norm"].reshape(128, 128)),
        "w_in": f(inp["w_in"]), "ret_ld": f(inp["ret_log_decay"].reshape(32, 1)), "ret_gn": f(inp["ret_gn"].reshape(8, 128)),
        "s5_lre2": f(inp["s5_lambda_re"].reshape(64, 128)), "s5_lim2": f(inp["s5_lambda_im"].reshape(64, 128)),
        "s5_ldt2": f(np.broadcast_to(inp["s5_log_dt"][..., None], (4, 2, 16, 64)).reshape(64, 128)),
        "s5_bre": f(inp["s5_b_re"]), "s5_bim": f(inp["s5_b_im"]), "s5_cre": f(inp["s5_c_re"]), "s5_cim": f(inp["s5_c_im"]),
        "s5_d": f(inp["s5_d"].reshape(8, 128)), "s5_wglu": f(inp["s5_w_glu"]), "s5_bglu": f(inp["s5_b_glu"].reshape(16, 128)),
        "gla_wg": f(inp["gla_w_gate"]), "gla_bg": f(inp["gla_b_gate"]), "gla_gn": f(inp["gla_gn"].reshape(8, 128)),
        "na_rpb": f(inp["na_rpb"]), "w_branch": f(inp["w_branch"]), "w_merge": f(inp["w_merge"]),
        "b_merge": f(inp["b_merge"].reshape(128, 128)), "w_out": f(inp["w_out"]), "w_mlp1": f(inp["w_mlp1"]), "w_mlp2": f(inp["w_mlp2"]),
    }
    for k, v in consts.items():
        m["c_" + k] = v
    return m


_CACHE = {}


def kernel(**inp):
    inp = {k: np.asarray(v) for k, v in inp.items()}
    if "nc" not in _CACHE:
        _CACHE["nc"] = build(DEPTH=4)
    nc, consts = _CACHE["nc"]
    in_maps = [core_inputs(inp, c, consts) for c in range(8)]
    res = run_bass_kernel_spmd(nc, in_maps, core_ids=list(range(8))).results
    yp = np.concatenate([r["yp"] for r in res], 0).astype(np.float32)
    ys = np.stack([res[b]["ys"] for b in range(4)], 0).astype(np.float32)
    nk = np.concatenate([r["onk"] for r in res], 0).reshape(16, 4, 256, 4, 64).astype(np.float32)
    nv = np.concatenate([r["onv"] for r in res], 0).reshape(16, 4, 256, 4, 64).astype(np.float32)
    sret = np.concatenate([r["osret"] for r in res], 0).astype(np.float32)
    ss5 = np.concatenate([r["oss5"] for r in res], 0).astype(np.float32)
    sgla = np.concatenate([r["osgla"] for r in res], 0).astype(np.float32)
    return (yp, ys, nk, nv, sret, ss5, sgla)
```

```python
from concourse.bass_utils import run_bass_kernel_spmd
import numpy as np
import concourse.bass as bass
import concourse.mybir as mybir
from concourse.ap import AP

F32 = mybir.dt.float32
BF16 = mybir.dt.bfloat16
ALU = mybir.AluOpType
AF = mybir.ActivationFunctionType

COMPUTE = ["pe", "act", "dve", "pool", "sp"]
NDMA = 24
SEM_ROT = 4
SAME_GAP = 10 ** 9


class Sched:
    def __init__(self, same_engine_sync=True):
        self.engs = COMPUTE + ["d%d" % i for i in range(NDMA)]
        self.ops = {e: [] for e in COMPUTE}
        self.count = {e: 0 for e in self.engs}
        self.seen = {e: {} for e in self.engs}
        self.snap = {e: [None] for e in self.engs}
        self.last_w = {}
        self.readers = {}
        self.dma_rr = 0
        self.same = same_engine_sync
        self.nwaits = 0

    def _conf_keys(self, res):
        name, sub = res
        if sub is None:
            return [k for k in self._names.get(name, ())]
        return [(name, sub), (name, None)]

    _names = None

    def _deps(self, eng, reads, writes):
        if self._names is None:
            self._names = {}
        deps = {}

        def add(e, i):
            if i > deps.get(e, 0):
                deps[e] = i

        for r in reads:
            for k in self._conf_keys(r):
                lw = self.last_w.get(k)
                if lw:
                    add(*lw)
        for w in writes:
            for k in self._conf_keys(w):
                lw = self.last_w.get(k)
                if lw:
                    add(*lw)
                for e, i in self.readers.get(k, {}).items():
                    add(e, i)
        return deps

    def _register(self, who, reads, writes):
        e, i = who
        for r in reads:
            self._names.setdefault(r[0], set()).add(r)
            self.readers.setdefault(r, {})[e] = i
        for w in writes:
            self._names.setdefault(w[0], set()).add(w)
            self.last_w[w] = (e, i)
            self.readers[w] = {}
            if w[1] is None:
                for k in self._names[w[0]]:
                    if k != w:
                        self.last_w.pop(k, None)
                        self.readers.pop(k, None)
                        self.last_w[k] = (e, i)
                        self.readers[k] = {}

    def _waits(self, eng, deps):
        waits = []
        seen = self.seen[eng]
        for e2, i2 in sorted(deps.items()):
            if e2 == eng and (not self.same or eng in ("sp", "pe")):
                continue
            if e2 == eng and self.count[eng] - i2 >= SAME_GAP:
                continue
            if seen.get(e2, 0) >= i2:
                continue
            waits.append((e2, i2))
        for e2, i2 in waits:
            seen[e2] = max(seen.get(e2, 0), i2)
            sn = self.snap[e2][i2] if i2 < len(self.snap[e2]) else None
            if sn:
                for k, v in sn.items():
                    if k != eng and seen.get(k, 0) < v:
                        seen[k] = v
        self.nwaits += len(waits)
        return waits

    @staticmethod
    def _norm(rs):
        out = []
        for r in rs:
            if isinstance(r, tuple):
                out.append(r)
            else:
                out.append((r, None))
        return out

    def op(self, eng, fn, reads=(), writes=()):
        reads = self._norm(reads)
        writes = self._norm(writes)
        deps = self._deps(eng, reads, writes)
        waits = self._waits(eng, deps)
        self.count[eng] += 1
        idx = self.count[eng]
        self.ops[eng].append(("op", fn, waits, idx))
        self.snap[eng].append(dict(self.seen[eng]))
        self._register((eng, idx), reads, writes)
        return idx

    def dma(self, queue, fn, reads=(), writes=()):
        reads = self._norm(reads)
        writes = self._norm(writes)
        lo, hi = {"pool": (0, 8), "sp": (8, 20), "act": (20, 24)}[queue]
        if not hasattr(self, "_rr"):
            self._rr = {}
        k = self._rr.get(queue, 0)
        self._rr[queue] = k + 1
        d = "d%d" % (lo + k % (hi - lo))
        deps = self._deps(queue, reads, writes)
        if self.count[d] > 0:
            deps[d] = max(deps.get(d, 0), self.count[d])
        waits = self._waits(queue, deps)
        self.count[d] += 1
        idx = self.count[d]
        self.ops[queue].append(("dma", fn, waits, (d, idx)))
        self.snap[d].append(dict(self.seen[queue]))
        self._register((d, idx), reads, writes)
        return (d, idx)

    def barrier(self):
        snapc = {e: c for e, c in self.count.items() if c > 0}
        for eng in COMPUTE:
            deps = {e: c for e, c in snapc.items() if e != eng}
            waits = self._waits(eng, deps)
            if waits:
                self.ops[eng].append(("wait", None, waits, None))

    def final_wait(self, eng="sp"):
        deps = {e: c for e, c in self.count.items() if c > 0 and e != eng}
        waits = self._waits(eng, deps)
        self.ops[eng].append(("wait", None, waits, None))

    def emit(self, nc, stack):
        sems = {}
        for e in COMPUTE:
            sems[e] = [stack.enter_context(nc.semaphore("s_%s%d" % (e, r))) for r in range(SEM_ROT)]
        for i in range(NDMA):
            sems["d%d" % i] = [stack.enter_context(nc.semaphore("s_d%d" % i))]

        def do_wait(engobj, e2, i2):
            if e2[1:].isdigit():
                engobj.wait_ge(sems[e2][0], 16 * i2)
            else:
                r = (i2 - 1) % SEM_ROT
                engobj.wait_ge(sems[e2][r], (i2 - 1) // SEM_ROT + 1)

        def run(engname, engobj):
            for kind, fn, waits, info in self.ops[engname]:
                for e2, i2 in waits:
                    do_wait(engobj, e2, i2)
                if kind == "op":
                    ins = fn(engobj)
                    ins.then_inc(sems[engname][(info - 1) % SEM_ROT], 1)
                elif kind == "dma":
                    ins = fn(engobj)
                    ins.then_inc(sems[info[0]][0], 16)

        block = stack.enter_context(nc.Block())

        @block.tensor
        def _(e):
            run("pe", e)

        @block.scalar
        def _(e):
            run("act", e)

        @block.vector
        def _(e):
            run("dve", e)

        @block.gpsimd
        def _(e):
            run("pool", e)

        @block.sync
        def _(e):
            run("sp", e)

import math
import numpy as np
from contextlib import ExitStack

D = 1024
T = 1536
NTILE = 12
EPS = 1e-6
NEG = -30000.0


def host_consts():
    c = {}
    c["identF"] = np.eye(128, dtype=np.float32)
    jj = np.arange(128)[:, None]
    ii = np.arange(128)[None, :]
    c["maskF"] = (jj <= ii).astype(np.float32)
    c["maskB"] = (jj >= ii).astype(np.float32)
    c["mavg"] = np.kron(np.eye(2, dtype=np.float32), np.full((64, 64), 1.0 / 64, np.float32))
    pos = np.zeros((128, 2, 128), np.float32)
    pos[:, 0, :] = np.arange(1, 129)[None, :]
    pos[:, 1, :] = (128 - np.arange(128))[None, :]
    c["posfb"] = pos
    gb = np.zeros((128, 128), np.float32)
    for idx in range(31):
        gb[idx, idx + 48] = 1.0
    c["gbig"] = gb
    c["dupI"] = np.concatenate([np.eye(64, dtype=np.float32)] * 2, axis=1)
    mn = np.full((128, 64), NEG, np.float32)
    for qc in range(64):
        cs = min(max(qc - 8, 0), 48)
        mn[cs:cs + 16, qc] = 0.0
        mn[64 + cs:64 + cs + 16, qc] = 0.0
    c["mneg"] = mn
    cm = np.zeros((128, 4, 128), np.float32)
    for glv in range(2):
        for q in range(4):
            gloc = 2 * q + glv
            cm[glv * 64:(glv + 1) * 64, q, gloc * 16:(gloc + 1) * 16] = 1.0
    c["cmask"] = cm
    hm = np.zeros((128, 4), np.float32)
    for h in range(4):
        hm[h * 32:(h + 1) * 32, h] = 1.0
    c["hmask"] = hm
    t = np.arange(1024)
    row = (t // 64).astype(np.float32)
    col = (t % 64).astype(np.float32)
    inv = (10000.0 ** (-np.arange(16, dtype=np.float32) / 16)).astype(np.float32)
    C = np.zeros((128, 1024), np.float32)
    Sg = np.zeros((128, 1024), np.float32)
    for p in range(128):
        d = p % 64
        half = d // 32
        j = d % 16
        blk = (d % 32) // 16
        posv = row if half == 0 else col
        ang = (posv * inv[j]).astype(np.float32)
        C[p] = np.cos(ang)
        Sg[p] = np.sin(ang) * (-1.0 if blk == 0 else 1.0)
    c["ropeC"] = C
    c["ropeS"] = Sg
    return c


def build(DEPTH=4, mixers=("ret", "s5", "gla", "na"), dbg=False):
    nc = bass.Bass("TRN2", target_bir_lowering=False)
    din = lambda name, shape: nc.dram_tensor(name, list(shape), F32, kind="ExternalInput").ap()
    dout = lambda name, shape: nc.dram_tensor(name, list(shape), F32, kind="ExternalOutput").ap()
    xp = din("xp", [2, 256, 1024]); xs = din("xs", [1024, 1024]); cv = din("cv", [16, 128])
    ck = din("ck", [4, 256, 256]); cvv = din("cvv", [4, 256, 256])
    sret0 = din("sret0", [4, 2, 4, 64, 64]); ss50 = din("ss50", [4, 2, 16, 64, 2]); sgla0 = din("sgla0", [4, 2, 4, 32, 64])
    w_ada = din("w_ada", [4, 1024, 6144]); b_ada = din("b_ada", [192, 128]); g_norm = din("g_norm", [128, 128])
    w_in = din("w_in", [4, 1024, 2848]); ret_ld = din("ret_ld", [32, 1]); ret_gn = din("ret_gn", [8, 128])
    s5_lre2 = din("s5_lre2", [64, 128]); s5_lim2 = din("s5_lim2", [64, 128]); s5_ldt2 = din("s5_ldt2", [64, 128])
    s5_bre = din("s5_bre", [4, 2, 16, 64, 16]); s5_bim = din("s5_bim", [4, 2, 16, 64, 16])
    s5_cre = din("s5_cre", [4, 2, 16, 16, 64]); s5_cim = din("s5_cim", [4, 2, 16, 16, 64])
    s5_d = din("s5_d", [8, 128]); s5_wglu = din("s5_wglu", [4, 256, 512]); s5_bglu = din("s5_bglu", [16, 128])
    gla_wg = din("gla_wg", [4, 2, 16, 128]); gla_bg = din("gla_bg", [4, 2, 128]); gla_gn = din("gla_gn", [8, 128])
    na_rpb = din("na_rpb", [4, 4, 15, 31])
    w_branch = din("w_branch", [4, 4, 256, 1024]); w_merge = din("w_merge", [4, 1024, 4096]); b_merge = din("b_merge", [128, 128])
    w_out = din("w_out", [4, 1024, 1024]); w_mlp1 = din("w_mlp1", [4, 1024, 4096]); w_mlp2 = din("w_mlp2", [4, 4096, 1024])
    consts = host_consts()
    cd = {k: din("c_" + k, v.shape) for k, v in consts.items()}
    yp = dout("yp", [2, 256, 1024]); ys = dout("ys", [1024, 1024])
    onk = dout("onk", [2, 4, 256, 256]); onv = dout("onv", [2, 4, 256, 256])
    osret = dout("osret", [2, 4, 2, 4, 64, 64]); oss5 = dout("oss5", [2, 4, 2, 16, 64, 2]); osgla = dout("osgla", [2, 4, 2, 4, 32, 64])
    bt_scr = nc.dram_tensor("bt_scr", [4, 15, 64, 64], F32, kind="Internal").ap()

    S = Sched()
    st = ExitStack()
    sbt = lambda n, s, d=F32: st.enter_context(nc.sbuf_tensor(n, list(s), d))
    xT = sbt("xT", [128, 8, T])
    hT = sbt("hT", [128, 8, T], BF16)
    accT = sbt("accT", [128, 8, T])
    brT = sbt("brT", [128, 8, T], BF16)
    wbuf = [sbt("wb%d" % i, [128, 8192], BF16) for i in range(2)]
    rstd = sbt("rstd", [128, T])
    tmpf = [sbt("tmpf%d" % i, [128, 512]) for i in range(2)]
    sqb = [sbt("sqb%d" % i, [128, 512], BF16) for i in range(2)]
    identF = sbt("identF", [128, 128]); identB = sbt("identB", [128, 128], BF16)
    onesB = sbt("onesB", [128, 128], BF16)
    maskF = sbt("maskF", [128, 128], BF16); maskB = sbt("maskB", [128, 128], BF16)
    mavgB = sbt("mavgB", [128, 128], BF16)
    gnT = sbt("gnT", [128, 128]); badaT = sbt("badaT", [128, 192]); bmT = sbt("bmT", [128, 128])
    retgnT = sbt("retgnT", [128, 8]); glagnT = sbt("glagnT", [128, 8]); s5dT = sbt("s5dT", [128, 8]); bgluT = sbt("bgluT", [128, 16])
    cT = sbt("cT", [128, 16]); scT = sbt("scT", [128, 8, 2], BF16)
    modT2 = sbt("modT", [128, 2, 48, 2]); dsc2 = sbt("dsc", [128, 2, 4, 8, 2])
    curl = [0]
    rowst = sbt("rowst", [128, 128])
    cln8 = sbt("cln8", [128, 1])
    dupI = sbt("dupI", [64, 128], BF16); gbig = sbt("gbig", [128, 128]); mneg = sbt("mneg", [128, 64]); Rrp = sbt("Rrp", [32, 60])
    pbank = [st.enter_context(nc.psum_tensor("pb%d" % i, [128, 512], F32)) for i in range(8)]
    bctr = [0]

    def nb():
        i = bctr[0] % 6
        bctr[0] += 1
        return pbank[i], "pb%d" % i

    wslot = [0]
    pre = {}

    def alloc_w(nel):
        if nel <= 4096:
            s = wslot[0] % 4
            wslot[0] += 1
            return wbuf[s // 2], (s % 2) * 4096, ("wb%d" % (s // 2), s % 2)
        if wslot[0] % 2:
            wslot[0] += 1
        s = wslot[0] % 4
        wslot[0] += 2
        return wbuf[s // 2], 0, "wb%d" % (s // 2)

    def load_w(src2d, kc, ncols, key=None, queue="pool"):
        if key is not None and key in pre:
            return pre.pop(key)
        buf, off, name = alloc_w(kc * ncols)
        view = buf[:, off:off + kc * ncols].rearrange("p (k n) -> p k n", k=kc)
        S.dma(queue, lambda e: e.dma_start(out=view, in_=src2d.rearrange("(k p) n -> p k n", p=128)), [], [name])
        return view, name

    def align_w():
        if wslot[0] % 2:
            wslot[0] += 1

    def prefetch(key, src2d, kc, ncols):
        if key not in pre:
            pre[key] = load_w(src2d, kc, ncols)

    def dma(q, out, in_, r, w):
        S.dma(q, lambda e: e.dma_start(out=out, in_=in_), r, w)

    def mm(out, lhsT, rhs, start, stop, r, w):
        S.op("pe", lambda e: e.matmul(out, lhsT, rhs, start=start, stop=stop), r, w)

    def tr(out, in_, ident, r, w):
        S.op("pe", lambda e: e.transpose(out, in_, ident), r, w)

    def act(out, in_, func, r, w, bias=None, scale=None):
        kw = {}
        if bias is not None:
            kw["bias"] = bias
        if scale is not None:
            kw["scale"] = scale
        S.op("act", lambda e: e.activation(out=out, in_=in_, func=func, **kw), r, w)

    def tt(eng, out, in0, in1, op, r, w):
        S.op(eng, lambda e: e.tensor_tensor(out=out, in0=in0, in1=in1, op=op), r, w)

    def ts(eng, out, in0, s1, s2, op0, op1, r, w):
        if s2 is None:
            S.op(eng, lambda e: e.tensor_scalar(out=out, in0=in0, scalar1=s1, scalar2=None, op0=op0), r, w)
        else:
            S.op(eng, lambda e: e.tensor_scalar(out=out, in0=in0, scalar1=s1, scalar2=s2, op0=op0, op1=op1), r, w)

    def stt(eng, out, in0, scalar, in1, op0, op1, r, w):
        S.op(eng, lambda e: e.scalar_tensor_tensor(out=out, in0=in0, scalar=scalar, in1=in1, op0=op0, op1=op1), r, w)

    def bcast_last(ap2, n):
        a = ap2.ap
        return AP(ap2.tensor, ap2.offset, [list(a[0]), list(a[1]), [0, n]])

    def bcast_mid(ap2, n):
        a = ap2.ap
        return AP(ap2.tensor, ap2.offset, [list(a[0]), [0, n], list(a[1])])

    dma("sp", identF[:], cd["identF"][:, :], [], ["identF"])
    dma("pool", identB[:], cd["identF"][:, :], [], ["identB"])
    dma("pool", maskF[:], cd["maskF"][:, :], [], ["maskF"])
    dma("pool", maskB[:], cd["maskB"][:, :], [], ["maskB"])
    dma("pool", mavgB[:], cd["mavg"][:, :], [], ["mavgB"])
    S.op("dve", lambda e: e.memset(onesB[:], 1.0), [], ["onesB"])
    S.op("dve", lambda e: e.memset(cln8[:], math.log(0.125)), [], ["cln8"])
    dma("sp", gbig[:], cd["gbig"][:, :], [], ["gbig"])
    dma("pool", dupI[:], cd["dupI"][:, :], [], ["dupI"])
    dma("sp", mneg[:], cd["mneg"][:, :], [], ["mneg"])

    def rows_to_cols(src, R, dst, dname):
        dma("sp", rowst[0:R, :], src, [], ["rowst"])
        b, bn = nb()
        tr(b[:, 0:R], rowst[0:R, :], identF[0:R, 0:R], ["rowst", "identF"], [bn])
        act(dst, b[:, 0:R], AF.Copy, [bn], [dname])

    rows_to_cols(g_norm[:, :], 128, gnT[:], "gnT")
    rows_to_cols(b_ada[0:128, :], 128, badaT[:, 0:128], "badaT")
    rows_to_cols(b_ada[128:192, :], 64, badaT[:, 128:192], "badaT")
    rows_to_cols(b_merge[:, :], 128, bmT[:], "bmT")
    rows_to_cols(ret_gn[:, :], 8, retgnT[:], "retgnT")
    rows_to_cols(gla_gn[:, :], 8, glagnT[:], "glagnT")
    rows_to_cols(s5_d[:, :], 8, s5dT[:], "s5dT")
    rows_to_cols(s5_bglu[:, :], 16, bgluT[:], "bgluT")
    rows_to_cols(cv[:, :], 16, cT[:], "cT")
    act(scT[:, :, 0], cT[:, 0:8], AF.Silu, ["cT"], ["scT"])
    act(scT[:, :, 1], cT[:, 8:16], AF.Silu, ["cT"], ["scT"])

    accS = accT[:].rearrange("p k t -> p (k t)")

    def stage(j):
        return accS[:, j * 1024:(j + 1) * 1024], ("accT", "st%d" % (j,))

    for ti in range(NTILE):
        sv, sn = stage(ti % 8)
        src = xp[ti // 2, (ti % 2) * 128:(ti % 2 + 1) * 128, :] if ti < 4 else xs[(ti - 4) * 128:(ti - 3) * 128, :]
        dma("sp", sv, src, [], [sn])
        for half in range(2):
            b, bn = nb()
            for kk in range(4):
                k = half * 4 + kk
                tr(b[:, kk * 128:(kk + 1) * 128], sv[:, k * 128:(k + 1) * 128], identF[:], [sn, "identF"], [bn])
            act(xT[:, half * 4:(half + 1) * 4, ti * 128:(ti + 1) * 128], b[:].rearrange("p (a b) -> p a b", a=4), AF.Copy,
                [bn], [("xT", ti)])
    S.barrier()

    def gpath(g):
        return 0 if g == 0 else 1

    def mod_src(l, cb):
        return w_ada[l, :, cb * 1024:(cb + 1) * 1024]

    def mod_mm(l, cb, bank=None):
        wv, wn = load_w(mod_src(l, cb), 8, 1024, key="ada%d_%d" % (l, cb))
        b, bn = bank if bank is not None else nb()
        for qq in range(8):
            for k in range(8):
                mm(b[:, qq * 2:(qq + 1) * 2], wv[:, k, qq * 128:(qq + 1) * 128], scT[:, k, :], k == 0, k == 7, [wn, "scT"], [bn])
        return b, bn

    def mod_evac(l, cb, b, bn):
        modT = modT2[:, l % 2]
        mN = "modT%d" % (l % 2)
        tt("dve", modT[:, cb * 8:(cb + 1) * 8, :], b[:, 0:16].rearrange("p (a b) -> p a b", a=8),
           bcast_last(badaT[:, l * 48 + cb * 8: l * 48 + cb * 8 + 8], 2), ALU.add, [bn, "badaT"], [mN])

    def mod_finish(l):
        modT = modT2[:, l % 2]
        dsc = dsc2[:, l % 2]
        mN = "modT%d" % (l % 2)
        dN = "dsc%d" % (l % 2)
        gn = lambda n: bcast_last(gnT[:, (l * 4 + n) * 8:(l * 4 + n) * 8 + 8], 2)
        stt("dve", dsc[:, 0], modT[:, 8:16, :], 1.0, gn(0), ALU.add, ALU.mult, [mN, "gnT"], [dN])
        tt("dve", dsc[:, 1], modT[:, 16:24, :], gn(1), ALU.mult, [mN, "gnT"], [dN])
        stt("dve", dsc[:, 2], modT[:, 32:40, :], 1.0, gn(2), ALU.add, ALU.mult, [mN, "gnT"], [dN])
        tt("dve", dsc[:, 3], modT[:, 40:48, :], gn(3), ALU.mult, [mN, "gnT"], [dN])

    def emit_mod(l):
        for cb in range(6):
            b, bn = mod_mm(l, cb)
            mod_evac(l, cb, b, bn)
        mod_finish(l)

    def norm_stats(src, sname):
        for g in range(3):
            b, bn = nb()
            for k in range(8):
                q = sqb[k % 2]
                act(q[:], src[:, k, g * 512:(g + 1) * 512], AF.Square, [sname], ["sqb%d" % (k % 2)])
                mm(b[:], onesB[:], q[:], k == 0, k == 7, ["onesB", "sqb%d" % (k % 2)], [bn])
            act(tmpf[0][:], b[:], AF.Ln, [bn], ["tmpf0"], bias=EPS, scale=1.0 / 1024)
            act(rstd[:, g * 512:(g + 1) * 512], tmpf[0][:], AF.Exp, ["tmpf0"], [("rstd", g)], scale=-0.5)

    def norm_to_h(src, sname, ai, bq):
        norm_stats(src, sname)
        modT = modT2[:, curl[0] % 2]
        dsc = dsc2[:, curl[0] % 2]
        mN = "modT%d" % (curl[0] % 2)
        dN = "dsc%d" % (curl[0] % 2)
        for g in range(3):
            p = gpath(g)
            for k in range(8):
                tf = tmpf[k % 2]
                tt("dve", tf[:], src[:, k, g * 512:(g + 1) * 512], rstd[:, g * 512:(g + 1) * 512], ALU.mult,
                   [sname, ("rstd", g)], ["tmpf%d" % (k % 2)])
                act(hT[:, k, g * 512:(g + 1) * 512], tf[:], AF.Identity, ["tmpf%d" % (k % 2), dN, mN], [("hT", g)],
                    bias=modT[:, bq + k, p:p + 1], scale=dsc[:, ai, k, p:p + 1])

    def resid_update(src, sname, ai):
        norm_stats(src, sname)
        modT = modT2[:, curl[0] % 2]
        dsc = dsc2[:, curl[0] % 2]
        mN = "modT%d" % (curl[0] % 2)
        dN = "dsc%d" % (curl[0] % 2)
        for g in range(3):
            p = gpath(g)
            for k in range(8):
                tf = tmpf[k % 2]
                tt("dve", tf[:], src[:, k, g * 512:(g + 1) * 512], rstd[:, g * 512:(g + 1) * 512], ALU.mult,
                   [sname, ("rstd", g)], ["tmpf%d" % (k % 2)])
                stt("dve", xT[:, k, g * 512:(g + 1) * 512], tf[:], dsc[:, ai, k, p:p + 1], xT[:, k, g * 512:(g + 1) * 512],
                    ALU.mult, ALU.add, ["tmpf%d" % (k % 2), dN, "xT"], ["xT"])

    def proj_fm(wv, wn, kc, c0, src, sname, g, evac):
        b, bn = nb()
        for k in range(kc):
            mm(b[:], wv[:, k, c0:c0 + 128], src[:, k, g * 512:(g + 1) * 512], k == 0, k == kc - 1, [wn, sname], [bn])
        evac(b, bn)

    def proj_tm(wv, wn, kc, c0, ncols, src, sname, ti, evac):
        b, bn = nb()
        for k in range(kc):
            mm(b[:, 0:ncols], src[:, k, ti * 128:(ti + 1) * 128], wv[:, k, c0:c0 + ncols], k == 0, k == kc - 1, [wn, sname], [bn])
        evac(b, bn)

    def carve(off, n, dtype=F32):
        a = accS[:, off:off + n]
        if dtype == BF16:
            a = a.bitcast(BF16)
        return a

    rotb = sbt("rotb", [128, 16, 128], BF16)
    rotf = sbt("rotf", [128, 8, 128])
    lgc = sbt("lgc", [128, 4, 2, 2])
    nlgc = sbt("nlgc", [128, 4, 2, 2])
    eend = sbt("eend", [128, 2, 2])
    posfb = sbt("posfb", [128, 2, 128])
    hmask = sbt("hmask", [128, 4])
    dma("sp", posfb[:], cd["posfb"][:, :, :], [], ["posfb"])
    dma("sp", hmask[:], cd["hmask"][:, :], [], ["hmask"])
    for l_ in range(4):
        for d_ in range(2):
            for h_ in range(4):
                e0 = ret_ld[(l_ * 2 + d_) * 4 + h_:(l_ * 2 + d_) * 4 + h_ + 1, 0:1]
                src = AP(e0.tensor, e0.offset, [[0, 64], [1, 1]])
                dma("sp", lgc[(h_ % 2) * 64:(h_ % 2) * 64 + 64, l_, d_, h_ // 2:h_ // 2 + 1], src, [], ["lgc"])
    ts("dve", nlgc[:].rearrange("p a b c -> p (a b c)"), lgc[:].rearrange("p a b c -> p (a b c)"), -1.0, None, ALU.mult, None, ["lgc"], ["nlgc"])

    def head_norm(oG, oGn, j, g, gcol, gate, gaten, outchunk):
        osb = oG[:, j, :]
        b1, b1n = nb()
        mm(b1[:], mavgB[:], osb, True, True, ["mavgB", oGn], [b1n])
        tt("dve", tmpf[0][:], osb, b1[:], ALU.subtract, [oGn, b1n], ["tmpf0"])
        sqv = tmpf[1][:].bitcast(BF16)[:, 0:512]
        act(sqv, tmpf[0][:], AF.Square, ["tmpf0"], ["tmpf1"])
        b2, b2n = nb()
        mm(b2[:], mavgB[:], sqv, True, True, ["mavgB", "tmpf1"], [b2n])
        act(rstd[:, 0:512], b2[:], AF.Ln, [b2n], [("rstd", 0)], bias=EPS, scale=1.0)
        act(rstd[:, 0:512], rstd[:, 0:512], AF.Exp, [("rstd", 0)], [("rstd", 0)], scale=-0.5)
        tt("dve", tmpf[0][:], tmpf[0][:], rstd[:, 0:512], ALU.mult, ["tmpf0", ("rstd", 0)], ["tmpf0"])
        stt("dve", brT[:, outchunk, g * 512:(g + 1) * 512], tmpf[0][:], gcol, gate[:, j, g * 512:(g + 1) * 512], ALU.mult, ALU.mult,
            ["tmpf0", gaten], [("brT", outchunk)])

    SEQS = [(0, 2, 0), (2, 2, 0), (4, 8, 1)]

    def mixer_ret(l):
        rq = carve(0, 1536, BF16).rearrange("p (j t) -> p j t", j=2)
        rk = carve(1536, 1536, BF16).rearrange("p (j t) -> p j t", j=2)
        rg = carve(3072, 1536, BF16).rearrange("p (j t) -> p j t", j=2)
        rvpad = carve(4608, 3072, BF16).rearrange("p (a h c) -> p a h c", a=12, h=4)
        Spad = carve(7680, 2304, BF16).rearrange("p (d c j x) -> p d c j x", d=2, c=9, j=2)
        Srun = carve(9984, 256, F32).rearrange("p (d j x) -> p d j x", d=2, j=2)
        U = carve(10240, 128, F32).rearrange("p (j x) -> p j x", j=2)
        oG = carve(10496, 512, BF16).rearrange("p (j x) -> p j x", j=2)
        Eq = carve(11520, 256, BF16).rearrange("p (d j x) -> p d j x", d=2, j=2)
        Ek = carve(11776, 256, BF16).rearrange("p (d j x) -> p d j x", d=2, j=2)
        rqs = brT[:, 2:4, :]
        rks = brT[:, 4:6, :]
        ropeC = brT[:, 6, 0:1024]
        ropeS = brT[:, 7, 0:1024]
        dma("pool", ropeC, cd["ropeC"][:, :], [], ["ropeC"])
        dma("pool", ropeS, cd["ropeS"][:, :], [], ["ropeS"])
        S.op("pool", lambda e: e.memset(rvpad, 0.0), [], ["rvpad"])
        S.op("pool", lambda e: e.memset(Spad, 0.0), [], ["Spad"])
        for d in range(2):
            for j in range(2):
                act(Eq[:, d, j, :], posfb[:, d, :], AF.Exp, ["posfb", "lgc"], ["Eq"], scale=lgc[:, l, d, j:j + 1])
                act(Ek[:, d, j, :], posfb[:, d, :], AF.Exp, ["posfb", "nlgc"], ["Ek"], scale=nlgc[:, l, d, j:j + 1], bias=cln8[:, 0:1])
        act(eend[:].rearrange("p a b -> p (a b)"), lgc[:, l].rearrange("p a b -> p (a b)"), AF.Exp, ["lgc"], ["eend"], scale=128.0)
        wv, wn = load_w(w_in[l, :, 0:1024], 8, 1024, key="ret%d" % l)
        for g in range(3):
            for j in range(2):
                proj_fm(wv, wn, 8, j * 128, hT, "hT", g, lambda b, bn, j=j, g=g: act(rq[:, j, g * 512:(g + 1) * 512], b[:], AF.Copy, [bn], ["rq"]))
                proj_fm(wv, wn, 8, 256 + j * 128, hT, "hT", g, lambda b, bn, j=j, g=g: act(rk[:, j, g * 512:(g + 1) * 512], b[:], AF.Copy, [bn], ["rk"]))
                proj_fm(wv, wn, 8, 768 + j * 128, hT, "hT", g, lambda b, bn, j=j, g=g: act(rg[:, j, g * 512:(g + 1) * 512], b[:], AF.Silu, [bn], ["rg"]))
        for ti in range(NTILE):
            def ev_v(b, bn, ti=ti):
                for h in range(4):
                    act(rvpad[:, ti, h, (h % 2) * 64:(h % 2) * 64 + 64], b[:, h * 64:(h + 1) * 64], AF.Copy, [bn], ["rvpad"])
            proj_tm(wv, wn, 8, 512, 256, hT, "hT", ti, ev_v)
        prefetch("s5%d" % l, w_in[l, :, 1024:1280], 8, 256)
        sbuf_, soff_, swn = alloc_w(4096)
        swv = sbuf_[:, soff_:soff_ + 4096].rearrange("p (k n) -> p k n", k=8)
        for blk in range(2):
            dstv = swv.rearrange("p k (m b x) -> p k m b x", b=2, x=16)[:, :, :, blk, :]
            srcv = wv[:, :, 0:512].rearrange("p k (m b x) -> p k m b x", b=2, x=16)[:, :, :, 1 - blk, :]
            S.op("act", lambda e, dstv=dstv, srcv=srcv: e.activation(out=dstv, in_=srcv, func=AF.Copy), [wn], [swn])
        for g in (1, 2):
            for j in range(2):
                proj_fm(swv, swn, 8, j * 128, hT, "hT", g, lambda b, bn, j=j, g=g: act(rqs[:, j, g * 512:(g + 1) * 512], b[:], AF.Copy, [bn], ["rqs"]))
                proj_fm(swv, swn, 8, 256 + j * 128, hT, "hT", g, lambda b, bn, j=j, g=g: act(rks[:, j, g * 512:(g + 1) * 512], b[:], AF.Copy, [bn], ["rks"]))
        for (r_, rn, s_, sn_) in ((rq, "rq", rqs, "rqs"), (rk, "rk", rks, "rks")):
            for j in range(2):
                tt("dve", r_[:, j, 512:1536], r_[:, j, 512:1536], ropeC, ALU.mult, [rn, "ropeC"], [rn])
                tt("pool", s_[:, j, 512:1536], s_[:, j, 512:1536], ropeS, ALU.mult, [sn_, "ropeS"], [sn_])
                tt("dve", r_[:, j, 512:1536], r_[:, j, 512:1536], s_[:, j, 512:1536], ALU.add, [rn, sn_], [rn])
        S.barrier()
        if not all(m in mixers for m in ("s5", "gla")):
            for n in range(2, 6):
                S.op("pool", lambda e, n=n: e.memset(brT[:, n, :], 0.0), [], [("brT", n)])
        rc = [0]

        def rb():
            i = rc[0] % 16
            rc[0] += 1
            return rotb[:, i, :], ("rotb", i)

        fc = [0]

        def rf():
            i = fc[0] % 4
            fc[0] += 1
            return rotf[:, i, :], ("rotf", i)

        for si, (t0, n, latent) in enumerate(SEQS):
            for d in range(2):
                for j in range(2):
                    for hl in range(2):
                        h = 2 * j + hl
                        if latent:
                            dma("sp", Srun[hl * 64:(hl + 1) * 64, d, j, :], sret0[l, d, h, :, :], [], ["Srun"])
                        else:
                            S.op("dve", lambda e, hl=hl, d=d, j=j: e.memset(Srun[hl * 64:(hl + 1) * 64, d, j, :], 0.0), [], ["Srun"])
                order = list(range(n)) if d == 0 else list(range(n - 1, -1, -1))
                kts = {}

                def p1_front(c, d=d, kts=kts):
                    ti = t0 + c
                    for j in range(2):
                        kf, kfn = rf()
                        tt("dve", kf, rk[:, j, ti * 128:(ti + 1) * 128], Ek[:, d, j, :], ALU.mult, ["rk", "Ek"], [kfn])
                        b, bn = nb()
                        tr(b[:, 0:128], kf, identF[:], [kfn, "identF"], [bn])
                        kt, ktn = rb()
                        act(kt, b[:, 0:128], AF.Copy, [bn], [ktn])
                        kts[(c, j)] = (kt, ktn)

                def p1_back(c, d=d, kts=kts):
                    ti = t0 + c
                    for hl in range(2):
                        S.op("dve", lambda e, hl=hl, d=d, c=c: e.tensor_copy(out=Spad[hl * 64:(hl + 1) * 64, d, c, :, hl * 64:(hl + 1) * 64],
                                                                            in_=Srun[hl * 64:(hl + 1) * 64, d, :, :]), ["Srun"], ["Spad"])
                    b2, b2n = nb()
                    for j in range(2):
                        kt, ktn = kts.pop((c, j))
                        for hl in range(2):
                            h = 2 * j + hl
                            mm(b2[:, j * 128 + hl * 64:j * 128 + (hl + 1) * 64], kt, rvpad[:, ti, h, hl * 64:(hl + 1) * 64], True, True, [ktn, "rvpad"], [b2n])
                    for hl in range(2):
                        ps = slice(hl * 64, (hl + 1) * 64)
                        uv = b2[ps, hl * 64:hl * 64 + 64]
                        uview = AP(uv.tensor, uv.offset, [list(uv.ap[0]), [128, 2], [1, 64]])
                        tt("dve", Srun[ps, d], Srun[ps, d], uview, ALU.add, ["Srun", b2n], ["Srun"])
                    tt("dve", Srun[:, d], Srun[:, d], bcast_last(eend[:, d, :], 64), ALU.mult, ["Srun", "eend"], ["Srun"])

                p1_front(order[0])
                for oi, c in enumerate(order):
                    if oi + 1 < len(order):
                        p1_front(order[oi + 1])
                    p1_back(c)
                if not latent:
                    for j in range(2):
                        for hl in range(2):
                            dma("sp", osret[si, l, d, 2 * j + hl, :, :], Srun[hl * 64:(hl + 1) * 64, d, j, :], ["Srun"], ["osret_d"])
            p2u = [(c, j, d, hl) for c in range(n) for j in range(2) for d in range(2) for hl in range(2)]
            qk = {}
            ams = {}

            def p2_front(u):
                c, j, d, hl = u
                ti = t0 + c
                if hl == 0:
                    q_, qn_ = rb()
                    tt("dve", q_, rq[:, j, ti * 128:(ti + 1) * 128], Eq[:, d, j, :], ALU.mult, ["rq", "Eq"], [qn_])
                    k_, kn_ = rb()
                    tt("dve", k_, rk[:, j, ti * 128:(ti + 1) * 128], Ek[:, d, j, :], ALU.mult, ["rk", "Ek"], [kn_])
                    qk[(c, j, d)] = (q_, qn_, k_, kn_)
                q_, qn_, k_, kn_ = qk[(c, j, d)]
                b, bn = nb()
                mm(b[:, 0:128], k_[hl * 64:(hl + 1) * 64, :], q_[hl * 64:(hl + 1) * 64, :], True, True, [kn_, qn_], [bn])
                ams[u] = (b, bn)

            def p2_mid(u):
                c, j, d, hl = u
                b, bn = ams[u]
                a_, an_ = rb()
                tt("dve", a_, b[:, 0:128], (maskF if d == 0 else maskB)[:], ALU.mult, [bn, "maskF", "maskB"], [an_])
                ams[u] = (a_, an_)

            def p2_back(u):
                c, j, d, hl = u
                ti = t0 + c
                g = ti // 4
                cg = ti % 4
                ob, obn = (pbank[6], "pb6") if (c * 2 + j) % 2 == 0 else (pbank[7], "pb7")
                a_, an_ = ams.pop(u)
                q_, qn_, k_, kn_ = qk[(c, j, d)]
                h = 2 * j + hl
                idx = 2 * (2 * d + hl)
                mm(ob[:, 0:128], rvpad[:, ti, h, :], a_, idx == 0, False, ["rvpad", an_], [obn])
                mm(ob[:, 0:128], Spad[hl * 64:(hl + 1) * 64, d, c, j, :], q_[hl * 64:(hl + 1) * 64, :], False, idx + 1 == 7, ["Spad", qn_], [obn])
                if d == 1 and hl == 1:
                    act(oG[:, j, cg * 128:(cg + 1) * 128], ob[:, 0:128], AF.Copy, [obn], ["oG"])
                    if cg == 3 and j == 1:
                        for jj in range(2):
                            head_norm(oG, "oG", jj, g, retgnT[:, l * 2 + jj:l * 2 + jj + 1], rg, "rg", jj)

            for ui in range(len(p2u) + 2):
                if ui < len(p2u):
                    p2_front(p2u[ui])
                if 1 <= ui <= len(p2u):
                    p2_mid(p2u[ui - 1])
                if ui >= 2:
                    p2_back(p2u[ui - 2])

    wgf = sbt("wgf", [33, 256]); wgb = sbt("wgb", [33, 256], BF16); gee = sbt("gee", [128, 3, 4])

    def rev(ap2):
        n = ap2.shape[-1]
        a = ap2.ap
        return AP(ap2.tensor, ap2.offset + (n - 1) * a[-1][0], [list(a[0]), [-a[-1][0], n]])

    def mixer_gla(l):
        gq = carve(0, 768, BF16)
        gk = carve(768, 768, BF16)
        gg = carve(1536, 1536, BF16).rearrange("p (j t) -> p j t", j=2)
        gvpad = carve(3072, 3072, BF16).rearrange("p (a h c) -> p a h c", a=12, h=4)
        glrT = carve(6144, 768, BF16)
        Sp = carve(6912, 2304, BF16).rearrange("p (d c j x) -> p d c j x", d=2, c=9, j=2)
        Srun = carve(9216, 128, F32).rearrange("p (d x) -> p d x", d=2)
        U = carve(9344, 64, F32)
        oG = carve(9472, 512, BF16).rearrange("p (j x) -> p j x", j=2)
        S.op("pool", lambda e: e.memset(gvpad, 0.0), [], ["gvpad"])
        S.op("pool", lambda e: e.memset(Sp, 0.0), [], ["Sp"])
        S.op("dve", lambda e: e.memset(wgf[:], 0.0), [], ["wgf"])
        dma("sp", wgf[0:16, 0:128], gla_wg[l, 0, :, :], ["wgf"], ["wgf"])
        dma("sp", wgf[16:32, 128:256], gla_wg[l, 1, :, :], ["wgf"], ["wgf"])
        dma("sp", wgf[32:33, :], gla_bg[l:l + 1].rearrange("a d x -> a (d x)"), ["wgf"], ["wgf"])
        S.op("dve", lambda e: e.tensor_copy(out=wgb[:], in_=wgf[:]), ["wgf"], ["wgb"])
        S.op("dve", lambda e: e.memset(glrT[32:33, :], 1.0), [], ["glrT"])
        wv, wn = load_w(w_in[l, :, 1280:2080], 8, 800, key="gla%d" % l)
        sc_q = 32.0 ** -0.5
        for g in range(3):
            proj_fm(wv, wn, 8, 0, hT, "hT", g, lambda b, bn, g=g: act(gq[:, g * 512:(g + 1) * 512], b[:], AF.Copy, [bn], ["gq"], scale=sc_q))
            proj_fm(wv, wn, 8, 128, hT, "hT", g, lambda b, bn, g=g: act(gk[:, g * 512:(g + 1) * 512], b[:], AF.Copy, [bn], ["gk"]))
            for j in range(2):
                proj_fm(wv, wn, 8, 512 + j * 128, hT, "hT", g, lambda b, bn, j=j, g=g: act(gg[:, j, g * 512:(g + 1) * 512], b[:], AF.Silu, [bn], ["gg"]))
            b, bn = nb()
            for k in range(8):
                mm(b[0:32, :], wv[:, k, 768:800], hT[:, k, g * 512:(g + 1) * 512], k == 0, k == 7, [wn, "hT"], [bn])
            act(glrT[0:32, g * 512:(g + 1) * 512], b[0:32, :], AF.Copy, [bn], ["glrT"])
        for ti in range(NTILE):
            def ev_v(b, bn, ti=ti):
                for h in range(4):
                    act(gvpad[:, ti, h, (h % 2) * 64:(h % 2) * 64 + 64], b[:, h * 64:(h + 1) * 64], AF.Copy, [bn], ["gvpad"])
            proj_tm(wv, wn, 8, 256, 256, hT, "hT", ti, ev_v)
        prefetch("na%d" % l, w_in[l, :, 2080:2848], 8, 768)
        rc = [0]

        def rb():
            i = rc[0] % 16
            rc[0] += 1
            return rotb[:, i, :], ("rotb", i)

        fc = [0]

        def rf():
            i = fc[0] % 8
            fc[0] += 1
            return rotf[:, i, :], ("rotf", i)

        gtb = carve(10496, 1536, BF16).rearrange("p (s c x) -> p s c x", s=3, c=2)
        onesR = carve(12032, 256, BF16)
        S.op("dve", lambda e: e.memset(onesR, 1.0), [], ["onesR"])
        S.op("dve", lambda e: e.memset(onesR.rearrange("p (c x) -> p c x", c=4)[:, :, 0:1], 0.0), ["onesR"], ["onesR"])
        gcache = {}
        gslots = [None, None, None]
        gctr = [0]

        def gtab(g, d):
            if (g, d) in gcache:
                return gcache[(g, d)]
            s = gctr[0] % 3
            gctr[0] += 1
            if gslots[s] is not None:
                del gcache[gslots[s]]
            gslots[s] = (g, d)
            nm = ("gtb", s)
            b, bn = nb()
            mm(b[:], wgb[0:33, d * 128:(d + 1) * 128], glrT[0:33, g * 512:(g + 1) * 512], True, True, ["wgb", "glrT"], [bn])
            sp = tmpf[0][:]
            cs = tmpf[1][:]
            act(sp, b[:], AF.Exp, [bn], ["tmpf0"], scale=-1.0)
            act(sp, sp, AF.Ln, ["tmpf0"], ["tmpf0"], bias=1.0)
            if d == 0:
                S.op("dve", lambda e: e.tensor_tensor_scan(out=cs, data0=onesR, data1=sp, initial=0.0, op0=ALU.mult, op1=ALU.add), ["onesR", "tmpf0"], ["tmpf1"])
                ends = tmpf[1][:].rearrange("p (c x) -> p c x", c=4)[:, :, 127]
            else:
                S.op("dve", lambda e: e.tensor_tensor_scan(out=rev(cs), data0=onesR, data1=rev(sp), initial=0.0, op0=ALU.mult, op1=ALU.add), ["onesR", "tmpf0"], ["tmpf1"])
                ends = tmpf[1][:].rearrange("p (c x) -> p c x", c=4)[:, :, 0]
            act(gtb[:, s, 0, :], cs, AF.Exp, ["tmpf1"], [nm], scale=-1.0 / 16)
            act(gtb[:, s, 1, :], cs, AF.Exp, ["tmpf1"], [nm], scale=1.0 / 16)
            act(gee[:, s, :], ends, AF.Exp, ["tmpf1"], [nm], scale=-1.0 / 16)
            gcache[(g, d)] = (s, nm)
            return gcache[(g, d)]

        def tables(ti, d):
            s, nm = gtab(ti // 4, d)
            cg_ = ti % 4
            return (gtb[:, s, 0, cg_ * 128:(cg_ + 1) * 128], nm, gtb[:, s, 1, cg_ * 128:(cg_ + 1) * 128], nm, gee[:, s, cg_:cg_ + 1])

        for si, (t0, n, latent) in enumerate(SEQS):
            for d in range(2):
                for h in range(4):
                    if latent:
                        dma("sp", Srun[32 * h:32 * h + 32, d, :], sgla0[l, d, h, :, :], [], ["gSrun"])
                if not latent:
                    S.op("dve", lambda e, d=d: e.memset(Srun[:, d, :], 0.0), [], ["gSrun"])
                order = list(range(n)) if d == 0 else list(range(n - 1, -1, -1))
                g1 = {}

                def g1_front(c, d=d, g1=g1):
                    ti = t0 + c
                    Eq_, Eqn, Ek_, Ekn, ee = tables(ti, d)
                    kf, kfn = rf()
                    tt("dve", kf, gk[:, ti * 128:(ti + 1) * 128], Ek_, ALU.mult, ["gk", Ekn], [kfn])
                    b, bn = nb()
                    tr(b[:, 0:128], kf, identF[:], [kfn, "identF"], [bn])
                    kt, ktn = rb()
                    act(kt, b[:, 0:128], AF.Copy, [bn], [ktn])
                    g1[c] = (kt, ktn, ee, Eqn)

                def g1_back(c, d=d, g1=g1):
                    ti = t0 + c
                    kt, ktn, ee, Eqn = g1.pop(c)
                    o0 = Sp[:, d, c, 0, 0:64]
                    spo = AP(o0.tensor, o0.offset, [list(o0.ap[0]), [192, 2], [1, 64]])
                    sri = bcast_mid(Srun[:, d, :], 2)
                    S.op("dve", lambda e, spo=spo, sri=sri: e.tensor_copy(out=spo, in_=sri), ["gSrun"], ["Sp"])
                    b2, b2n = nb()
                    for h in range(4):
                        mm(b2[:, h * 64:(h + 1) * 64], kt, gvpad[:, ti, h, (h % 2) * 64:(h % 2) * 64 + 64], True, True, [ktn, "gvpad"], [b2n])
                    ts("dve", U, b2[:, 0:64], hmask[:, 0:1], None, ALU.mult, None, [b2n, "hmask"], ["gU"])
                    for h in range(1, 4):
                        stt("dve", U, b2[:, h * 64:(h + 1) * 64], hmask[:, h:h + 1], U, ALU.mult, ALU.add, [b2n, "hmask", "gU"], ["gU"])
                    tt("dve", Srun[:, d, :], Srun[:, d, :], U, ALU.add, ["gSrun", "gU"], ["gSrun"])
                    ts("dve", Srun[:, d, :], Srun[:, d, :], ee, None, ALU.mult, None, ["gSrun", Eqn], ["gSrun"])

                g1_front(order[0])
                for oi, c in enumerate(order):
                    if oi + 1 < len(order):
                        g1_front(order[oi + 1])
                    g1_back(c)
                if not latent:
                    for h in range(4):
                        dma("sp", osgla[si, l, d, h, :, :], Srun[32 * h:32 * h + 32, d, :], ["gSrun"], ["osgla_d"])
            g2u = [(c, d, h) for c in range(n) for d in range(2) for h in range(4)]
            obs = [(pbank[6], "pb6"), (pbank[7], "pb7")]
            tb2 = {}
            g2 = {}

            def g2_front(u):
                c, d, h = u
                ti = t0 + c
                if h == 0:
                    Eq_, Eqn, Ek_, Ekn, ee = tables(ti, d)
                    kfb, kfbn = rb()
                    tt("dve", kfb, gk[:, ti * 128:(ti + 1) * 128], Ek_, ALU.mult, ["gk", Ekn], [kfbn])
                    tb2[(c, d)] = (Eq_, Eqn, kfb, kfbn)
                Eq_, Eqn, kfb, kfbn = tb2[(c, d)]
                qh, qhn = rb()
                stt("dve", qh, gq[:, ti * 128:(ti + 1) * 128], hmask[:, h:h + 1], Eq_, ALU.mult, ALU.mult, ["gq", "hmask", Eqn], [qhn])
                b, bn = nb()
                mm(b[:, 0:128], kfb, qh, True, True, [kfbn, qhn], [bn])
                g2[u] = (qh, qhn, b, bn)

            def g2_mid(u):
                c, d, h = u
                qh, qhn, b, bn = g2[u]
                a_, an_ = rb()
                tt("dve", a_, b[:, 0:128], (maskF if d == 0 else maskB)[:], ALU.mult, [bn, "maskF", "maskB"], [an_])
                g2[u] = (qh, qhn, a_, an_)

            def g2_back(u):
                c, d, h = u
                ti = t0 + c
                g = ti // 4
                cg = ti % 4
                j, hl = h // 2, h % 2
                ob, obn = obs[j]
                qh, qhn, a_, an_ = g2.pop(u)
                first = (d == 0 and hl == 0)
                last = (d == 1 and hl == 1)
                mm(ob[:, 0:128], gvpad[:, ti, h, :], a_, first, False, ["gvpad", an_], [obn])
                mm(ob[:, 0:128], Sp[:, d, c, hl, :], qh, False, last, ["Sp", qhn], [obn])
                if d == 1 and h == 3:
                    for jj in range(2):
                        act(oG[:, jj, cg * 128:(cg + 1) * 128], obs[jj][0][:, 0:128], AF.Copy, [obs[jj][1]], ["goG"])
                    if cg == 3:
                        for jj in range(2):
                            head_norm(oG, "goG", jj, g, glagnT[:, l * 2 + jj:l * 2 + jj + 1], gg, "gg", 4 + jj)

            for ui in range(len(g2u) + 2):
                if ui < len(g2u):
                    g2_front(g2u[ui])
                if 1 <= ui <= len(g2u):
                    g2_mid(g2u[ui - 1])
                if ui >= 2:
                    g2_back(g2u[ui - 2])

    lreT = sbt("lreT", [128, 64]); limT = sbt("limT", [128, 64]); ldtT = sbt("ldtT", [128, 64])
    rows_to_cols(s5_lre2[:, :], 64, lreT[:], "lreT")
    rows_to_cols(s5_lim2[:, :], 64, limT[:], "limT")
    rows_to_cols(s5_ldt2[:, :], 64, ldtT[:], "ldtT")
    cmask = sbt("cmask", [128, 4, 128], BF16)
    dma("pool", cmask[:], cd["cmask"][:, :, :], [], ["cmask"])
    PI = math.pi
    negpi = sbt("negpi", [128, 1]); glm = sbt("glm", [128, 2]); s5fin = sbt("s5fin", [128, 8, 2]); s5x0 = sbt("s5x0", [128, 8, 2])
    S.op("dve", lambda e: e.memset(negpi[:], -PI), [], ["negpi"])
    S.op("dve", lambda e: e.memset(glm[:], 0.0), [], ["glm"])
    S.op("dve", lambda e: e.memset(glm[0:64, 0:1], 1.0), ["glm"], ["glm"])
    S.op("dve", lambda e: e.memset(glm[64:128, 1:2], 1.0), ["glm"], ["glm"])

    def rev3(ap3):
        a = ap3.ap
        n = a[-1][1]
        return AP(ap3.tensor, ap3.offset + (n - 1) * a[-1][0], [list(a[0]), list(a[1]), [-a[-1][0], n]])

    I32 = mybir.dt.int32

    def sin_reduced(out, ang, w, names_in, name_out, out_view=None):
        q = tmpf[1][:, 0:w]
        qi = tmpf[1][:, 256:256 + w].bitcast(I32)
        r = tmpf[1][:, 128:128 + w]
        c = tmpf[1][:, 384:384 + w]
        ts("dve", q, ang, 1.0 / (2 * PI), None, ALU.mult, None, names_in, ["tmpf1"])
        S.op("dve", lambda e: e.tensor_copy(out=qi, in_=q), ["tmpf1"], ["tmpf1"])
        S.op("dve", lambda e: e.tensor_copy(out=q, in_=qi), ["tmpf1"], ["tmpf1"])
        stt("dve", r, q, -2 * PI, ang, ALU.mult, ALU.add, ["tmpf1"] + names_in, ["tmpf1"])
        ts("dve", c, r, PI, -2 * PI, ALU.is_gt, ALU.mult, ["tmpf1"], ["tmpf1"])
        tt("dve", r, r, c, ALU.add, ["tmpf1"], ["tmpf1"])
        ts("dve", c, r, -PI, 2 * PI, ALU.is_lt, ALU.mult, ["tmpf1"], ["tmpf1"])
        tt("dve", r, r, c, ALU.add, ["tmpf1"], ["tmpf1"])
        act(out, r if out_view is None else out_view(r), AF.Sin, ["tmpf1"], name_out)

    def mixer_s5(l):
        su = carve(0, 1536, BF16).rearrange("p (j t) -> p j t", j=2)
        ysum = carve(1536, 1536, BF16).rearrange("p (j t) -> p j t", j=2)
        s5x = carve(3072, 1536, BF16)
        rtab = carve(3072, 1024, F32).rearrange("p (g x) -> p g x", g=8)
        Bb = carve(4608, 2048, BF16).rearrange("p (d c g x) -> p d c g x", d=2, c=2, g=8)
        Cl = carve(6656, 2048, BF16).rearrange("p (d c g x) -> p d c g x", d=2, c=2, g=8)
        csT = carve(8704, 2048, BF16).rearrange("p (d c g x) -> p d c g x", d=2, c=2, g=8)
        sm = carve(10752, 256, F32).rearrange("p (a g) -> p a g", g=8)
        wri = carve(11008, 1024, F32).rearrange("p (c g x) -> p c g x", c=2, g=4)
        zri = rotf[:].rearrange("p (c g) x -> p c g x", c=2)
        xb = rotb[:].rearrange("p (s g) x -> p s g x", s=4)
        DT, A_, W_, R_, T0, T1, NR, NI, DEN, CR, CI = range(11)
        E1R, E1I, E127R, E127I, E128R, E128I = 11, 12, 13, 14, 15, 16
        WIR, WII = 17, 18
        smd = lambda d, i: sm[:, (d * 0 + i), :]
        bbf = wri.rearrange("p c g x -> p (c g x)")
        wv, wn = load_w(w_in[l, :, 1024:1280], 8, 256, key="s5%d" % l)
        for g in range(3):
            for j in range(2):
                proj_fm(wv, wn, 8, j * 128, hT, "hT", g, lambda b, bn, j=j, g=g: act(su[:, j, g * 512:(g + 1) * 512], b[:], AF.Copy, [bn], ["su"]))
        def sin_wide(out, ang, names_in, name_out):
            ys_ = carve(1536, 3072, F32)
            q = ys_[:, 0:1024]
            r = ys_[:, 1024:2048]
            c = ys_[:, 2048:3072]
            qi = rotb[:].rearrange("p a x -> p (a x)").bitcast(I32)
            ts("dve", q, ang, 1.0 / (2 * PI), None, ALU.mult, None, names_in, ["s5q"])
            S.op("dve", lambda e: e.tensor_copy(out=qi, in_=q), ["s5q"], ["s5qi"])
            S.op("dve", lambda e: e.tensor_copy(out=q, in_=qi), ["s5qi"], ["s5q"])
            stt("dve", r, q, -2 * PI, ang, ALU.mult, ALU.add, ["s5q"] + names_in, ["s5r"])
            ts("dve", c, r, PI, -2 * PI, ALU.is_gt, ALU.mult, ["s5r"], ["s5c"])
            tt("dve", r, r, c, ALU.add, ["s5r", "s5c"], ["s5r"])
            ts("dve", c, r, -PI, 2 * PI, ALU.is_lt, ALU.mult, ["s5r"], ["s5c"])
            tt("dve", r, r, c, ALU.add, ["s5r", "s5c"], ["s5r"])
            act(out, r, AF.Sin, ["s5r"], name_out)

        par = {}
        for d in range(2):
            col = (l * 2 + d) * 8
            pr = {}
            t = lambda i, d=d: sm[:, d * 16 + i, :]
            N = "s5sm"
            act(t(0), ldtT[:, col:col + 8], AF.Exp, ["ldtT"], [N])
            tt("dve", t(1), lreT[:, col:col + 8], t(0), ALU.mult, ["lreT", N], [N])
            tt("dve", t(2), limT[:, col:col + 8], t(0), ALU.mult, ["limT", N], [N])
            act(t(3), t(1), AF.Exp, [N], [N])
            ang3 = tmpf[0][:, 0:24]
            ang3v = ang3.rearrange("p (m g) -> p m g", m=3)
            for mi, mult in enumerate((1.0, 127.0, 128.0)):
                ts("dve", ang3v[:, mi, :], t(2), float(mult), None, ALU.mult, None, [N], ["tmpf0"])
            r0 = sm[:, d * 16 + 6, :]
            cos_out = AP(r0.tensor, r0.offset, [list(r0.ap[0]), [16, 3], [1, 8]])
            sin_out = AP(r0.tensor, r0.offset + 8, [list(r0.ap[0]), [16, 3], [1, 8]])
            sin_reduced(sin_out, ang3, 24, ["tmpf0"], [N], out_view=lambda a: a.rearrange("p (m g) -> p m g", m=3))
            ts("dve", ang3, ang3, PI / 2, None, ALU.add, None, ["tmpf0"], ["tmpf0"])
            sin_reduced(cos_out, ang3, 24, ["tmpf0"], [N], out_view=lambda a: a.rearrange("p (m g) -> p m g", m=3))
            tt("dve", t(12), t(3), t(6), ALU.mult, [N], [N])
            ts("dve", t(12), t(12), -1.0, None, ALU.add, None, [N], [N])
            tt("dve", t(13), t(3), t(7), ALU.mult, [N], [N])
            tt("dve", t(4), lreT[:, col:col + 8], lreT[:, col:col + 8], ALU.mult, ["lreT"], [N])
            tt("dve", t(5), limT[:, col:col + 8], limT[:, col:col + 8], ALU.mult, ["limT"], [N])
            tt("dve", t(4), t(4), t(5), ALU.add, [N], [N])
            S.op("dve", lambda e, t=t: e.reciprocal(t(4), t(4)), [N], [N])
            tt("dve", t(14), t(12), lreT[:, col:col + 8], ALU.mult, [N, "lreT"], [N])
            tt("dve", t(5), t(13), limT[:, col:col + 8], ALU.mult, [N, "limT"], [N])
            tt("dve", t(14), t(14), t(5), ALU.add, [N], [N])
            tt("dve", t(14), t(14), t(4), ALU.mult, [N], [N])
            tt("dve", t(15), t(13), lreT[:, col:col + 8], ALU.mult, [N, "lreT"], [N])
            tt("dve", t(5), t(12), limT[:, col:col + 8], ALU.mult, [N, "limT"], [N])
            tt("dve", t(15), t(15), t(5), ALU.subtract, [N], [N])
            tt("dve", t(15), t(15), t(4), ALU.mult, [N], [N])
            for ii in (6, 7, 10, 11):
                tt("dve", t(ii), t(ii), t(3), ALU.mult, [N], [N])
            angw = rotf[:]
            pb_ = posfb[:, 0, :]
            posb = AP(pb_.tensor, pb_.offset, [list(pb_.ap[0]), [0, 8], list(pb_.ap[1])])
            wb_ = bcast_last(t(2), 128)
            tt("dve", angw, posb, wb_, ALU.mult, ["posfb", N], ["rotf"])
            tt("dve", angw, angw, wb_, ALU.subtract, ["rotf", N], ["rotf"])
            sin_wide(csT[:, d, 1].rearrange("p g x -> p (g x)"), rotf[:].rearrange("p g x -> p (g x)"), ["rotf"], ["csT"])
            ts("dve", angw, angw, PI / 2, None, ALU.add, None, ["rotf"], ["rotf"])
            sin_wide(csT[:, d, 0].rearrange("p g x -> p (g x)"), rotf[:].rearrange("p g x -> p (g x)"), ["rotf"], ["csT"])
            bre = bbf[:, 0:128].rearrange("p (g x) -> p g x", g=8)
            bim = bbf[:, 128:256].rearrange("p (g x) -> p g x", g=8)
            bbr = bbf[:, 256:384].rearrange("p (g x) -> p g x", g=8)
            bbi = bbf[:, 384:512].rearrange("p (g x) -> p g x", g=8)
            tmpb = bbf[:, 512:640].rearrange("p (g x) -> p g x", g=8)
            inb = bbf[:, 640:768]
            for (dst_, src_) in ((bre, s5_bre), (bim, s5_bim)):
                e0 = src_[l, d, 0:1, 0:1, 0:1]
                sap = AP(e0.tensor, e0.offset, [[16, 128], [2048, 8], [1, 16]])
                dma("sp", dst_, sap, [], ["s5b"])
            crb = bcast_last(t(14), 16)
            cib = bcast_last(t(15), 16)
            tt("dve", bbr, bre, crb, ALU.mult, ["s5b", N], ["s5b"])
            tt("dve", tmpb, bim, cib, ALU.mult, ["s5b", N], ["s5b"])
            tt("dve", bbr, bbr, tmpb, ALU.subtract, ["s5b"], ["s5b"])
            tt("dve", bbi, bim, crb, ALU.mult, ["s5b", N], ["s5b"])
            tt("dve", tmpb, bre, cib, ALU.mult, ["s5b", N], ["s5b"])
            tt("dve", bbi, bbi, tmpb, ALU.add, ["s5b"], ["s5b"])
            for c_, bb_ in ((0, bbr), (1, bbi)):
                for kc in range(2):
                    inv = inb.rearrange("p (a b x) -> p a b x", a=4, b=2)
                    for glb in range(2):
                        ts("dve", inv[:, :, glb, :], bb_[:, 4 * kc:4 * kc + 4, :], glm[:, glb:glb + 1], None, ALU.mult, None, ["s5b", "glm"], ["s5in"])
                    b, bn = nb()
                    tr(b[:, 0:128], inb, identF[:], ["s5in", "identF"], [bn])
                    for gpl in range(4):
                        ts("dve", Bb[:, d, c_, 4 * kc + gpl, :], b[:, 0:128], hmask[:, gpl:gpl + 1], None, ALU.mult, None, [bn, "hmask"], ["Bb"])
            for c_, src_ in ((0, s5_cre), (1, s5_cim)):
                for mc in range(2):
                    cin = bbf[:, 768:896].rearrange("p (b x) -> p b x", b=2)
                    rows = src_[l, d].rearrange("g h p -> (g h) p")[mc * 128:(mc + 1) * 128, :]
                    for glb in range(2):
                        dma("sp", cin[:, glb, :], rows, [], ["s5cin"])
                    b, bn = nb()
                    tr(b[:, 0:128], bbf[:, 768:896], identF[:], ["s5cin", "identF"], [bn])
                    cT_ = bbf[:, 896:1024]
                    act(cT_, b[:, 0:128], AF.Copy, [bn], ["s5cT"], scale=(1.0 if c_ == 0 else -1.0))
                    for gpl in range(4):
                        tt("dve", Cl[:, d, c_, 4 * mc + gpl, :], cT_, cmask[:, gpl, :], ALU.mult, ["s5cT", "cmask"], ["Cl"])
        S.barrier()
        def cmul(outr, outi, ar, ai, br, bi, n1, n2, wd=8, eng="dve"):
            r4, r5, r6 = (4, 5, 20) if eng == "dve" else (14, 15, 30)
            T4 = sm[:, r4, 0:wd]
            T5 = sm[:, r5, 0:wd]
            T6 = sm[:, r6, 0:wd]
            tn = "s5tmp_" + eng
            tt(eng, T4, ar, br, ALU.mult, n1, [tn])
            tt(eng, T5, ai, bi, ALU.mult, n1, [tn])
            tt(eng, T5, T4, T5, ALU.subtract, [tn], [tn])
            tt(eng, T4, ar, bi, ALU.mult, n1, [tn])
            tt(eng, T6, ai, br, ALU.mult, n1, [tn])
            tt(eng, outi, T6, T4, ALU.add, [tn], n2)
            S.op(eng, lambda e: e.tensor_copy(out=outr, in_=T5), [tn], n2)

        slot = [0]
        ucnt = [0]
        def bslot(ap2):
            return ap2.rearrange("p (g x) -> p g x", g=4)
        tf0 = tmpf[0][:].bitcast(BF16)
        tf1 = tmpf[1][:].bitcast(BF16)
        rfb = rotf[:].rearrange("p a x -> p (a x)").bitcast(BF16)
        wsb = wri.rearrange("p c g x -> p (c g x)").bitcast(BF16)
        s5slots = {("bu", 0, 0): bslot(tf0[:, 0:512]), ("bu", 0, 1): bslot(tf0[:, 512:1024]),
                   ("bu", 1, 0): bslot(tf1[:, 0:512]), ("bu", 1, 1): bslot(tf1[:, 512:1024]),
                   ("t", 0): bslot(rfb[:, 0:512]), ("t", 1): bslot(rfb[:, 512:1024]), ("t", 2): bslot(rfb[:, 1024:1536]), ("t", 3): bslot(rfb[:, 1536:2048]),
                   ("z", 0): bslot(wsb[:, 0:512]), ("z", 1): bslot(wsb[:, 512:1024]),
                   ("w", 0, 0): bslot(wsb[:, 1024:1536]), ("w", 0, 1): bslot(wsb[:, 1536:2048]),
                   ("w", 1, 0): bslot(sqb[0][:]), ("w", 1, 1): bslot(sqb[1][:])}
        S.barrier()
        rt_dir = [None]
        units = []
        for si, (t0, n, latent) in enumerate(SEQS):
            for d in range(2):
                order = list(range(n)) if d == 0 else list(range(n - 1, -1, -1))
                for ci_, c in enumerate(order):
                    for kc in range(2):
                        units.append(dict(si=si, t0=t0, n=n, latent=latent, d=d, ci=ci_, c=c, kc=kc, u=len(units) % 2))

        def stage_a(un):
            d, kc, ti, u_ = un["d"], un["kc"], un["t0"] + un["c"], un["u"]
            br_, brn = nb()
            bi_, bin_ = nb()
            for gpl in range(4):
                gp = 4 * kc + gpl
                mm(br_[:, gpl * 128:(gpl + 1) * 128], Bb[:, d, 0, gp, :], su[:, kc, ti * 128:(ti + 1) * 128], True, True, ["Bb", "su"], [brn])
                mm(bi_[:, gpl * 128:(gpl + 1) * 128], Bb[:, d, 1, gp, :], su[:, kc, ti * 128:(ti + 1) * 128], True, True, ["Bb", "su"], [bin_])
            bur = br_[:].rearrange("p (g x) -> p g x", g=4)
            bui = bi_[:].rearrange("p (g x) -> p g x", g=4)
            if d == 1:
                bur = rev3(bur)
                bui = rev3(bui)
            burb, buib = s5slots[("bu", u_, 0)], s5slots[("bu", u_, 1)]
            bun = ("s5bu", u_)
            act(burb, bur, AF.Copy, [brn], [bun])
            act(buib, bui, AF.Copy, [bin_], [bun])

        def stage_b(un):
            si, t0, n, latent, d, ci_, c, kc, u_ = (un[x] for x in ("si", "t0", "n", "latent", "d", "ci", "c", "kc", "u"))
            ti = t0 + c
            t = lambda i, d=d: sm[:, d * 16 + i, :]
            WR = t(12)
            WI = t(13)
            if ci_ == 0 and kc == 0:
                if latent:
                    e0 = ss50[l, d, 0:1, 0:1, 0:1]
                    for glb in range(2):
                        sap = AP(e0.tensor, e0.offset + glb * 128, [[2, 64], [256, 8], [1, 2]])
                        dma("sp", s5x0[glb * 64:(glb + 1) * 64, :, :], sap, [], ["s5x0"])
                    cmul(WR, WI, s5x0[:, :, 0], s5x0[:, :, 1], t(6), t(7), ["s5x0", "s5sm"], ["s5w0"])
                else:
                    S.op("dve", lambda e, WR=WR: e.memset(WR, 0.0), [], ["s5w0"])
                    S.op("dve", lambda e, WI=WI: e.memset(WI, 0.0), [], ["s5w0"])
            cs_ = csT[:, d, 0, 4 * kc:4 * kc + 4, :]
            sn_ = csT[:, d, 1, 4 * kc:4 * kc + 4, :]
            burb, buib = s5slots[("bu", u_, 0)], s5slots[("bu", u_, 1)]
            bun = ("s5bu", u_)
            t1, t2 = s5slots[("t", 0)], s5slots[("t", 1)]
            t3, t4 = s5slots[("t", 2)], s5slots[("t", 3)]
            zr, zi = s5slots[("z", 0)], s5slots[("z", 1)]
            wr_, wi_ = s5slots[("w", u_, 0)], s5slots[("w", u_, 1)]
            wrn = ("s5w", u_)
            tt("dve", t1, burb, cs_, ALU.mult, [bun, "csT"], [("s5t", 0)])
            tt("dve", t2, buib, sn_, ALU.mult, [bun, "csT"], [("s5t", 1)])
            tt("dve", t3, buib, cs_, ALU.mult, [bun, "csT"], [("s5t", 2)])
            tt("dve", t4, burb, sn_, ALU.mult, [bun, "csT"], [("s5t", 3)])
            tt("dve", zr, t1, t2, ALU.add, [("s5t", 0), ("s5t", 1)], ["s5zr"])
            tt("dve", zi, t3, t4, ALU.subtract, [("s5t", 2), ("s5t", 3)], ["s5zi"])
            gsl0 = slice(4 * kc, 4 * kc + 4)
            if rt_dir[0] != d:
                rt_dir[0] = d
                S.op("dve", lambda e, d=d: e.tensor_copy(out=rtab, in_=bcast_last(sm[:, d * 16 + 3, :], 128)), ["s5sm"], ["rtab"])
                S.op("dve", lambda e: e.memset(rtab[:, :, 0:1], 0.0), ["rtab"], ["rtab"])
            if not (ci_ == 0 and not latent):
                tt("dve", zr[:, :, 0], zr[:, :, 0], WR[:, gsl0], ALU.add, ["s5zr", ("s5w0", kc)], ["s5zr"])
                tt("dve", zi[:, :, 0], zi[:, :, 0], WI[:, gsl0], ALU.add, ["s5zi", ("s5w0", kc)], ["s5zi"])
            fl = lambda a3: a3.rearrange("p g x -> p (g x)")
            rt_ = fl(rtab[:, 4 * kc:4 * kc + 4, :])
            S.op("dve", lambda e, rt_=rt_, zr=zr, wr_=wr_: e.tensor_tensor_scan(out=fl(wr_), data0=rt_, data1=fl(zr), initial=0.0, op0=ALU.mult, op1=ALU.add),
                 ["rtab", "s5zr"], [wrn])
            S.op("dve", lambda e, rt_=rt_, zi=zi, wi_=wi_: e.tensor_tensor_scan(out=fl(wi_), data0=rt_, data1=fl(zi), initial=0.0, op0=ALU.mult, op1=ALU.add),
                 ["rtab", "s5zi"], [wrn])
            s0 = slot[0] % 2
            slot[0] += 1
            p1, p2 = xb[:, 2 * s0], xb[:, 2 * s0 + 1]
            p3, p4 = bslot(s5x[:, 2048:2560]), bslot(s5x[:, 2560:3072])
            pn34 = [("s5p", 3), ("s5p", 4)]
            xn = ("xb", s0)
            tt("dve", p1, wr_, cs_, ALU.mult, [wrn, "csT"], [xn])
            stt("dve", p2, wi_, -1.0, sn_, ALU.mult, ALU.mult, [wrn, "csT"], [xn])
            tt("dve", p3, wr_, sn_, ALU.mult, [wrn, "csT"], [pn34[0]])
            tt("dve", p4, wi_, cs_, ALU.mult, [wrn, "csT"], [pn34[1]])
            last = (ci_ == n - 1)
            gsl = slice(4 * kc, 4 * kc + 4)
            wer = wr_[:, :, 127]
            wei = wi_[:, :, 127]
            wn_ = [wrn, "s5sm"]
            if last and not latent:
                cmul(s5fin[:, gsl, 0], s5fin[:, gsl, 1], wer, wei, t(8)[:, gsl], t(9)[:, gsl], wn_, ["s5fin"], 4, "pool")
            if not last:
                cmul(WR[:, gsl], WI[:, gsl], wer, wei, t(10)[:, gsl], t(11)[:, gsl], wn_, [("s5w0", kc)], 4, "pool")
            by, byn = nb()
            k_ = 0
            for gpl in range(4):
                gp = 4 * kc + gpl
                for (pp, pnm, cc) in ((p1, xn, 0), (p2, xn, 0), (p3, pn34[0], 1), (p4, pn34[1], 1)):
                    mm(by[:, 0:128], Cl[:, d, cc, gp, :], pp[:, gpl, :], k_ == 0, k_ == 15, ["Cl", pnm], [byn])
                    k_ += 1
            ysl = ysum[:, kc, ti * 128:(ti + 1) * 128]
            if d == 0:
                act(ysl, by[:, 0:128], AF.Copy, [byn], [("ysum", (kc, ti))])
            else:
                ysr = AP(ysl.tensor, ysl.offset + 127 * ysl.ap[-1][0], [list(ysl.ap[0]), [-ysl.ap[-1][0], 128]])
                tt("dve", ysr, by[:, 0:128], ysr, ALU.add, [byn, ("ysum", (kc, ti))], [("ysum", (kc, ti))])
            if last and kc == 1 and not latent:
                e0 = oss5[si, l, d, 0:1, 0:1, 0:1]
                for glb in range(2):
                    dap = AP(e0.tensor, e0.offset + glb * 128, [[2, 64], [256, 8], [1, 2]])
                    dma("sp", dap, s5fin[glb * 64:(glb + 1) * 64, :, :], ["s5fin"], ["oss5_d"])

        do_mod = (l + 1 < DEPTH)
        modq = []
        if do_mod:
            align_w()
            prefetch("ada%d_0" % (l + 1), mod_src(l + 1, 0), 8, 1024)
        stage_a(units[0])
        for i_, un in enumerate(units):
            if i_ + 1 < len(units):
                stage_a(units[i_ + 1])
            if do_mod and i_ % 8 == 2:
                cb = i_ // 8
                if modq:
                    mod_evac(l + 1, *modq.pop(0))
                if cb + 1 < 6:
                    prefetch("ada%d_%d" % (l + 1, cb + 1), mod_src(l + 1, cb + 1), 8, 1024)
                bk = (pbank[6], "pb6") if cb % 2 == 0 else (pbank[7], "pb7")
                b_, bn_ = mod_mm(l + 1, cb, bank=bk)
                modq.append((cb, b_, bn_))
            stage_b(un)
        while modq:
            mod_evac(l + 1, *modq.pop(0))
        if do_mod:
            mod_finish(l + 1)
        align_w()
        prefetch("glu%d" % l, s5_wglu[l, :, :], 2, 512)
        prefetch("gla%d" % l, w_in[l, :, 1280:2080], 8, 800)
        S.barrier()
        ge = carve(4608, 1536, BF16).rearrange("p (j t) -> p j t", j=2)
        for j in range(2):
            stt("dve", ysum[:, j, :], su[:, j, :], s5dT[:, l * 2 + j:l * 2 + j + 1], ysum[:, j, :], ALU.mult, ALU.add, ["su", "s5dT", "ysum"], ["ysum"])
            for g in range(3):
                ysl = ysum[:, j, g * 512:(g + 1) * 512]
                tt("dve", tmpf[0][:], ysl, ysl, ALU.mult, ["ysum"], ["tmpf0"])
                ts("dve", tmpf[0][:], tmpf[0][:], 0.044715, 1.0, ALU.mult, ALU.add, ["tmpf0"], ["tmpf0"])
                tt("dve", tmpf[0][:], tmpf[0][:], ysl, ALU.mult, ["tmpf0", "ysum"], ["tmpf0"])
                act(tmpf[1][:], tmpf[0][:], AF.Sigmoid, ["tmpf0"], ["tmpf1"], scale=1.5957691216057308)
                tt("dve", ge[:, j, g * 512:(g + 1) * 512], ysl, tmpf[1][:], ALU.mult, ["ysum", "tmpf1"], ["s5ge"])
        wv, wn = load_w(s5_wglu[l, :, :], 2, 512, key="glu%d" % l)
        for j in range(2):
            for g in range(3):
                gt = sqb[g % 2]
                gtn = "sqb%d" % (g % 2)
                proj_fm(wv, wn, 2, (2 + j) * 128, ge, "s5ge", g,
                        lambda b, bn, gt=gt, gtn=gtn, j=j: act(gt[:], b[:], AF.Sigmoid, [bn, "bgluT"], [gtn], bias=bgluT[:, l * 4 + 2 + j:l * 4 + 3 + j]))
                proj_fm(wv, wn, 2, j * 128, ge, "s5ge", g,
                        lambda b, bn, gt=gt, gtn=gtn, j=j, g=g: stt("dve", brT[:, 2 + j, g * 512:(g + 1) * 512], b[:], bgluT[:, l * 4 + j:l * 4 + j + 1], gt[:],
                                                                 ALU.add, ALU.mult, [bn, "bgluT", gtn], [("brT", 2 + j)]))

    def mixer_na(l):
        nq = carve(0, 1536, BF16).rearrange("p (j t) -> p j t", j=2)
        nk = carve(1536, 1536, BF16).rearrange("p (j t) -> p j t", j=2)
        vpad = carve(3072, 3072, BF16).rearrange("p (a h c) -> p a h c", a=12, h=4)
        kcT = carve(6144, 256, BF16).rearrange("p (j t) -> p j t", j=2)
        vcpad = carve(6400, 512, BF16).rearrange("p (a h c) -> p a h c", a=2, h=4)
        opad = carve(6912, 256, BF16).rearrange("p (h c) -> p h c", h=4)
        BT = carve(7168, 1920, BF16).rearrange("p (h d c) -> p h d c", h=4, d=15)
        negt = carve(9088, 64, F32)
        pts = [carve(9152 + i * 256, 256, BF16) for i in range(3)]
        stg = carve(9920, 256, F32)
        stg2 = carve(10176, 256, F32)
        BTall = carve(0, 3840, F32).rearrange("p (a c) -> p a c", a=60)
        den = carve(11392, 512, F32)
        dma("sp", rowst[0:60, 0:31], na_rpb[l].rearrange("h d i -> (h d) i"), [], ["rowst"])
        b, bn = nb()
        tr(b[0:31, 0:60], rowst[0:60, 0:31], identF[0:60, 0:60], ["rowst", "identF"], [bn])
        act(Rrp[0:31, :], b[0:31, 0:60], AF.Copy, [bn], ["Rrp"])
        for q8 in range(8):
            b, bn = nb()
            for qq in range(8):
                qc = q8 * 8 + qq
                g0 = gbig[0:31, 63 - qc:63 - qc + 64]
                mm(b[0:64, qq * 60:(qq + 1) * 60], g0, Rrp[0:31, :], True, True, ["gbig", "Rrp"], [bn])
            bi = b[0:64, 0:1]
            src = AP(bi.tensor, bi.offset, [list(bi.ap[0]), [1, 60], [60, 8]])
            S.op("dve", lambda e, src=src, q8=q8: e.tensor_copy(out=BTall[0:64, :, q8 * 8:(q8 + 1) * 8], in_=src), [bn], ["BTall"])
        m0 = mneg[0:64, :]
        mk = AP(m0.tensor, m0.offset, [list(m0.ap[0]), [0, 60], list(m0.ap[1])])
        BT64 = carve(3840, 1920, BF16)
        tt("dve", BT64[0:64].rearrange("p (a c) -> p a c", a=60), BTall[0:64], mk, ALU.add, ["BTall", "mneg"], ["BT64"])
        BTflat = BT.rearrange("p h d c -> p (h d c)")
        for i8 in range(8):
            b, bn = nb()
            mm(b[:, 0:480], dupI[0:64, :], BT64[0:64, i8 * 480:(i8 + 1) * 480], True, True, ["dupI", "BT64"], [bn])
            act(BTflat[:, i8 * 480:(i8 + 1) * 480], b[:, 0:480], AF.Copy, [bn], ["BT"])
        S.barrier()
        S.op("pool", lambda e: e.memset(vpad, 0.0), [], ["vpad"])
        S.op("pool", lambda e: e.memset(vcpad, 0.0), [], ["vcpad"])
        S.op("pool", lambda e: e.memset(opad, 0.0), [], ["opad"])
        S.op("dve", lambda e: e.memset(negt, NEG), [], ["negt"])
        for h in range(4):
            S.op("pool", lambda e, h=h: e.memset(opad[:, h, (h % 2) * 64:(h % 2) * 64 + 64], 1.0), ["opad"], ["opad"])
        wv, wn = load_w(w_in[l, :, 2080:2848], 8, 768, key="na%d" % l)
        for g in range(3):
            for j in range(2):
                proj_fm(wv, wn, 8, j * 128, hT, "hT", g,
                        lambda b, bn, j=j, g=g: act(nq[:, j, g * 512:(g + 1) * 512], b[:], AF.Copy, [bn], ["nq"], scale=0.125))
                proj_fm(wv, wn, 8, 256 + j * 128, hT, "hT", g,
                        lambda b, bn, j=j, g=g: act(nk[:, j, g * 512:(g + 1) * 512], b[:], AF.Copy, [bn], ["nk"]))
        for ti in range(NTILE):
            def ev_v(b, bn, ti=ti):
                for h in range(4):
                    act(vpad[:, ti, h, (h % 2) * 64:(h % 2) * 64 + 64], b[:, h * 64:(h + 1) * 64], AF.Copy, [bn], ["vpad"])
                if ti < 4:
                    act(stg2, b[:, 0:256], AF.Copy, [bn], ["stg2"])
                    dma("sp", onv[ti // 2, l, (ti % 2) * 128:(ti % 2 + 1) * 128, :], stg2, ["stg2"], ["onv_d"])
            proj_tm(wv, wn, 8, 512, 256, hT, "hT", ti, ev_v)
            if ti < 4:
                def ev_k(b, bn, ti=ti):
                    act(stg, b[:, 0:256], AF.Copy, [bn], ["stg"])
                    dma("sp", onk[ti // 2, l, (ti % 2) * 128:(ti % 2 + 1) * 128, :], stg, ["stg"], ["onk_d"])
                proj_tm(wv, wn, 8, 256, 256, hT, "hT", ti, ev_k)
        align_w()
        prefetch("mg%d_0_0" % l, w_merge[l, :, 0:512], 8, 512)
        prefetch("br%d_0_0" % l, w_branch[l, 0, :, 0:512], 2, 512)
        for a in range(2):
            dma("sp", stg, ck[l, a * 128:(a + 1) * 128, :], [], ["stg"])
            b, bn = nb()
            for j in range(2):
                tr(b[:, j * 128:(j + 1) * 128], stg[:, j * 128:(j + 1) * 128], identF[:], ["stg", "identF"], [bn])
            act(kcT[:, :, a * 128:(a + 1) * 128], b[:, 0:256].rearrange("p (j t) -> p j t", j=2), AF.Copy, [bn], ["kcT"])
            dma("sp", stg2, cvv[l, a * 128:(a + 1) * 128, :], [], ["stg2"])
            for h in range(4):
                S.op("dve", lambda e, a=a, h=h: e.tensor_copy(out=vcpad[:, a, h, (h % 2) * 64:(h % 2) * 64 + 64], in_=stg2[:, h * 64:(h + 1) * 64]),
                     ["stg2"], ["vcpad"])
        pctr = [0]

        def attend(j, qlo, qn, keyspecs, out_cols):
            bo_, bon = pbank[6], "pb6"
            bd_, bdn = pbank[7], "pb7"
            n = len(keyspecs)
            LA = 2
            pend = {}

            def front(idx):
                (h, kT, kname, vp, vname, biasf) = keyspecs[idx]
                hl = h % 2
                bs_, bsn = nb()
                mm(bs_[:, 0:qn], kT, nq[hl * 64:(hl + 1) * 64, j, qlo:qlo + qn], True, True, [kname, "nq"], [bsn])
                pend[idx] = (bs_, bsn)

            def mid(idx):
                (h, kT, kname, vp, vname, biasf) = keyspecs[idx]
                bs_, bsn = pend[idx]
                pt = pts[pctr[0] % 3]
                ptn = "pt%d" % (pctr[0] % 3)
                pctr[0] += 1
                if biasf is None:
                    act(pt[:, 0:qn], bs_[:, 0:qn], AF.Exp, [bsn], [ptn])
                else:
                    tb, tbn = tmpf[idx % 2], "tmpf%d" % (idx % 2)
                    biasf(bs_, bsn, tb, tbn)
                    act(pt[:, 0:qn], tb[:, 0:qn], AF.Exp, [tbn], [ptn])
                pend[idx] = (h, vp, vname, pt, ptn)

            def back(idx):
                (h, vp, vname, pt, ptn) = pend.pop(idx)
                mm(bo_[:, 0:qn], vp, pt[:, 0:qn], idx == 0, idx == n - 1, [vname, ptn], [bon])
                mm(bd_[:, 0:qn], opad[:, h, :], pt[:, 0:qn], idx == 0, idx == n - 1, ["opad", ptn], [bdn])

            for i_ in range(n + 3):
                if i_ < n:
                    front(i_)
                if 1 <= i_ <= n:
                    mid(i_ - 1)
                if i_ >= 3:
                    back(i_ - 3)
            S.op("dve", lambda e: e.reciprocal(den[:, 0:qn], bd_[:, 0:qn]), [bdn], ["den"])
            tt("dve", brT[:, 6 + j, out_cols], bo_[:, 0:qn], den[:, 0:qn], ALU.mult, [bon, "den"], [("brT", 6 + j)])

        for sq in range(2):
            t0 = sq * 256
            for j in range(2):
                specs = []
                for hl in range(2):
                    h = 2 * j + hl
                    for kt in range(2):
                        ti = sq * 2 + kt
                        specs.append((h, nk[hl * 64:(hl + 1) * 64, j, ti * 128:(ti + 1) * 128], "nk", vpad[:, ti, h, :], "vpad", None))
                attend(j, t0, 256, specs, slice(t0, t0 + 256))
        for qi in range(8):
            t0 = 512 + qi * 128
            rows = [2 * qi, 2 * qi + 1]
            rs = [min(max(r - 4, 0), 8) for r in rows]
            jlo = rs[0] // 2
            jhi = (rs[1] + 7) // 2
            for j in range(2):
                specs = []
                for hl in range(2):
                    h = 2 * j + hl
                    for kj in range(jlo, jhi + 1):
                        ti = 4 + kj

                        def biasf(bs_, bsn, tb, tbn, h=h, kj=kj):
                            for krl in range(2):
                                kr = 2 * kj + krl
                                inw = [rs[qrl] <= kr < rs[qrl] + 8 for qrl in range(2)]
                                ps = slice(krl * 64, (krl + 1) * 64)
                                if inw[0] and inw[1]:
                                    b0 = BT[ps, h, kr - rows[0] + 7, :]
                                    bpair = AP(b0.tensor, b0.offset, [list(b0.ap[0]), [-64, 2], [1, 64]])
                                    tt("dve", tb[ps, 0:128].rearrange("p (a b) -> p a b", a=2), bs_[ps, 0:128].rearrange("p (a b) -> p a b", a=2),
                                       bpair, ALU.add, [bsn, "BT"], [tbn])
                                elif not inw[0] and not inw[1]:
                                    n0 = negt[ps, :]
                                    npair = AP(n0.tensor, n0.offset, [list(n0.ap[0]), [0, 2], [1, 64]])
                                    tt("dve", tb[ps, 0:128].rearrange("p (a b) -> p a b", a=2), bs_[ps, 0:128].rearrange("p (a b) -> p a b", a=2),
                                       npair, ALU.add, [bsn, "negt"], [tbn])
                                else:
                                    for qrl in range(2):
                                        qr = rows[qrl]
                                        osl = tb[ps, qrl * 64:(qrl + 1) * 64]
                                        isl = bs_[ps, qrl * 64:(qrl + 1) * 64]
                                        if inw[qrl]:
                                            tt("dve", osl, isl, BT[ps, h, kr - qr + 7, :], ALU.add, [bsn, "BT"], [tbn])
                                        else:
                                            tt("dve", osl, isl, negt[ps, :], ALU.add, [bsn, "negt"], [tbn])
                        specs.append((h, nk[hl * 64:(hl + 1) * 64, j, ti * 128:(ti + 1) * 128], "nk", vpad[:, ti, h, :], "vpad", biasf))
                    for a in range(2):
                        specs.append((h, kcT[hl * 64:(hl + 1) * 64, j, a * 128:(a + 1) * 128], "kcT", vcpad[:, a, h, :], "vcpad", None))
                attend(j, t0, 128, specs, slice(t0, t0 + 128))

    def layer(l):
        curl[0] = l
        if l == 0:
            emit_mod(0)
        norm_to_h(xT, "xT", 0, 0)
        allmix = all(m in mixers for m in ("ret", "s5", "gla", "na"))
        if not allmix:
            S.barrier()
            for n in range(6):
                S.op("pool", lambda e, n=n: e.memset(brT[:, n, :], 0.0), [], [("brT", n)])
        if "ret" in mixers:
            mixer_ret(l)
            S.barrier()
        if "s5" in mixers:
            mixer_s5(l)
            S.barrier()
        if "gla" in mixers:
            mixer_gla(l)
            S.barrier()
        if "na" in mixers:
            mixer_na(l)
        else:
            for n in (6, 7):
                S.op("pool", lambda e, n=n: e.memset(brT[:, n, :], 0.0), [], [("brT", n)])
        S.barrier()
        sT = carve(0, 6144, BF16).rearrange("p (k t) -> p k t", k=8)
        gts = [sqb[0], sqb[1]]
        mg_src = lambda n, half: w_merge[l, :, n * 1024 + half * 512: n * 1024 + half * 512 + 512]
        br_src = lambda n, half: w_branch[l, n, :, half * 512:(half + 1) * 512]
        for n in range(4):
            for half in range(2):
                wm, wmn = load_w(mg_src(n, half), 8, 512, key="mg%d_%d_%d" % (l, n, half))
                wbv, wbn = load_w(br_src(n, half), 2, 512, key="br%d_%d_%d" % (l, n, half))
                nxt = n * 2 + half + 1
                if nxt < 8:
                    prefetch("mg%d_%d_%d" % (l, nxt // 2, nxt % 2), mg_src(nxt // 2, nxt % 2), 8, 512)
                    prefetch("br%d_%d_%d" % (l, nxt // 2, nxt % 2), br_src(nxt // 2, nxt % 2), 2, 512)
                else:
                    prefetch("out%d" % l, w_out[l, :, :], 8, 1024)
                for fc in range(4):
                    fo = half * 4 + fc
                    for g in range(3):
                        gi = (fc + g) % 2
                        gt = gts[gi]
                        proj_fm(wm, wmn, 8, fc * 128, hT, "hT", g,
                                lambda b, bn, gt=gt, gi=gi, fo=fo: act(gt[:], b[:], AF.Sigmoid, [bn, "bmT"], ["sqb%d" % gi],
                                                                     bias=bmT[:, l * 32 + n * 8 + fo: l * 32 + n * 8 + fo + 1]))
                        brv = brT[:, 2 * n:2 * n + 2, :]

                        def ev_up(b, bn, gt=gt, gi=gi, fo=fo, g=g, n=n):
                            sl = sT[:, fo, g * 512:(g + 1) * 512]
                            if n == 0:
                                tt("dve", sl, b[:], gt[:], ALU.mult, [bn, "sqb%d" % gi], [("sT", (fo, g))])
                            else:
                                tt("dve", tmpf[1][:], b[:], gt[:], ALU.mult, [bn, "sqb%d" % gi], ["tmpf1"])
                                tt("pool", sl, sl, tmpf[1][:], ALU.add, ["tmpf1", ("sT", (fo, g))], [("sT", (fo, g))])
                        proj_fm(wbv, wbn, 2, fc * 128, brv, ("brT", 2 * n), g, ev_up)
        mT = carve(6144, 6144, BF16).rearrange("p (k t) -> p k t", k=8)
        wv, wn = load_w(w_out[l, :, :], 8, 1024, key="out%d" % l)
        for fo in range(8):
            for g in range(3):
                proj_fm(wv, wn, 8, fo * 128, sT, "sT", g,
                        lambda b, bn, fo=fo, g=g: act(mT[:, fo, g * 512:(g + 1) * 512], b[:], AF.Copy, [bn], ["mT"]))
        S.barrier()
        if l + 1 < DEPTH and "s5" not in mixers:
            emit_mod(l + 1)
        align_w()
        prefetch("m1_%d_0" % l, w_mlp1[l, :, 0:512], 8, 512)
        prefetch("m2_%d_0" % l, w_mlp2[l, 0:512, :], 4, 1024)
        resid_update(mT, "mT", 1)
        norm_to_h(xT, "xT", 2, 24)
        S.barrier()
        for hg in range(8):
            w1, w1n = load_w(w_mlp1[l, :, hg * 512:(hg + 1) * 512], 8, 512, key="m1_%d_%d" % (l, hg))
            w2, w2n = load_w(w_mlp2[l, hg * 512:(hg + 1) * 512, :], 4, 1024, key="m2_%d_%d" % (l, hg))
            if hg + 1 < 8:
                prefetch("m1_%d_%d" % (l, hg + 1), w_mlp1[l, :, (hg + 1) * 512:(hg + 2) * 512], 8, 512)
                prefetch("m2_%d_%d" % (l, hg + 1), w_mlp2[l, (hg + 1) * 512:(hg + 2) * 512, :], 4, 1024)
            elif l + 1 < DEPTH:
                prefetch("ret%d" % (l + 1), w_in[l + 1, :, 0:1024], 8, 1024)
            fb = brT[:, (hg % 2) * 4:(hg % 2) * 4 + 4, :]
            fbn = "fb%d" % (hg % 2)
            for fc in range(4):
                for g in range(3):
                    def ev_f(b, bn, fc=fc, g=g, fb=fb, fbn=fbn):
                        act(tmpf[0][:], b[:], AF.Relu, [bn], ["tmpf0"])
                        tt("dve", fb[:, fc, g * 512:(g + 1) * 512], tmpf[0][:], tmpf[0][:], ALU.mult, ["tmpf0"], [(fbn, (fc, g))])
                    proj_fm(w1, w1n, 8, fc * 128, hT, "hT", g, ev_f)
            for fo in range(8):
                for g in range(3):
                    def ev_o(b, bn, fo=fo, g=g, hg=hg):
                        sl = accT[:, fo, g * 512:(g + 1) * 512]
                        if hg == 0:
                            act(sl, b[:], AF.Copy, [bn], [("accT", (fo, g))])
                        else:
                            tt("dve", sl, b[:], sl, ALU.add, [bn, ("accT", (fo, g))], [("accT", (fo, g))])
                    proj_fm(w2, w2n, 4, fo * 128, fb, fbn, g, ev_o)
        resid_update(accT, "accT", 3)
        S.barrier()

    for l in range(DEPTH):
        layer(l)

    for ti in range(NTILE):
        sv, sn = stage(ti % 8)
        for half in range(2):
            b, bn = nb()
            for kk in range(4):
                k = half * 4 + kk
                tr(b[:, kk * 128:(kk + 1) * 128], xT[:, k, ti * 128:(ti + 1) * 128], identF[:], ["xT", "identF"], [bn])
            act(sv[:, half * 512:(half + 1) * 512], b[:], AF.Copy, [bn], [sn])
        dst = yp[ti // 2, (ti % 2) * 128:(ti % 2 + 1) * 128, :] if ti < 4 else ys[(ti - 4) * 128:(ti - 3) * 128, :]
        dma("sp", dst, sv, [sn], ["y_d"])
    S.final_wait("sp")
    S.emit(nc, st)
    st.close()
    return nc, consts


def core_inputs(inp, c, consts):
    b = c % 4
    f = lambda a: np.ascontiguousarray(a, dtype=np.float32)
    m = {
        "xp": f(inp["x_prompt"][2 * c:2 * c + 2]), "xs": f(inp["x_sample"][b]),
        "cv": f(np.concatenate([inp["c_ctx"].reshape(8, 128), inp["c"][b].reshape(8, 128)], 0)),
        "ck": f(inp["cache_na_k"][b].reshape(4, 256, 256)), "cvv": f(inp["cache_na_v"][b].reshape(4, 256, 256)),
        "sret0": f(inp["state_ret"][b]), "ss50": f(inp["state_s5"][b]), "sgla0": f(inp["state_gla"][b]),
        "w_ada": f(inp["w_ada"]), "b_ada": f(inp["b_ada"].reshape(192, 128)), "g_norm": f(inp["g_norm"].reshape(128, 128)),
        "w_in": f(inp["w_in"]), "ret_ld": f(inp["ret_log_decay"].reshape(32, 1)), "ret_gn": f(inp["ret_gn"].reshape(8, 128)),
        "s5_lre2": f(inp["s5_lambda_re"].reshape(64, 128)), "s5_lim2": f(inp["s5_lambda_im"].reshape(64, 128)),
        "s5_ldt2": f(np.broadcast_to(inp["s5_log_dt"][..., None], (4, 2, 16, 64)).reshape(64, 128)),
        "s5_bre": f(inp["s5_b_re"]), "s5_bim": f(inp["s5_b_im"]), "s5_cre": f(inp["s5_c_re"]), "s5_cim": f(inp["s5_c_im"]),
        "s5_d": f(inp["s5_d"].reshape(8, 128)), "s5_wglu": f(inp["s5_w_glu"]), "s5_bglu": f(inp["s5_b_glu"].reshape(16, 128)),
        "gla_wg": f(inp["gla_w_gate"]), "gla_bg": f(inp["gla_b_gate"]), "gla_gn": f(inp["gla_gn"].reshape(8, 128)),
        "na_rpb": f(inp["na_rpb"]), "w_branch": f(inp["w_branch"]), "w_merge": f(inp["w_merge"]),
        "b_merge": f(inp["b_merge"].reshape(128, 128)), "w_out": f(inp["w_out"]), "w_mlp1": f(inp["w_mlp1"]), "w_mlp2": f(inp["w_mlp2"]),
    }
    for k, v in consts.items():
        m["c_" + k] = v
    return m


_CACHE = {}


def kernel(**inp):
    inp = {k: np.asarray(v) for k, v in inp.items()}
    if "nc" not in _CACHE:
        _CACHE["nc"] = build(DEPTH=4)
    nc, consts = _CACHE["nc"]
    in_maps = [core_inputs(inp, c, consts) for c in range(8)]
    res = run_bass_kernel_spmd(nc, in_maps, core_ids=list(range(8))).results
    yp = np.concatenate([r["yp"] for r in res], 0).astype(np.float32)
    ys = np.stack([res[b]["ys"] for b in range(4)], 0).astype(np.float32)
    nk = np.concatenate([r["onk"] for r in res], 0).reshape(16, 4, 256, 4, 64).astype(np.float32)
    nv = np.concatenate([r["onv"] for r in res], 0).reshape(16, 4, 256, 4, 64).astype(np.float32)
    sret = np.concatenate([r["osret"] for r in res], 0).astype(np.float32)
    ss5 = np.concatenate([r["oss5"] for r in res], 0).astype(np.float32)
    sgla = np.concatenate([r["osgla"] for r in res], 0).astype(np.float32)
    return (yp, ys, nk, nv, sret, ss5, sgla)
```
